# Optimizing a Trainium2 kernel written in Bass

```python
import math
import jax, jax.numpy as jnp
from jax import lax
import numpy as np

D_MODEL = 2048
BATCH = 2
SEQ = 16384
DEPTH = 2

N_MIXERS = 2
N_A_LAYERS = (DEPTH + 1) // 2
N_B_LAYERS = DEPTH // 2
GRID_W = 64
EPS = 1e-6
A_WIDTH = D_MODEL
A_CHUNK = 128
A_GROUPS = 16
A_GROUP_DIM = A_WIDTH // A_GROUPS
HEAD_DIM = 128
N_HEADS = D_MODEL // HEAD_DIM
N_KV_HEADS = 4
GQA_GROUP = N_HEADS // N_KV_HEADS
Q_BLOCK = 128
ROPE_THETA = 10000.0
ROPE_AXIS_DIM = HEAD_DIM // 2
D_FF = 5632
CONV_W = 3

kernel_name = "hybrid_gmlp_gqa_convffn_encoder"


def _rmsnorm(x, gain):
    x32 = x.astype(jnp.float32)
    y = x32 * lax.rsqrt(jnp.mean(x32 * x32, axis=-1, keepdims=True) + EPS)
    return (y * gain.astype(jnp.float32)).astype(x.dtype)


def _gmlp_mixer(h, w_in, g_v, w_s, b_s, w_out):
    B, S, _ = h.shape
    n_chunks = S // A_CHUNK
    z = jax.nn.gelu(h @ w_in, approximate=False)
    u, v = jnp.split(z, 2, axis=-1)
    v = _rmsnorm(v, g_v)
    v = v.reshape(B, n_chunks, A_CHUNK, A_GROUPS, A_GROUP_DIM)
    sv = jnp.einsum("gij,bnjgc->bnigc", w_s, v)
    sv = sv + jnp.transpose(b_s)[None, None, :, :, None]
    y = u * sv.reshape(B, S, A_WIDTH)
    return y @ w_out


def _axial_rope_tables(S):
    rows = S // GRID_W
    row_pos = jnp.broadcast_to(jnp.arange(rows)[:, None], (rows, GRID_W)).reshape(S)
    col_pos = jnp.broadcast_to(jnp.arange(GRID_W)[None, :], (rows, GRID_W)).reshape(S)
    inv_freq = ROPE_THETA ** (-jnp.arange(0, ROPE_AXIS_DIM, 2, dtype=jnp.float32) / ROPE_AXIS_DIM)
    ang_r = row_pos.astype(jnp.float32)[:, None] * inv_freq[None, :]
    ang_c = col_pos.astype(jnp.float32)[:, None] * inv_freq[None, :]
    return jnp.cos(ang_r), jnp.sin(ang_r), jnp.cos(ang_c), jnp.sin(ang_c)


def _rotate(xh, cos, sin):
    S = xh.shape[1]
    shp = (1, S) + (1,) * (xh.ndim - 3) + (cos.shape[-1],)
    cos = cos.reshape(shp)
    sin = sin.reshape(shp)
    x1, x2 = jnp.split(xh, 2, axis=-1)
    return jnp.concatenate([x1 * cos - x2 * sin, x2 * cos + x1 * sin], axis=-1)


def _apply_axial_rope(t, tables):
    cr, sr, cc, sc = tables
    t32 = t.astype(jnp.float32)
    t_row, t_col = jnp.split(t32, 2, axis=-1)
    out = jnp.concatenate([_rotate(t_row, cr, sr), _rotate(t_col, cc, sc)], axis=-1)
    return out.astype(t.dtype)


def _gqa_mixer(h, w_qkv, g_q, g_k, w_o):
    B, S, _ = h.shape
    qkv = h @ w_qkv
    q, k, v = jnp.split(qkv, [N_HEADS * HEAD_DIM, (N_HEADS + N_KV_HEADS) * HEAD_DIM], axis=-1)
    q = q.reshape(B, S, N_KV_HEADS, GQA_GROUP, HEAD_DIM)
    k = k.reshape(B, S, N_KV_HEADS, HEAD_DIM)
    v = v.reshape(B, S, N_KV_HEADS, HEAD_DIM)
    q = _rmsnorm(q, g_q)
    k = _rmsnorm(k, g_k)
    tables = _axial_rope_tables(S)
    q = _apply_axial_rope(q, tables)
    k = _apply_axial_rope(k, tables)
    scale = 1.0 / math.sqrt(HEAD_DIM)
    n_blk = S // Q_BLOCK
    qb = q.reshape(B, n_blk, Q_BLOCK, N_KV_HEADS, GQA_GROUP, HEAD_DIM).transpose(1, 0, 2, 3, 4, 5)

    def attend(q_blk):
        s = jnp.einsum("bqhgd,bkhd->bhgqk", q_blk, k, preferred_element_type=jnp.float32) * scale
        p = jax.nn.softmax(s, axis=-1).astype(v.dtype)
        return jnp.einsum("bhgqk,bkhd->bqhgd", p, v)

    o = lax.map(attend, qb)
    o = o.transpose(1, 0, 2, 3, 4, 5).reshape(B, S, N_HEADS * HEAD_DIM)
    return o @ w_o


def _conv_ffn(h, w_up, conv_w, conv_b, w_down):
    a = h @ w_up
    S = a.shape[1]
    ap = jnp.pad(a, ((0, 0), (1, 1), (0, 0)))
    a = (conv_w[0] * ap[:, 0:S] + conv_w[1] * ap[:, 1:S + 1] + conv_w[2] * ap[:, 2:S + 2]
         + conv_b)
    g, val = jnp.split(a, 2, axis=-1)
    return (jax.nn.gelu(g, approximate=False) * val) @ w_down


def setup_inputs(seed: int = 0) -> dict:
    key = jax.random.key(seed)
    ks = jax.random.split(key, 24)
    f32 = jnp.float32

    def nrm(k, shape, fan_in, mult=1.0):
        return jax.random.normal(k, shape, f32) * (mult * fan_in ** -0.5)

    D = D_MODEL
    qkv_out = (N_HEADS + 2 * N_KV_HEADS) * HEAD_DIM
    return {
        "x": jax.random.normal(ks[0], (BATCH, SEQ, D), f32),
        "c": jax.random.normal(ks[1], (BATCH, D), f32),
        "w_ada": nrm(ks[2], (DEPTH, D, 6 * D), D, 0.5),
        "b_ada": 0.02 * jax.random.normal(ks[3], (DEPTH, 6 * D), f32),
        "g_norm": 1.0 + 0.02 * jax.random.normal(ks[4], (DEPTH, 2, D), f32),
        "g_final": 1.0 + 0.02 * jax.random.normal(ks[5], (D,), f32),
        "a_w_in": nrm(ks[6], (N_A_LAYERS, D, 2 * A_WIDTH), D),
        "a_g_v": 1.0 + 0.02 * jax.random.normal(ks[7], (N_A_LAYERS, A_WIDTH), f32),
        "a_w_s": nrm(ks[8], (N_A_LAYERS, A_GROUPS, A_CHUNK, A_CHUNK), A_CHUNK),
        "a_b_s": 1.0 + 0.02 * jax.random.normal(ks[9], (N_A_LAYERS, A_GROUPS, A_CHUNK), f32),
        "a_w_out": nrm(ks[10], (N_A_LAYERS, A_WIDTH, D), A_WIDTH),
        "b_w_qkv": nrm(ks[11], (N_B_LAYERS, D, qkv_out), D),
        "b_g_q": 1.0 + 0.02 * jax.random.normal(ks[12], (N_B_LAYERS, HEAD_DIM), f32),
        "b_g_k": 1.0 + 0.02 * jax.random.normal(ks[13], (N_B_LAYERS, HEAD_DIM), f32),
        "b_w_o": nrm(ks[14], (N_B_LAYERS, N_HEADS * HEAD_DIM, D), N_HEADS * HEAD_DIM),
        "f_w_up": nrm(ks[15], (DEPTH, D, 2 * D_FF), D),
        "f_conv_w": nrm(ks[16], (DEPTH, CONV_W, 2 * D_FF), CONV_W),
        "f_conv_b": 0.02 * jax.random.normal(ks[17], (DEPTH, 2 * D_FF), f32),
        "f_w_down": nrm(ks[18], (DEPTH, D_FF, D), D_FF),
    }


def reference(x, c, w_ada, b_ada, g_norm, g_final,
              a_w_in, a_g_v, a_w_s, a_b_s, a_w_out,
              b_w_qkv, b_g_q, b_g_k, b_w_o,
              f_w_up, f_conv_w, f_conv_b, f_w_down):
    cond = jax.nn.silu(c)
    for i in range(DEPTH):
        mod = cond @ w_ada[i] + b_ada[i]
        sh1, sc1, gt1, sh2, sc2, gt2 = jnp.split(mod[:, None, :], 6, axis=-1)
        h = _rmsnorm(x, g_norm[i, 0]) * (1.0 + sc1) + sh1
        j = i // N_MIXERS
        if i % N_MIXERS == 0:
            y = _gmlp_mixer(h, a_w_in[j], a_g_v[j], a_w_s[j], a_b_s[j], a_w_out[j])
        else:
            y = _gqa_mixer(h, b_w_qkv[j], b_g_q[j], b_g_k[j], b_w_o[j])
        x = x + gt1 * y
        h = _rmsnorm(x, g_norm[i, 1]) * (1.0 + sc2) + sh2
        x = x + gt2 * _conv_ffn(h, f_w_up[i], f_conv_w[i], f_conv_b[i], f_w_down[i])
    return _rmsnorm(x, g_final)
```

```python
import contextlib
import math
import numpy as np
import ml_dtypes
import concourse.bass as bass
import concourse.mybir as mybir
from concourse.bass_utils import run_bass_kernel_spmd

F32 = mybir.dt.float32
BF16 = mybir.dt.bfloat16
AF = mybir.ActivationFunctionType
ALU = mybir.AluOpType

D = 2048
KC = 16
NT = 4096
DFF = 5632
NF = 44
EPS = 1e-6
SUBW = 410
NSUBT = 10
COMPUTE = ("pe", "act", "dve", "pool")


class Op:
    __slots__ = ("eng", "fn", "raw", "oth", "flag", "idx", "group", "gcount", "is_dma", "inc")

    def __init__(self, eng, fn, is_dma, group):
        self.eng = eng
        self.fn = fn
        self.raw = set()
        self.oth = set()
        self.flag = False
        self.idx = 0
        self.group = group
        self.gcount = 0
        self.is_dma = is_dma
        self.inc = 16


def _ekey(o):
    return ("dma", o.group) if o.is_dma else o.eng


class Prog:
    def __init__(self, nc):
        self.nc = nc
        self.ops = []
        self.last_writer = {}
        self.readers = {}
        self.group_counts = {}
        self.group_last = {}
        self.last_on_eng = {}
        self.pending_barrier = {}
        self.bg_groups = set()
        self.group_inc = {}

    def _add(self, op, reads, writes):
        for r in reads:
            w = self.last_writer.get(r)
            if w is not None:
                op.raw.add(w)
        for wkey in writes:
            w = self.last_writer.get(wkey)
            if w is not None:
                op.oth.add(w)
            for rd in self.readers.get(wkey, {}).values():
                op.oth.add(rd)
        pb = self.pending_barrier.pop(op.eng, None)
        if pb:
            op.raw.update(pb)
        op.raw.discard(op)
        op.oth.discard(op)
        op.oth -= op.raw
        for r in reads:
            self.readers.setdefault(r, {})[_ekey(op)] = op
        for wkey in writes:
            self.last_writer[wkey] = op
            self.readers[wkey] = {}
        self.ops.append(op)
        self.last_on_eng[op.eng] = op
        return op

    def op(self, eng, fn, reads=(), writes=()):
        return self._add(Op(eng, fn, False, None), reads, writes)

    def dma(self, queue, group, fn, reads=(), writes=(), inc=16):
        o = Op(queue, fn, True, group)
        o.inc = inc
        self.group_inc[group] = inc
        self.group_counts[group] = self.group_counts.get(group, 0) + 1
        o.gcount = self.group_counts[group]
        self.group_last[group] = o
        return self._add(o, reads, writes)

    def barrier(self):
        lasts = set()
        for e, o in self.last_on_eng.items():
            if not o.is_dma:
                lasts.add(o)
        for g, o in self.group_last.items():
            if g not in self.bg_groups:
                lasts.add(o)
        for e in ("pe", "act", "dve", "pool", "sp"):
            self.pending_barrier[e] = set(lasts)

    def emit(self, final_groups=()):
        nc = self.nc
        ops = self.ops

        def needed(o, d, is_raw):
            if d.is_dma or o.is_dma:
                return True
            if d.eng != o.eng:
                return True
            if o.eng == "pe":
                return False
            return is_raw

        for o in ops:
            for d in o.raw:
                if needed(o, d, True) and not d.is_dma:
                    d.flag = True
            for d in o.oth:
                if needed(o, d, False) and not d.is_dma:
                    d.flag = True
        counts = {e: 0 for e in COMPUTE}
        for o in ops:
            if not o.is_dma and o.flag:
                counts[o.eng] += 1
                o.idx = counts[o.eng]
        groups = sorted(self.group_counts, key=str)
        with contextlib.ExitStack() as st:
            sem_eng = {e: st.enter_context(nc.semaphore("m_" + e)) for e in COMPUTE}
            sem_grp = {g: st.enter_context(nc.semaphore("g%d" % i)) for i, g in enumerate(groups)}
            block = st.enter_context(nc.Block())
            per_eng = {"sp": []}
            for o in ops:
                per_eng.setdefault(o.eng, []).append(o)

            def make(engname, lst):
                def body(eng):
                    known = {}
                    for o in lst:
                        waits = {}
                        for is_raw, ds in ((True, o.raw), (False, o.oth)):
                            for d in ds:
                                if not needed(o, d, is_raw):
                                    continue
                                if d.is_dma:
                                    key = ("g", d.group)
                                    val = d.inc * d.gcount
                                else:
                                    key = ("e", d.eng)
                                    val = d.idx
                                if known.get(key, 0) >= val:
                                    continue
                                if waits.get(key, 0) < val:
                                    waits[key] = val
                        for key, val in waits.items():
                            known[key] = val
                            sem = sem_grp[key[1]] if key[0] == "g" else sem_eng[key[1]]
                            eng.wait_ge(sem, val)
                        ins = o.fn(eng)
                        if o.is_dma:
                            ins.then_inc(sem_grp[o.group], o.inc)
                        elif o.flag:
                            ins.then_inc(sem_eng[o.eng], 1)
                    if engname == "sp":
                        for g in final_groups:
                            eng.wait_ge(sem_grp[g], self.group_inc[g] * self.group_counts[g])
                return body

            attr = {"pe": "tensor", "act": "scalar", "dve": "vector", "pool": "gpsimd", "sp": "sync"}
            for e, lst in per_eng.items():
                getattr(block, attr[e])(make(e, lst))


class Arena:
    def __init__(self, ap, nwords):
        self.ap = ap
        self.n = nwords
        self.off = 0

    def _view(self, v, shape):
        if len(shape) == 2:
            return v
        if len(shape) == 3:
            return v.rearrange("p (a b) -> p a b", a=shape[1], b=shape[2])
        if len(shape) == 4:
            return v.rearrange("p (a b c) -> p a b c", a=shape[1], b=shape[2], c=shape[3])
        raise ValueError(shape)

    def f32(self, shape):
        n = int(np.prod(shape[1:]))
        assert self.off + n <= self.n, ("arena overflow", self.off, n)
        v = self.ap[:, self.off:self.off + n]
        self.off += n
        return self._view(v, shape)

    def bf16(self, shape):
        n = int(np.prod(shape[1:]))
        w = (n + 1) // 2
        assert self.off + w <= self.n, ("arena overflow", self.off, w)
        v = self.ap[:, self.off:self.off + w].bitcast(BF16)[:, 0:n]
        self.off += w
        return self._view(v, shape)


def fchunk(X):
    return X.rearrange("(kc p) t -> p kc t", p=128)


def build_program():
    nc = bass.Bass("TRN2", target_bir_lowering=False)
    p = Prog(nc)
    T = {}
    GROUPS = [[0, 1, 2, 3], [4, 5, 6, 7]]

    def dram(name, shape, dtype, kind):
        if kind == "cc":
            T[name] = nc.dram_tensor(name, list(shape), dtype).ap()
        else:
            k = {"in": "ExternalInput", "out": "ExternalOutput", "tmp": "Internal"}[kind]
            T[name] = nc.dram_tensor(name, list(shape), dtype, kind=k).ap()
        return T[name]

    layers = [0, 1]
    for nm, shp in (("cT", [128, 16]), ("g_normT", [128, 2, 2, 16]), ("conv_wT", [128, 2, 3, 88]),
                    ("conv_bT", [128, 2, 88]), ("g_finalT", [128, 16]), ("gqk", [128, 4]), ("hmask", [128, 2]), ("sel", [128, 8]),
                    ("w_adaS", [2, D, 3072]), ("b_adaS", [128, 48]), ("xT", [D, NT]), ("a_w_in", [D, 2 * D]), ("a_w_out", [D, D]),
                    ("w_sT", [128, D]), ("g_v_bc", [128, D]), ("b_s_bc", [128, D]),
                    ("f_w_up0", [D, 2 * DFF]), ("f_w_down0", [DFF, D]), ("f_w_up1", [D, 2 * DFF]), ("f_w_down1", [DFF, D]),
                    ("w_qk", [D, 2560]), ("rperm", [128, 128]), ("w_v", [D, 512]), ("cosT", [128, NT]), ("sinT", [128, NT]),
                    ("b_w_o", [D, D])):
        dram(nm, shp, F32, "in")
    for nm, shp in (("Wb_in", [D, 2 * D]), ("Wb_out", [D, D]), ("Wb_up0", [22, 128, 8192]), ("Wb_down0", [16, 128, NF * 128]),
                    ("Wb_up1", [22, 128, 8192]), ("Wb_down1", [16, 128, NF * 128]), ("Wb_qk", [D, 2560]),
                    ("Wb_v", [D, 512]), ("Wb_o", [D, D]), ("QT", [16, 128, NT]), ("OT", [16, 128, NT])):
        dram(nm, shp, BF16, "tmp")
    for nm in ("x1T", "x2T", "x3T", "x4T"):
        dram(nm, [D, NT], F32, "tmp")
    dram("outT", [D, NT], F32, "out")
    dram("KVin", [128, 128, 128], F32, "cc")
    dram("KVout", [128, 512, 128], F32, "cc")
    dram("A_in", [128, 48], F32, "cc")
    dram("A_out", [512, 48], F32, "cc")
    for l in (0, 1):
        dram("E_in%d" % l, [128, 32], F32, "cc")
        dram("E_out%d" % l, [512, 32], F32, "cc")
    KVin_bf = T["KVin"].bitcast(BF16)
    KVout_bf = T["KVout"].bitcast(BF16)

    with contextlib.ExitStack() as st:
        arena_t = st.enter_context(nc.sbuf_tensor("arena", [128, 49152], F32))
        ps = st.enter_context(nc.psum_tensor("ps", [128, 8, 512], F32))
        A = Arena(arena_t[:, :], 49152)

        ones_bf = A.bf16([128, 128])
        ones_f = A.f32([128, 128])
        eps_t = A.f32([128, 1])
        zero_t = A.f32([128, 1])
        cT = A.f32([128, 16])
        cond = A.f32([128, 16])
        modT = A.f32([128, 2, 96])
        gmul = A.f32([128, 2, 2, 16])
        b_adaT = A.f32([128, 2, 96])
        g_normT = A.f32([128, 2, 2, 16])
        conv_wT = A.f32([128, 2, 3, 88])
        conv_bT = A.f32([128, 2, 88])
        g_finalT = A.f32([128, 16])
        gqk = A.f32([128, 4])
        hmask = A.f32([128, 2])
        sel = A.f32([128, 8])
        halo_sb = A.f32([128, 16, 2])
        w_sT = A.bf16([128, 16, 128])
        PERSIST = A.off

        p.op("dve", lambda e: e.memset(ones_bf, 1.0), writes=["ones_bf"])
        p.op("dve", lambda e: e.memset(ones_f, 1.0), writes=["ones_f"])
        p.op("dve", lambda e: e.memset(eps_t, EPS), writes=["eps"])
        p.op("dve", lambda e: e.memset(zero_t, 0.0), writes=["zero"])
        for nm, dst in (("cT", cT), ("g_normT", g_normT), ("conv_wT", conv_wT),
                        ("conv_bT", conv_bT), ("g_finalT", g_finalT), ("gqk", gqk), ("hmask", hmask), ("sel", sel)):
            p.dma("sp", "const", lambda e, nm=nm, dst=dst: e.dma_start(out=dst, in_=T[nm]), writes=["c_" + nm])
        CONST_KEYS = ["c_cT", "c_b_adaT", "c_g_normT", "c_conv_wT", "c_conv_bT", "c_g_finalT", "c_gqk", "c_hmask",
                      "ones_bf", "ones_f", "eps", "zero"]

        p.dma("pool", "wsT", lambda e: e.dma_start(out=w_sT, in_=T["w_sT"].rearrange("p (g i) -> p g i", g=16)), writes=["w_sT"])

        def convert(src, dst, key):
            R, C = T[src].shape
            tot = R * C
            per = tot // 128
            sv = T[src].rearrange("(p r) c -> p (r c)", p=128)
            dv = T[dst].rearrange("(p r) c -> p (r c)", p=128)
            CH = 8192
            grp = "cv_" + key
            p.bg_groups.add(grp)
            for a in range(0, per, CH):
                b = min(per, a + CH)
                p.dma("pool", grp, lambda e, a=a, b=b: e.dma_start(out=dv[:, a:b], in_=sv[:, a:b]), writes=["W_" + key])

        def convert_up(l, gate=()):
            grp = "cv_up%d" % l
            p.bg_groups.add(grp)
            S5 = T["f_w_up%d" % l].rearrange("(kc p) (wh fb c) -> fb p wh kc c", p=128, wh=2, fb=22, c=256)
            for fb in range(22):
                p.dma("pool", grp, lambda e, fb=fb: e.dma_start(
                    out=T["Wb_up%d" % l][fb].rearrange("p (wh kc c) -> p wh kc c", wh=2, kc=16, c=256), in_=S5[fb]),
                    reads=list(gate) if fb == 0 else [], writes=["W_up%d" % l])

        def convert_down(l):
            grp = "cv_down%d" % l
            p.bg_groups.add(grp)
            S4 = T["f_w_down%d" % l].rearrange("(f p) (dc c) -> dc p f c", p=128, c=128)
            for dc in range(16):
                p.dma("pool", grp, lambda e, dc=dc: e.dma_start(
                    out=T["Wb_down%d" % l][dc].rearrange("p (f c) -> p f c", f=NF, c=128), in_=S4[dc]),
                    writes=["W_down%d" % l])

        def convert_first():
            convert("a_w_in", "Wb_in", "in")
            convert("a_w_out", "Wb_out", "out")

        def convert_early():
            convert_up(0, gate=[("ob", 0)])
            convert_down(0)

        def convert_late():
            convert("w_v", "Wb_v", "v")
            convert("w_qk", "Wb_qk", "qk")
            convert("b_w_o", "Wb_o", "o")
            convert_up(1)
            convert_down(1)

        convert_first()

        def phase_ada_all():
            A.off = PERSIST
            wbl = [A.f32([128, 16, 512]) for _ in range(2)]
            modS = A.f32([128, 48])
            b_adaS = A.f32([128, 48])
            p.dma("sp", "const", lambda e: e.dma_start(out=b_adaS, in_=T["b_adaS"]), writes=["b_adaS"])
            p.op("act", lambda e: e.activation(out=cond, in_=cT, func=AF.Silu), reads=["c_cT"], writes=["cond"])
            blocks = [(l, cb) for l in range(2) for cb in range(6)]
            Wl = [fchunk(T["w_adaS"][l]) for l in range(2)]

            def load(i):
                l, cb = blocks[i]
                s = i % 2
                p.dma("sp", ("ada", s), lambda e, l=l, cb=cb, s=s: e.dma_start(out=wbl[s], in_=Wl[l][:, :, cb * 512:(cb + 1) * 512]),
                      writes=[("adaw", s)])
            load(0)
            for i, (l, cb) in enumerate(blocks):
                if i + 1 < len(blocks):
                    load(i + 1)
                s = i % 2
                for j in range(4):
                    col = l * 24 + cb * 4 + j
                    for kc in range(KC):
                        p.op("pe", lambda e, s=s, j=j, kc=kc, col=col: e.matmul(
                            ps[:, 0, col:col + 1], lhsT=wbl[s][:, kc, j * 128:(j + 1) * 128], rhs=cond[:, kc:kc + 1],
                            start=(kc == 0), stop=(kc == KC - 1)),
                            reads=[("adaw", s), "cond"], writes=[("ps", 0)])
            p.op("dve", lambda e: e.tensor_tensor(out=modS, in0=ps[:, 0, 0:48], in1=b_adaS, op=ALU.add),
                 reads=[("ps", 0), "b_adaS"], writes=["modS"])
            p.dma("sp", "ain", lambda e: e.dma_start(out=T["A_in"], in_=modS), reads=["modS"], writes=["A_in"])
            p.dma("pool", "cc_ada", lambda e: e.collective_compute(
                "AllGather", ALU.bypass, replica_groups=GROUPS, ins=[T["A_in"]], outs=[T["A_out"]]),
                reads=["A_in"], writes=["A_out"], inc=1)
            p.dma("sp", "aout", lambda e: e.dma_start(
                out=modT.rearrange("p l (r j) -> p l r j", r=4, j=24),
                in_=T["A_out"].rearrange("(r p) (l j) -> p l r j", p=128, l=2)),
                reads=["A_out"], writes=["modT"])
            for l in range(2):
                for s_ in range(2):
                    sc0 = 16 + 48 * s_
                    p.op("dve", lambda e, l=l, s_=s_, sc0=sc0: e.scalar_tensor_tensor(
                        out=gmul[:, l, s_, :], in0=modT[:, l, sc0:sc0 + 16], scalar=1.0, in1=g_normT[:, l, s_, :],
                        op0=ALU.add, op1=ALU.mult), reads=["modT", "c_g_normT"], writes=["gmul"])
            p.barrier()

        def norm_mod(xin, N, sq, psb, rstd, tmps, hT, gm, sh, xkey, hkey, plain_out=None, sqkey="sq"):
            p.op("act", lambda e: e.activation(out=sq[:, :, 0:N], in_=xin[:, :, 0:N], func=AF.Square),
                 reads=[xkey], writes=[sqkey])
            for kc in range(KC):
                p.op("pe", lambda e, kc=kc: e.matmul(ps[:, psb, 0:N], lhsT=ones_bf, rhs=sq[:, kc, 0:N],
                                                     start=(kc == 0), stop=(kc == KC - 1)),
                     reads=[sqkey, "ones_bf"], writes=[("ps", psb)])
            p.op("act", lambda e: e.activation(out=rstd[:, 0:N], in_=ps[:, psb, 0:N], func=AF.Sqrt, bias=eps_t[:, 0:1],
                                               scale=1.0 / D), reads=[("ps", psb), "eps"], writes=["rstd"])
            p.op("dve", lambda e: e.reciprocal(out=rstd[:, 0:N], in_=rstd[:, 0:N]), reads=["rstd"], writes=["rstd"])
            for kc in range(KC):
                if plain_out is not None:
                    p.op("dve", lambda e, kc=kc: e.scalar_tensor_tensor(
                        out=plain_out[:, kc, 0:N], in0=xin[:, kc, 0:N], scalar=gm[:, kc:kc + 1], in1=rstd[:, 0:N],
                        op0=ALU.mult, op1=ALU.mult), reads=[xkey, "rstd", "c_g_finalT"], writes=[hkey])
                    continue
                tb = tmps[kc % len(tmps)]
                tk = ("nm_tmp", kc % len(tmps))
                p.op("dve", lambda e, kc=kc, tb=tb: e.scalar_tensor_tensor(
                    out=tb[:, 0:N], in0=xin[:, kc, 0:N], scalar=gm[:, kc:kc + 1], in1=rstd[:, 0:N],
                    op0=ALU.mult, op1=ALU.mult), reads=[xkey, "rstd", "gmul"], writes=[tk])
                p.op("act", lambda e, kc=kc, tb=tb: e.activation(
                    out=hT[:, kc, 0:N], in_=tb[:, 0:N], func=AF.Identity, bias=sh[:, kc:kc + 1], scale=1.0),
                    reads=[tk, "modT"], writes=[hkey])

        class WStream:
            def __init__(self, name, slots, items, wkeys):
                self.name = name
                self.slots = slots
                self.items = items
                self.wkeys = wkeys
                self.next = 0

            def issue_upto(self, k):
                while self.next <= min(k, len(self.items) - 1):
                    i = self.next
                    s = i % len(self.slots)
                    for (o_ap, i_ap) in self.items[i](self.slots[s]):
                        p.dma("sp", (self.name, s), lambda e, o_ap=o_ap, i_ap=i_ap: e.dma_start(out=o_ap, in_=i_ap),
                              reads=self.wkeys, writes=[(self.name, s)])
                    self.next += 1

            def use(self, i):
                self.issue_upto(i + len(self.slots) - 1)
                s = i % len(self.slots)
                return self.slots[s], (self.name, s)

        def proj_residual(rhsT, rhskey, ws, ws_base, xsrc, xdst, gt, tok0, N, xr, ob, dkey, banks):
            XS = fchunk(xsrc)
            XD = fchunk(xdst)
            pend = []
            for ob_ in range(4):
                wblk, wk = ws.use(ws_base + ob_)
                for j in range(4):
                    dc = ob_ * 4 + j
                    bk = banks[dc % len(banks)]
                    rs = dc % 2
                    p.dma("sp", ("xr", rs), lambda e, dc=dc, rs=rs: e.dma_start(out=xr[rs][:, 0:N], in_=XS[:, dc, tok0:tok0 + N]),
                          writes=[("xr", rs)])
                    while pend:
                        pend.pop(0)()
                    for g in range(KC):
                        p.op("pe", lambda e, g=g, j=j, bk=bk, wblk=wblk: e.matmul(
                            ps[:, bk, 0:N], lhsT=wblk[:, g, j * 128:(j + 1) * 128], rhs=rhsT[:, g, 0:N],
                            start=(g == 0), stop=(g == KC - 1)), reads=[wk, rhskey], writes=[("ps", bk)])
                    p.op("dve", lambda e, dc=dc, bk=bk, rs=rs: e.scalar_tensor_tensor(
                        out=ob[rs][:, 0:N], in0=ps[:, bk, 0:N], scalar=gt[:, dc:dc + 1], in1=xr[rs][:, 0:N],
                        op0=ALU.mult, op1=ALU.add), reads=[("ps", bk), ("xr", rs), "modT"], writes=[("ob", rs)])
                    pend.append(lambda dc=dc, rs=rs: p.dma("sp", ("xo", rs), lambda e: e.dma_start(
                        out=XD[:, dc, tok0:tok0 + N], in_=ob[rs][:, 0:N]), reads=[("ob", rs)]))
            while pend:
                pend.pop(0)()

        def colblocks(Wb, c0, nblk, width=512):
            V = fchunk(Wb)
            return [(lambda slot, i=i: [(slot[:, :, 0:width], V[:, :, c0 + i * width:c0 + (i + 1) * width])]) for i in range(nblk)]

        def phase_gmlp():
            A.off = PERSIST
            xt = A.f32([128, 16, 512])
            bufA = A.bf16([128, 16, 512])
            bufB = A.bf16([128, 16, 512])
            vn = A.bf16([128, 4, 2048])
            rstd = A.f32([128, 512])
            tmps = [A.f32([128, 512]) for _ in range(2)]
            vts = [A.f32([128, 512]) for _ in range(2)]
            junk = A.bf16([128, 512])
            ssq = A.f32([128, 16])
            ssv = A.f32([128, 4])
            svts = [A.f32([128, 512]) for _ in range(2)]
            g_v_bc = A.f32([128, 2048])
            b_s_bc = A.f32([128, 2048])
            wsl = [A.bf16([128, 16, 512]) for _ in range(3)]
            xr = [A.f32([128, 512]) for _ in range(2)]
            ob = [A.f32([128, 512]) for _ in range(2)]
            p.dma("sp", "const", lambda e: e.dma_start(out=g_v_bc, in_=T["g_v_bc"]), writes=["g_v_bc"])
            p.dma("sp", "const", lambda e: e.dma_start(out=b_s_bc, in_=T["b_s_bc"]), writes=["b_s_bc"])
            ntile = NT // 512
            items = []
            for tt in range(ntile):
                items += colblocks(T["Wb_in"], 2048, 4) + colblocks(T["Wb_in"], 0, 4) + colblocks(T["Wb_out"], 0, 4)
            ws = WStream("gw", wsl, items, ["W_in", "W_out"])
            XS = fchunk(T["xT"])
            sh1 = modT[:, 0, 0:16]
            gt1 = modT[:, 0, 32:48]
            for tt in range(ntile):
                tok0 = tt * 512
                p.dma("sp", "xt", lambda e, tok0=tok0: e.dma_start(out=xt, in_=XS[:, :, tok0:tok0 + 512]),
                      reads=[("X", "xT")], writes=["xt"])
                norm_mod(xt, 512, bufA, 7, rstd, tmps, bufB, gmul[:, 0, 0, :], sh1, "xt", "bufB", sqkey="bufA")
                base = tt * 12
                for vb in range(4):
                    wblk, wk = ws.use(base + vb)
                    for tcn in range(4):
                        u = vb * 4 + tcn
                        bk = u % 4
                        for kc in range(KC):
                            p.op("pe", lambda e, kc=kc, tcn=tcn, bk=bk, wblk=wblk: e.matmul(
                                ps[:, bk, :], lhsT=bufB[:, kc, tcn * 128:(tcn + 1) * 128], rhs=wblk[:, kc, :],
                                start=(kc == 0), stop=(kc == KC - 1)), reads=[wk, "bufB"], writes=[("ps", bk)])
                        vt = vts[u % 2]
                        vk = ("vt", u % 2)
                        p.op("act", lambda e, bk=bk, vt=vt: e.activation(out=vt, in_=ps[:, bk, :], func=AF.Gelu),
                             reads=[("ps", bk)], writes=[vk])
                        p.op("dve", lambda e, vt=vt, tcn=tcn, vb=vb: e.scalar_tensor_tensor(
                            out=junk, in0=vt, scalar=1.0, in1=vt, op0=ALU.mult, op1=ALU.mult,
                            accum_out=ssq[:, tcn * 4 + vb:tcn * 4 + vb + 1]), reads=[vk], writes=["junk", "ssq"])
                        p.op("act", lambda e, vt=vt, tcn=tcn, vb=vb: e.activation(out=vn[:, tcn, vb * 512:(vb + 1) * 512], in_=vt, func=AF.Copy),
                             reads=[vk], writes=[("vraw", tcn)])
                for tcn in range(4):
                    p.op("dve", lambda e, tcn=tcn: e.tensor_reduce(out=ssv[:, tcn:tcn + 1], in_=ssq[:, tcn * 4:(tcn + 1) * 4],
                                                                   axis=mybir.AxisListType.X, op=ALU.add),
                         reads=["ssq"], writes=["ssv"])
                p.op("act", lambda e: e.activation(out=ssv, in_=ssv, func=AF.Sqrt, bias=eps_t[:, 0:1], scale=1.0 / D),
                     reads=["ssv", "eps"], writes=["ssv"])
                p.op("dve", lambda e: e.reciprocal(out=ssv, in_=ssv), reads=["ssv"], writes=["ssv"])
                for tcn in range(4):
                    p.op("dve", lambda e, tcn=tcn: e.scalar_tensor_tensor(
                        out=vn[:, tcn, :], in0=vn[:, tcn, :], scalar=ssv[:, tcn:tcn + 1], in1=g_v_bc,
                        op0=ALU.mult, op1=ALU.mult), reads=[("vraw", tcn), "ssv", "g_v_bc"], writes=[("vn", tcn)])
                for ub in range(4):
                    wblk, wk = ws.use(base + 4 + ub)
                    for j in range(4):
                        uc = ub * 4 + j
                        bk = uc % 4
                        for kc in range(KC):
                            p.op("pe", lambda e, kc=kc, j=j, bk=bk, wblk=wblk: e.matmul(
                                ps[:, bk, :], lhsT=wblk[:, kc, j * 128:(j + 1) * 128], rhs=bufB[:, kc, :],
                                start=(kc == 0), stop=(kc == KC - 1)), reads=[wk, "bufB"], writes=[("ps", bk)])
                        p.op("act", lambda e, bk=bk, uc=uc: e.activation(out=bufA[:, uc, :], in_=ps[:, bk, :], func=AF.Gelu),
                             reads=[("ps", bk)], writes=["bufA"])
                for tcn in range(4):
                    for gq in range(4):
                        u = tcn * 4 + gq
                        bk = 4 + u % 3
                        for gi in range(4):
                            g = gq * 4 + gi
                            p.op("pe", lambda e, g=g, gi=gi, bk=bk, tcn=tcn: e.matmul(
                                ps[:, bk, gi * 128:(gi + 1) * 128], lhsT=vn[:, tcn, g * 128:(g + 1) * 128], rhs=w_sT[:, g, :],
                                start=True, stop=True), reads=[("vn", tcn), "w_sT"], writes=[("ps", bk)])
                        sv = svts[u % 2]
                        sk = ("svt", u % 2)
                        p.op("dve", lambda e, bk=bk, sv=sv, gq=gq: e.tensor_tensor(
                            out=sv, in0=ps[:, bk, :], in1=b_s_bc[:, gq * 512:(gq + 1) * 512], op=ALU.add),
                            reads=[("ps", bk), "b_s_bc"], writes=[sk])
                        p.op("dve", lambda e, sv=sv, gq=gq, tcn=tcn: e.tensor_tensor(
                            out=bufB[:, gq * 4:(gq + 1) * 4, tcn * 128:(tcn + 1) * 128],
                            in0=sv.rearrange("p (a b) -> p a b", a=4, b=128),
                            in1=bufA[:, gq * 4:(gq + 1) * 4, tcn * 128:(tcn + 1) * 128], op=ALU.mult),
                            reads=[sk, "bufA"], writes=["bufB"])
                proj_residual(bufB, "bufB", ws, base + 8, T["xT"], T["x1T"], gt1, tok0, 512, xr, ob, "x1", [0, 1, 2, 3])
                if tt == 0:
                    convert_early()
            p.barrier()

        def phase_ffn(l, xsrc_name, xdst_name):
            A.off = PERSIST
            act = A.bf16([128, 2, NF, SUBW])
            act_off_end = A.off
            h2T = A.bf16([128, 2, 16, SUBW + 2])
            rstd = A.f32([128, 512])
            tmps = [A.f32([128, 512]) for _ in range(2)]
            tg = [A.f32([128, 2, SUBW]) for _ in range(2)]
            tv = [A.f32([128, 2, SUBW]) for _ in range(2)]
            wsl = [A.bf16([128, 2, 16, 256]) for _ in range(3)]
            xr = [A.f32([128, 2, SUBW]) for _ in range(2)]
            ob = [A.f32([128, 2, SUBW]) for _ in range(2)]
            save = A.off
            A.off = PERSIST
            xin = A.f32([128, 16, SUBW + 2])
            sq = A.bf16([128, 16, SUBW + 2])
            assert A.off <= act_off_end
            A.off = save
            XS = fchunk(T[xsrc_name])
            XD = fchunk(T[xdst_name])
            Wup = T["Wb_up%d" % l]
            Wdn = T["Wb_down%d" % l]
            sh2 = modT[:, l, 48:64]
            gt2 = modT[:, l, 80:96]
            gm2 = gmul[:, l, 1, :]
            nsup = NSUBT // 2
            items = []
            for s_ in range(nsup):
                for fb in range(NF // 2):
                    items.append(lambda slot, fb=fb: [(slot.rearrange("p a k c -> p (a k c)"), Wup[fb])])
                for dc in range(KC):
                    items.append(lambda slot, dc=dc: [(slot.rearrange("p a k c -> p (a k c)")[:, 0:NF * 128], Wdn[dc])])
            ws = WStream("fw", wsl, items, ["W_up%d" % l, "W_down%d" % l])
            per_sup = NF // 2 + KC
            unit = 0
            for sp_ in range(nsup):
                toks = []
                for sub in range(2):
                    s_ = sp_ * 2 + sub
                    tok0 = min(s_ * SUBW, NT - SUBW)
                    toks.append(tok0)
                    if s_ == 0:
                        p.dma("sp", "xin", lambda e: e.dma_start(out=xin[:, :, 1:SUBW + 2], in_=XS[:, :, 0:SUBW + 1]),
                              reads=[("X", xsrc_name)], writes=["actreg"])
                        p.op("dve", lambda e: e.tensor_copy(out=xin[:, :, 0:1], in_=halo_sb[:, :, 0:1]),
                             reads=["halo_sb"], writes=["actreg"])
                    elif s_ == NSUBT - 1:
                        p.dma("sp", "xin", lambda e, tok0=tok0: e.dma_start(out=xin[:, :, 0:SUBW + 1], in_=XS[:, :, tok0 - 1:NT]),
                              reads=[("X", xsrc_name)], writes=["actreg"])
                        p.op("dve", lambda e: e.tensor_copy(out=xin[:, :, SUBW + 1:SUBW + 2], in_=halo_sb[:, :, 1:2]),
                             reads=["halo_sb"], writes=["actreg"])
                    else:
                        p.dma("sp", "xin", lambda e, tok0=tok0: e.dma_start(out=xin, in_=XS[:, :, tok0 - 1:tok0 + SUBW + 1]),
                              reads=[("X", xsrc_name)], writes=["actreg"])
                    norm_mod(xin, SUBW + 2, sq, 7, rstd, tmps, h2T[:, sub, :, :], gm2, sh2, "actreg", ("h2T", sub), sqkey="actreg")
                    if s_ == 0:
                        p.op("dve", lambda e, sub=sub: e.tensor_scalar(out=h2T[:, sub, :, 0:1], in0=h2T[:, sub, :, 0:1],
                                                                        scalar1=hmask[:, 0:1], scalar2=None, op0=ALU.mult),
                             reads=[("h2T", sub), "c_hmask"], writes=[("h2T", sub)])
                    if s_ == NSUBT - 1:
                        p.op("dve", lambda e, sub=sub: e.tensor_scalar(out=h2T[:, sub, :, SUBW + 1:SUBW + 2],
                                                                        in0=h2T[:, sub, :, SUBW + 1:SUBW + 2],
                                                                        scalar1=hmask[:, 1:2], scalar2=None, op0=ALU.mult),
                             reads=[("h2T", sub), "c_hmask"], writes=[("h2T", sub)])
                base = sp_ * per_sup
                for f in range(NF):
                    wblk, wk = ws.use(base + f // 2)
                    fl = f % 2
                    for which in range(2):
                        slot = unit % 3
                        unit += 1
                        b0 = 2 * slot
                        for kc in range(KC):
                            for sub in range(2):
                                p.op("pe", lambda e, kc=kc, sub=sub, b0=b0, which=which, fl=fl, wblk=wblk: e.matmul(
                                    ps[:, b0 + sub, 0:SUBW + 2], lhsT=wblk[:, which, kc, fl * 128:(fl + 1) * 128],
                                    rhs=h2T[:, sub, kc, :], start=(kc == 0), stop=(kc == KC - 1)),
                                    reads=[wk, ("h2T", 0), ("h2T", 1)], writes=[("ps", b0), ("ps", b0 + 1)])
                        tb = (tg if which == 0 else tv)[f % 2]
                        tk = ("tg" if which == 0 else "tv", f % 2)
                        fc = which * NF + f
                        pk = [("ps", b0), ("ps", b0 + 1)]
                        p.op("act", lambda e, b0=b0, tb=tb, fc=fc: e.activation(
                            out=tb, in_=ps[:, b0:b0 + 2, 1:SUBW + 1], func=AF.Identity,
                            bias=conv_bT[:, l, fc:fc + 1], scale=conv_wT[:, l, 1, fc:fc + 1]),
                            reads=pk + ["c_conv_wT", "c_conv_bT"], writes=[tk])
                        p.op("dve", lambda e, b0=b0, tb=tb, fc=fc: e.scalar_tensor_tensor(
                            out=tb, in0=ps[:, b0:b0 + 2, 0:SUBW], scalar=conv_wT[:, l, 0, fc:fc + 1], in1=tb,
                            op0=ALU.mult, op1=ALU.add), reads=pk + [tk], writes=[tk])
                        p.op("dve", lambda e, b0=b0, tb=tb, fc=fc: e.scalar_tensor_tensor(
                            out=tb, in0=ps[:, b0:b0 + 2, 2:SUBW + 2], scalar=conv_wT[:, l, 2, fc:fc + 1], in1=tb,
                            op0=ALU.mult, op1=ALU.add), reads=pk + [tk], writes=[tk])
                        if which == 0:
                            p.op("act", lambda e, tb=tb: e.activation(out=tb, in_=tb, func=AF.Gelu), reads=[tk], writes=[tk])
                    p.op("dve", lambda e, f=f: e.tensor_tensor(out=act[:, :, f, :], in0=tg[f % 2], in1=tv[f % 2], op=ALU.mult),
                         reads=[("tg", f % 2), ("tv", f % 2)], writes=["actreg"])
                pend = []
                for dc in range(KC):
                    wblk, wk = ws.use(base + NF // 2 + dc)
                    wd = wblk.rearrange("p a k c -> p (a k c)")[:, 0:NF * 128].rearrange("p (f c) -> p f c", f=NF, c=128)
                    b0 = 6 if dc % 2 == 0 else 4
                    rs = dc % 2
                    for sub in range(2):
                        p.dma("sp", ("xr", rs), lambda e, dc=dc, rs=rs, sub=sub, tk_=toks[sub]: e.dma_start(
                            out=xr[rs][:, sub, :], in_=XS[:, dc, tk_:tk_ + SUBW]),
                            writes=[("xr", rs)])
                    while pend:
                        pend.pop(0)()
                    for f in range(NF):
                        for sub in range(2):
                            p.op("pe", lambda e, f=f, sub=sub, b0=b0, wd=wd: e.matmul(
                                ps[:, b0 + sub, 0:SUBW], lhsT=wd[:, f, :], rhs=act[:, sub, f, :],
                                start=(f == 0), stop=(f == NF - 1)), reads=[wk, "actreg"], writes=[("ps", b0), ("ps", b0 + 1)])
                    p.op("dve", lambda e, dc=dc, b0=b0, rs=rs: e.scalar_tensor_tensor(
                        out=ob[rs], in0=ps[:, b0:b0 + 2, 0:SUBW], scalar=gt2[:, dc:dc + 1], in1=xr[rs],
                        op0=ALU.mult, op1=ALU.add), reads=[("ps", b0), ("ps", b0 + 1), ("xr", rs), "modT"], writes=[("ob", rs)])
                    for sub in range(2):
                        pend.append(lambda dc=dc, rs=rs, sub=sub, tk_=toks[sub]: p.dma("sp", ("xo", rs), lambda e: e.dma_start(
                            out=XD[:, dc, tk_:tk_ + SUBW], in_=ob[rs][:, sub, :]), reads=[("ob", rs)]))
                while pend:
                    pend.pop(0)()
            p.barrier()

        def phase_qkv():
            A.off = PERSIST
            xt = A.f32([128, 16, 512])
            sq = A.bf16([128, 16, 512])
            hT = A.bf16([128, 16, 512])
            rstd = A.f32([128, 512])
            tmps = [A.f32([128, 512]) for _ in range(2)]
            cs = A.f32([128, 512])
            sn = A.f32([128, 512])
            t1 = [A.f32([128, 512]) for _ in range(2)]
            t2 = [A.f32([128, 512]) for _ in range(2)]
            sqh = [A.bf16([128, 512]) for _ in range(2)]
            rs_ = [A.f32([128, 512]) for _ in range(2)]
            qr = [A.bf16([128, 512]) for _ in range(2)]
            vb_ = [A.bf16([128, 512]) for _ in range(2)]
            wsl = [A.bf16([128, 2, 16, 256]) for _ in range(3)]
            ntile = NT // 512
            Vv = fchunk(T["Wb_v"])
            Vqk = fchunk(T["Wb_qk"])
            rp_f = A.f32([128, 128])
            rp_b = A.bf16([128, 128])
            qb = [A.bf16([128, 512]) for _ in range(2)]
            p.dma("sp", "const", lambda e: e.dma_start(out=rp_f, in_=T["rperm"]), writes=["rp_f"])
            p.op("dve", lambda e: e.tensor_copy(out=rp_b, in_=rp_f), reads=["rp_f"], writes=["rp_b"])
            items = []
            for tt in range(ntile):
                items.append(lambda slot: [(slot.rearrange("p a k c -> p (a k c)").rearrange("p (k c) -> p k c", k=16, c=512), Vv[:, :, :])])
                for hb in range(10):
                    items.append(lambda slot, hb=hb: [(slot[:, 0, :, :], Vqk[:, :, hb * 256:(hb + 1) * 256])])
            ws = WStream("qw", wsl, items, ["W_v", "W_qk"])
            XS = fchunk(T["x2T"])
            sh1 = modT[:, 1, 0:16]
            KV5 = KVin_bf.rearrange("(kv h c) p f -> kv h c p f", kv=2, h=4, c=16)

            def kv_collectives(tt):
                for h in range(4):
                    for j in range(2):
                        for kv in range(2):
                            c = kv * 64 + h * 16 + 2 * tt + j
                            p.dma("pool", "cc_kv", lambda e, c=c: e.collective_compute(
                                "AllGather", ALU.bypass, replica_groups=GROUPS, ins=[T["KVin"][c]], outs=[T["KVout"][c]]),
                                reads=[("KVin", c)], writes=["KVout"], inc=1)
            hcnt = 0
            for tt in range(ntile):
                tok0 = tt * 512
                p.dma("sp", "xt", lambda e, tok0=tok0: e.dma_start(out=xt, in_=XS[:, :, tok0:tok0 + 512]),
                      reads=[("X", "x2T")], writes=["xt"])
                p.dma("sp", "cs", lambda e, tok0=tok0: e.dma_start(out=cs, in_=T["cosT"][:, tok0:tok0 + 512]), writes=["cs"])
                p.dma("sp", "cs", lambda e, tok0=tok0: e.dma_start(out=sn, in_=T["sinT"][:, tok0:tok0 + 512]), writes=["sn"])
                norm_mod(xt, 512, sq, 7, rstd, tmps, hT, gmul[:, 1, 0, :], sh1, "xt", "hT")
                base = tt * 11
                wblk, wk = ws.use(base)
                wv = wblk.rearrange("p a k c -> p (a k c)").rearrange("p (k c) -> p k c", k=16, c=512)
                for tcn in range(4):
                    bk = 4 + tcn % 2
                    for kc in range(KC):
                        p.op("pe", lambda e, kc=kc, tcn=tcn, bk=bk, wv=wv: e.matmul(
                            ps[:, bk, :], lhsT=hT[:, kc, tcn * 128:(tcn + 1) * 128], rhs=wv[:, kc, :],
                            start=(kc == 0), stop=(kc == KC - 1)), reads=[wk, "hT"], writes=[("ps", bk)])
                    vs = tcn % 2
                    p.op("act", lambda e, bk=bk, vs=vs: e.activation(out=vb_[vs], in_=ps[:, bk, :], func=AF.Copy),
                         reads=[("ps", bk)], writes=[("vb", vs)])
                    vc = tt * 2 + tcn // 2
                    off = (tcn % 2) * 128
                    p.dma("sp", ("vst", vs), lambda e, vs=vs, vc=vc, off=off: e.dma_start(
                        out=KV5[1, :, vc, :, off:off + 128].rearrange("h p d -> p h d"),
                        in_=vb_[vs].rearrange("p (h d) -> p h d", h=4, d=128)),
                        reads=[("vb", vs)], writes=[("KVin", 64 + h_ * 16 + vc) for h_ in range(4)])
                pending = []
                for hb in range(10):
                    wblk, wk = ws.use(base + 1 + hb)
                    for hh in range(2):
                        head = hb * 2 + hh
                        hs = hcnt % 2
                        hcnt += 1
                        bq = 0 + 2 * hs
                        bp = 1 + 2 * hs
                        for kc in range(KC):
                            p.op("pe", lambda e, kc=kc, hh=hh, bq=bq, wblk=wblk: e.matmul(
                                ps[:, bq, :], lhsT=wblk[:, 0, kc, hh * 128:(hh + 1) * 128], rhs=hT[:, kc, :],
                                start=(kc == 0), stop=(kc == KC - 1)), reads=[wk, "hT"], writes=[("ps", bq)])
                        p.op("act", lambda e, bq=bq, hs=hs: e.activation(out=sqh[hs], in_=ps[:, bq, :], func=AF.Square),
                             reads=[("ps", bq)], writes=[("sqh", hs)])
                        p.op("act", lambda e, bq=bq, hs=hs: e.activation(out=qb[hs], in_=ps[:, bq, :], func=AF.Copy),
                             reads=[("ps", bq)], writes=[("qb", hs)])
                        while pending:
                            pending.pop(0)()

                        def stage2(head=head, hs=hs, bq=bq, bp=bp, tok0=tok0, tt=tt):
                            bs = 6
                            p.op("pe", lambda e: e.matmul(ps[:, bs, :], lhsT=ones_bf, rhs=sqh[hs], start=True, stop=True),
                                 reads=[("sqh", hs), "ones_bf"], writes=[("ps", bs)])
                            p.op("pe", lambda e: e.matmul(ps[:, bp, :], lhsT=rp_b, rhs=qb[hs], start=True, stop=True),
                                 reads=[("qb", hs), "rp_b"], writes=[("ps", bp)])
                            p.op("act", lambda e: e.activation(out=rs_[hs], in_=ps[:, bs, :], func=AF.Sqrt,
                                                               bias=eps_t[:, 0:1], scale=1.0 / 128),
                                 reads=[("ps", bs), "eps"], writes=[("rs", hs)])
                            p.op("dve", lambda e: e.reciprocal(out=rs_[hs], in_=rs_[hs]), reads=[("rs", hs)], writes=[("rs", hs)])
                            gc = 0 if head < 16 else 2
                            p.op("dve", lambda e: e.scalar_tensor_tensor(
                                out=t1[hs], in0=ps[:, bq, :], scalar=gqk[:, gc:gc + 1], in1=cs, op0=ALU.mult, op1=ALU.mult),
                                reads=[("ps", bq), "c_gqk", "cs"], writes=[("t1", hs)])
                            p.op("dve", lambda e: e.scalar_tensor_tensor(
                                out=t2[hs], in0=ps[:, bp, :], scalar=gqk[:, gc + 1:gc + 2], in1=sn, op0=ALU.mult, op1=ALU.mult),
                                reads=[("ps", bp), "c_gqk", "sn"], writes=[("t2", hs)])
                            p.op("dve", lambda e: e.tensor_tensor(out=t1[hs], in0=t1[hs], in1=t2[hs], op=ALU.add),
                                 reads=[("t1", hs), ("t2", hs)], writes=[("t1", hs)])
                            p.op("dve", lambda e: e.tensor_tensor(out=qr[hs], in0=t1[hs], in1=rs_[hs], op=ALU.mult),
                                 reads=[("t1", hs), ("rs", hs)], writes=[("qr", hs)])
                            if head < 16:
                                p.dma("sp", ("qst", hs), lambda e: e.dma_start(
                                    out=T["QT"][head, :, tok0:tok0 + 512], in_=qr[hs]), reads=[("qr", hs)])
                            else:
                                c0 = (head - 16) * 16 + 2 * tt
                                p.dma("sp", ("qst", hs), lambda e: e.dma_start(
                                    out=KVin_bf[c0:c0 + 2].rearrange("c d t -> d c t"),
                                    in_=qr[hs].rearrange("p (c t) -> p c t", c=2, t=256)),
                                    reads=[("qr", hs)], writes=[("KVin", c0), ("KVin", c0 + 1)])
                        pending.append(stage2)
                while pending:
                    pending.pop(0)()
                if tt >= 1:
                    kv_collectives(tt - 1)
            kv_collectives(ntile - 1)
            p.barrier()

        def phase_attn():
            A.off = PERSIST
            KTs = [A.bf16([128, 4, NT]) for _ in range(2)]
            Vs = [A.bf16([128, 4, 32, 128]) for _ in range(2)]
            Qs = [A.bf16([128, 4, 128]) for _ in range(3)]
            Ps = [A.bf16([128, 2, 512]) for _ in range(4)]
            acc2 = [A.f32([128, 2, 512]) for _ in range(2)]
            accs = [A.f32([128, 512]) for _ in range(2)]
            rinv = [A.f32([128, 512]) for _ in range(2)]
            osb = [A.bf16([128, 4, 128]) for _ in range(2)]
            scale = 1.0 / math.sqrt(128.0)
            def load_kv(h):
                s = h % 2
                for r in range(4):
                    p.dma("sp", ("kv", s), lambda e, h=h, s=s, r=r: e.dma_start(
                        out=KTs[s][:, r, :].rearrange("p (c t) -> p c t", c=16, t=256),
                        in_=KVout_bf[h * 16:(h + 1) * 16, r * 128:(r + 1) * 128, :].rearrange("c d t -> d c t")),
                        reads=["KVout"], writes=[("K", s)])
                    p.dma("sp", ("kv", s), lambda e, h=h, s=s, r=r: e.dma_start(
                        out=Vs[s][:, r, :, :].rearrange("p k d -> p (k d)").rearrange("p (c f) -> p c f", c=16, f=256),
                        in_=KVout_bf[64 + h * 16:64 + (h + 1) * 16, r * 128:(r + 1) * 128, :].rearrange("c q f -> q c f")),
                        reads=["KVout"], writes=[("V", s)])

            nq = NT // 128
            qi_all = [(h, qt) for h in range(4) for qt in range(nq)]

            def load_q(i):
                h, qt = qi_all[i]
                s = i % 3
                p.dma("sp", ("q", s), lambda e, h=h, qt=qt, s=s: e.dma_start(
                    out=Qs[s], in_=T["QT"][4 * h:4 * h + 4, :, qt * 128:(qt + 1) * 128].rearrange("g d q -> d g q")),
                    reads=["QT"], writes=[("Q", s)])

            load_kv(0)
            load_q(0)
            load_q(1)
            npair = 64
            slot_ctr = [0]
            pairs = [(i, kp) for i in range(len(qi_all)) for kp in range(npair)]
            sslot = {}
            NSL = 2
            ACCK = [("ps", 6), ("ps", 7)]

            def emit_S(n):
                i, kp = pairs[n]
                h, qt = qi_all[i]
                s = h % 2
                qs = i % 3
                if kp == 0:
                    if qt == 0 and h + 1 < 4:
                        load_kv(h + 1)
                    if i + 2 < len(qi_all):
                        load_q(i + 2)
                Kt = KTs[s].rearrange("p r t -> p (r t)")
                Qt = Qs[qs].rearrange("p g q -> p (g q)")
                sb = slot_ctr[0] % NSL
                slot_ctr[0] += 1
                sslot[n] = sb
                for j in range(2):
                    ktile = 2 * kp + j
                    p.op("pe", lambda e, sb=sb, j=j, ktile=ktile, Kt=Kt, Qt=Qt: e.matmul(
                        ps[:, 2 * sb + j, :], lhsT=Kt[:, ktile * 128:(ktile + 1) * 128], rhs=Qt, start=True, stop=True),
                        reads=[("K", s), ("Q", qs)], writes=[("ps", 2 * sb), ("ps", 2 * sb + 1)])

            def emit_rest(n):
                i, kp = pairs[n]
                h, qt = qi_all[i]
                s = h % 2
                Vt = Vs[s].rearrange("p r k d -> p (r k) d")
                ob_ = 4 + i % 2
                sb = sslot.pop(n)
                pb = n % 4
                p.op("act", lambda e, sb=sb, pb=pb: e.activation(
                    out=Ps[pb], in_=ps[:, 2 * sb:2 * sb + 2, :], func=AF.Exp, bias=zero_t[:, 0:1], scale=scale),
                    reads=[("ps", 2 * sb), ("ps", 2 * sb + 1), "zero"], writes=[("P", pb)])
                for j in range(2):
                    ktile = 2 * kp + j
                    p.op("pe", lambda e, pb=pb, j=j, ktile=ktile, Vt=Vt, ob_=ob_, kp=kp: e.matmul(
                        ps[:, ob_, :], lhsT=Vt[:, ktile, :], rhs=Ps[pb][:, j, :],
                        start=(kp == 0 and j == 0), stop=(kp == npair - 1 and j == 1)),
                        reads=[("V", s), ("P", pb)], writes=[("ps", ob_)])
                a2p = acc2[i % 2]
                apk = ("acc2p", i % 2)
                if kp % 4 == 1:
                    if kp == 1:
                        p.op("pool", lambda e, pb=pb, a2p=a2p: e.tensor_copy(out=a2p, in_=Ps[pb]), reads=[("P", pb)], writes=[apk])
                    else:
                        p.op("pool", lambda e, pb=pb, a2p=a2p: e.tensor_tensor(out=a2p, in0=a2p, in1=Ps[pb], op=ALU.add),
                             reads=[("P", pb), apk], writes=[apk])
                elif kp == 0:
                    p.op("dve", lambda e, pb=pb: e.tensor_copy(out=ps[:, 6:8, :], in_=Ps[pb]), reads=[("P", pb)], writes=ACCK)
                else:
                    p.op("dve", lambda e, pb=pb: e.tensor_tensor(out=ps[:, 6:8, :], in0=ps[:, 6:8, :], in1=Ps[pb], op=ALU.add),
                         reads=[("P", pb)] + ACCK, writes=ACCK)
                if kp == npair - 1:
                    r_ = i % 2
                    p.op("dve", lambda e, r_=r_: e.tensor_copy(out=rinv[r_], in_=ps[:, 6, :]), reads=ACCK, writes=[("rinv", r_)])
                    p.op("dve", lambda e, r_=r_: e.tensor_tensor(out=accs[r_], in0=ps[:, 7, :], in1=rinv[r_], op=ALU.add),
                         reads=ACCK + [("rinv", r_)], writes=[("accs", r_)])
                    p.op("dve", lambda e, r_=r_, a2p=a2p: e.tensor_tensor(out=accs[r_], in0=accs[r_], in1=a2p[:, 0, :], op=ALU.add),
                         reads=[apk, ("accs", r_)], writes=[("accs", r_)])
                    p.op("dve", lambda e, r_=r_, a2p=a2p: e.tensor_tensor(out=accs[r_], in0=accs[r_], in1=a2p[:, 1, :], op=ALU.add),
                         reads=[apk, ("accs", r_)], writes=[("accs", r_)])

            def emit_final(i):
                h, qt = qi_all[i]
                r_ = i % 2
                ob_ = 4 + i % 2
                sbk = 2 * (slot_ctr[0] % NSL)
                slot_ctr[0] += 1
                p.op("pe", lambda e, r_=r_, sbk=sbk: e.matmul(ps[:, sbk, :], lhsT=ones_f, rhs=accs[r_], start=True, stop=True),
                     reads=[("accs", r_), "ones_f"], writes=[("ps", sbk), ("ps", sbk + 1)])
                p.op("dve", lambda e, r_=r_, sbk=sbk: e.reciprocal(out=rinv[r_], in_=ps[:, sbk, :]),
                     reads=[("ps", sbk), ("ps", sbk + 1)], writes=[("rinv", r_)])
                p.op("dve", lambda e, r_=r_, ob_=ob_: e.tensor_tensor(
                    out=osb[r_].rearrange("p g q -> p (g q)"), in0=ps[:, ob_, :], in1=rinv[r_], op=ALU.mult),
                    reads=[("ps", ob_), ("rinv", r_)], writes=[("osb", r_)])
                p.dma("sp", ("ost", r_), lambda e, r_=r_, h=h, qt=qt: e.dma_start(
                    out=T["OT"][4 * h:4 * h + 4, :, qt * 128:(qt + 1) * 128].rearrange("g d q -> d g q"), in_=osb[r_]),
                    reads=[("osb", r_)])

            emit_S(0)
            pending_final = None
            for n in range(len(pairs)):
                if n + 1 < len(pairs):
                    emit_S(n + 1)
                emit_rest(n)
                if pending_final is not None:
                    emit_final(pending_final)
                    pending_final = None
                if pairs[n][1] == npair - 1:
                    pending_final = pairs[n][0]
            if pending_final is not None:
                emit_final(pending_final)
            p.barrier()

        def phase_wo():
            A.off = PERSIST
            oT = [A.bf16([128, 16, 512]) for _ in range(2)]
            wsl = [A.bf16([128, 16, 512]) for _ in range(3)]
            xr = [A.f32([128, 512]) for _ in range(2)]
            ob = [A.f32([128, 512]) for _ in range(2)]
            ntile = NT // 512
            items = []
            for tt in range(ntile):
                items += colblocks(T["Wb_o"], 0, 4)
            ws = WStream("ow", wsl, items, ["W_o"])
            gt1 = modT[:, 1, 32:48]
            OTv = T["OT"].rearrange("h d t -> d h t")
            for tt in range(ntile):
                tok0 = tt * 512
                s = tt % 2
                p.dma("sp", ("oT", s), lambda e, s=s, tok0=tok0: e.dma_start(out=oT[s], in_=OTv[:, :, tok0:tok0 + 512]),
                      reads=["OT"], writes=[("oT", s)])
                proj_residual(oT[s], ("oT", s), ws, tt * 4, T["x2T"], T["x3T"], gt1, tok0, 512, xr, ob, "x3", [0, 1, 2, 3])
            p.barrier()

        def phase_final():
            A.off = PERSIST
            xt = [A.f32([128, 16, 512]) for _ in range(2)]
            sq = A.bf16([128, 16, 512])
            rstd = A.f32([128, 512])
            yo = [A.f32([128, 16, 512]) for _ in range(2)]
            XS = fchunk(T["x4T"])
            XD = fchunk(T["outT"])
            ntile = NT // 512
            for tt in range(ntile):
                tok0 = tt * 512
                s = tt % 2
                p.dma("sp", ("fx", s), lambda e, s=s, tok0=tok0: e.dma_start(out=xt[s], in_=XS[:, :, tok0:tok0 + 512]),
                      reads=[("X", "x4T")], writes=[("fxt", s)])
                norm_mod(xt[s], 512, sq, 7, rstd, None, None, g_finalT, None, ("fxt", s), ("fyo", s), plain_out=yo[s])
                p.dma("sp", ("fo", s), lambda e, s=s, tok0=tok0: e.dma_start(out=XD[:, :, tok0:tok0 + 512], in_=yo[s]),
                      reads=[("fyo", s)])
            p.barrier()

        def phase_halo(l, xname):
            A.off = PERSIST
            eo = A.f32([128, 4, 16, 2])
            X = fchunk(T[xname])
            Ein = T["E_in%d" % l]
            Eout = T["E_out%d" % l]
            Ein3 = Ein.rearrange("p (k e) -> p k e", e=2)
            p.dma("sp", "ein", lambda e: e.dma_start(out=Ein3[:, :, 0:1], in_=X[:, :, 0:1], allow_slow_non_contiguous=True),
                  reads=[("X", xname)], writes=["E_in"])
            p.dma("sp", "ein", lambda e: e.dma_start(out=Ein3[:, :, 1:2], in_=X[:, :, NT - 1:NT], allow_slow_non_contiguous=True),
                  reads=[("X", xname)], writes=["E_in"])
            p.dma("pool", "cc_halo", lambda e: e.collective_compute(
                "AllGather", ALU.bypass, replica_groups=GROUPS, ins=[Ein], outs=[Eout]),
                reads=["E_in"], writes=["E_out"], inc=1)
            p.dma("sp", "eo", lambda e: e.dma_start(out=eo, in_=Eout.rearrange("(r p) (k e) -> p r k e", p=128, e=2)),
                  reads=["E_out"], writes=["eo"])
            for side in range(2):
                src_e = 1 - side
                for r in range(4):
                    sc = sel[:, side * 4 + r:side * 4 + r + 1]
                    if r == 0:
                        p.op("dve", lambda e, side=side, src_e=src_e, sc=sc: e.tensor_scalar(
                            out=halo_sb[:, :, side], in0=eo[:, 0, :, src_e], scalar1=sc, scalar2=None, op0=ALU.mult),
                            reads=["eo", "c_sel"], writes=["halo_sb"])
                    else:
                        p.op("dve", lambda e, side=side, src_e=src_e, sc=sc, r=r: e.scalar_tensor_tensor(
                            out=halo_sb[:, :, side], in0=eo[:, r, :, src_e], scalar=sc, in1=halo_sb[:, :, side],
                            op0=ALU.mult, op1=ALU.add), reads=["eo", "c_sel", "halo_sb"], writes=["halo_sb"])
            p.barrier()

        p.barrier()
        phase_ada_all()
        phase_gmlp()
        phase_halo(0, "x1T")
        convert_late()
        phase_ffn(0, "x1T", "x2T")
        phase_qkv()
        phase_attn()
        phase_wo()
        phase_halo(1, "x3T")
        phase_ffn(1, "x3T", "x4T")
        phase_final()
        final_groups = [g for g in p.group_counts if g not in p.bg_groups]
        p.emit(final_groups=final_groups)
    return nc, T


def _rope_tables(t0):
    S = 16384
    s = np.arange(t0, t0 + NT)
    row = (s // 64).astype(np.float32)
    col = (s % 64).astype(np.float32)
    inv = (10000.0 ** (-np.arange(0, 64, 2, dtype=np.float32) / 64.0)).astype(np.float32)
    ang_r = (row[:, None] * inv[None, :]).astype(np.float32)
    ang_c = (col[:, None] * inv[None, :]).astype(np.float32)
    cr, sr, cc, sc = np.cos(ang_r), np.sin(ang_r), np.cos(ang_c), np.sin(ang_c)
    cosT = np.concatenate([cr, cr, cc, cc], axis=1).T
    sinT = np.concatenate([-sr, sr, -sc, sc], axis=1).T
    return np.ascontiguousarray(cosT, dtype=np.float32), np.ascontiguousarray(sinT, dtype=np.float32)


def _partner_idx(n):
    d = np.arange(n)
    w = d % 64
    return np.where(w < 32, d + 32, d - 32)


_PROG_CACHE = {}


def _get_prog():
    if "p" not in _PROG_CACHE:
        _PROG_CACHE["p"] = build_program()
    return _PROG_CACHE["p"]


def _prep(x, c, w_ada, b_ada, g_norm, g_final, a_w_in, a_g_v, a_w_s, a_b_s, a_w_out,
          b_w_qkv, b_g_q, b_g_k, b_w_o, f_w_up, f_conv_w, f_conv_b, f_w_down):
    f = lambda a: np.ascontiguousarray(np.asarray(a), dtype=np.float32)
    x, c, w_ada, b_ada, g_norm, g_final = f(x), f(c), f(w_ada), f(b_ada), f(g_norm), f(g_final)
    a_w_in, a_g_v, a_w_s, a_b_s, a_w_out = f(a_w_in), f(a_g_v), f(a_w_s), f(a_b_s), f(a_w_out)
    b_w_qkv, b_g_q, b_g_k, b_w_o = f(b_w_qkv), f(b_g_q), f(b_g_k), f(b_w_o)
    f_w_up, f_conv_w, f_conv_b, f_w_down = f(f_w_up), f(f_conv_w), f(f_conv_b), f(f_w_down)

    shared = {
        "g_normT": f(g_norm.reshape(2, 2, 16, 128).transpose(3, 0, 1, 2)),
        "conv_wT": f(f_conv_w.reshape(2, 3, 88, 128).transpose(3, 0, 1, 2)),
        "conv_bT": f(f_conv_b.reshape(2, 88, 128).transpose(2, 0, 1)),
        "g_finalT": f(g_final.reshape(16, 128).T),
        "a_w_in": a_w_in[0], "a_w_out": a_w_out[0],
        "w_sT": f(a_w_s[0].transpose(2, 0, 1).reshape(128, 2048)),
        "g_v_bc": f(np.broadcast_to(a_g_v[0][None, :], (128, 2048))),
        "b_s_bc": f(np.broadcast_to(a_b_s[0].reshape(1, 2048), (128, 2048))),
        "f_w_up0": f_w_up[0], "f_w_up1": f_w_up[1], "f_w_down0": f_w_down[0], "f_w_down1": f_w_down[1],
        "b_w_o": b_w_o[0],
    }
    wqk = f(b_w_qkv[0][:, 0:2560])
    pidx = _partner_idx(2560)
    shared["w_qk"] = wqk
    rp = np.zeros((128, 128), np.float32)
    pp = _partner_idx(128)
    rp[pp, np.arange(128)] = 1.0
    shared["rperm"] = rp
    shared["w_v"] = f(b_w_qkv[0][:, 2560:3072])
    p128 = _partner_idx(128)
    shared["gqk"] = f(np.stack([b_g_q[0], b_g_q[0][p128], b_g_k[0], b_g_k[0][p128]], axis=1))

    cores = []
    for ci in range(8):
        b, r = ci // 4, ci % 4
        t0 = r * NT
        m = dict(shared)
        m["xT"] = f(x[b, t0:t0 + NT, :].T)
        m["cT"] = f(c[b].reshape(16, 128).T)
        cs, sn = _rope_tables(t0)
        m["cosT"], m["sinT"] = cs, sn
        m["hmask"] = f(np.broadcast_to(np.array([[0.0 if r == 0 else 1.0, 0.0 if r == 3 else 1.0]], np.float32), (128, 2)))
        sl = np.zeros((128, 8), np.float32)
        if r > 0:
            sl[:, r - 1] = 1.0
        if r < 3:
            sl[:, 4 + r + 1] = 1.0
        m["sel"] = sl
        m["w_adaS"] = f(w_ada[:, :, r * 3072:(r + 1) * 3072])
        m["b_adaS"] = f(b_ada[:, r * 3072:(r + 1) * 3072].reshape(2, 24, 128).transpose(2, 0, 1).reshape(128, 48))
        cores.append(m)

    return cores


IN_NAMES = ["cT", "g_normT", "conv_wT", "conv_bT", "g_finalT", "gqk", "hmask", "sel", "w_adaS", "b_adaS", "xT",
            "a_w_in", "a_w_out", "w_sT", "g_v_bc", "b_s_bc", "f_w_up0", "f_w_down0", "f_w_up1", "f_w_down1",
            "w_qk", "rperm", "w_v", "cosT", "sinT", "b_w_o"]


def kernel(**inputs):
    cores = _prep(**inputs)
    nc, T = _get_prog()
    maps = [{n: m[n] for n in IN_NAMES} for m in cores]
    res = run_bass_kernel_spmd(nc, maps, core_ids=list(range(8)))
    out = np.empty((2, 16384, D), np.float32)
    for ci in range(8):
        b, r = ci // 4, ci % 4
        out[b, r * NT:(r + 1) * NT, :] = res.results[ci]["outT"].T
    return out
```

```python
import contextlib
import math
import numpy as np
import ml_dtypes
import concourse.bass as bass
import concourse.mybir as mybir
from concourse.bass_utils import run_bass_kernel_spmd

F32 = mybir.dt.float32
BF16 = mybir.dt.bfloat16
AF = mybir.ActivationFunctionType
ALU = mybir.AluOpType

D = 2048
KC = 16
NT = 4096
DFF = 5632
NF = 44
EPS = 1e-6
SUBW = 410
NSUBT = 10
COMPUTE = ("pe", "act", "dve", "pool")


class Op:
    __slots__ = ("eng", "fn", "raw", "oth", "flag", "idx", "group", "gcount", "is_dma", "inc")

    def __init__(self, eng, fn, is_dma, group):
        self.eng = eng
        self.fn = fn
        self.raw = set()
        self.oth = set()
        self.flag = False
        self.idx = 0
        self.group = group
        self.gcount = 0
        self.is_dma = is_dma
        self.inc = 16


def _ekey(o):
    return ("dma", o.group) if o.is_dma else o.eng


class Prog:
    def __init__(self, nc):
        self.nc = nc
        self.ops = []
        self.last_writer = {}
        self.readers = {}
        self.group_counts = {}
        self.group_last = {}
        self.last_on_eng = {}
        self.pending_barrier = {}
        self.bg_groups = set()
        self.group_inc = {}

    def _add(self, op, reads, writes):
        for r in reads:
            w = self.last_writer.get(r)
            if w is not None:
                op.raw.add(w)
        for wkey in writes:
            w = self.last_writer.get(wkey)
            if w is not None:
                op.oth.add(w)
            for rd in self.readers.get(wkey, {}).values():
                op.oth.add(rd)
        pb = self.pending_barrier.pop(op.eng, None)
        if pb:
            op.raw.update(pb)
        op.raw.discard(op)
        op.oth.discard(op)
        op.oth -= op.raw
        for r in reads:
            self.readers.setdefault(r, {})[_ekey(op)] = op
        for wkey in writes:
            self.last_writer[wkey] = op
            self.readers[wkey] = {}
        self.ops.append(op)
        self.last_on_eng[op.eng] = op
        return op

    def op(self, eng, fn, reads=(), writes=()):
        return self._add(Op(eng, fn, False, None), reads, writes)

    def dma(self, queue, group, fn, reads=(), writes=(), inc=16):
        o = Op(queue, fn, True, group)
        o.inc = inc
        self.group_inc[group] = inc
        self.group_counts[group] = self.group_counts.get(group, 0) + 1
        o.gcount = self.group_counts[group]
        self.group_last[group] = o
        return self._add(o, reads, writes)

    def barrier(self):
        lasts = set()
        for e, o in self.last_on_eng.items():
            if not o.is_dma:
                lasts.add(o)
        for g, o in self.group_last.items():
            if g not in self.bg_groups:
                lasts.add(o)
        for e in ("pe", "act", "dve", "pool", "sp"):
            self.pending_barrier[e] = set(lasts)

    def emit(self, final_groups=()):
        nc = self.nc
        ops = self.ops

        def needed(o, d, is_raw):
            if d.is_dma or o.is_dma:
                return True
            if d.eng != o.eng:
                return True
            if o.eng == "pe":
                return False
            return is_raw

        for o in ops:
            for d in o.raw:
                if needed(o, d, True) and not d.is_dma:
                    d.flag = True
            for d in o.oth:
                if needed(o, d, False) and not d.is_dma:
                    d.flag = True
        counts = {e: 0 for e in COMPUTE}
        for o in ops:
            if not o.is_dma and o.flag:
                counts[o.eng] += 1
                o.idx = counts[o.eng]
        groups = sorted(self.group_counts, key=str)
        with contextlib.ExitStack() as st:
            sem_eng = {e: st.enter_context(nc.semaphore("m_" + e)) for e in COMPUTE}
            sem_grp = {g: st.enter_context(nc.semaphore("g%d" % i)) for i, g in enumerate(groups)}
            block = st.enter_context(nc.Block())
            per_eng = {"sp": []}
            for o in ops:
                per_eng.setdefault(o.eng, []).append(o)

            def make(engname, lst):
                def body(eng):
                    known = {}
                    for o in lst:
                        waits = {}
                        for is_raw, ds in ((True, o.raw), (False, o.oth)):
                            for d in ds:
                                if not needed(o, d, is_raw):
                                    continue
                                if d.is_dma:
                                    key = ("g", d.group)
                                    val = d.inc * d.gcount
                                else:
                                    key = ("e", d.eng)
                                    val = d.idx
                                if known.get(key, 0) >= val:
                                    continue
                                if waits.get(key, 0) < val:
                                    waits[key] = val
                        for key, val in waits.items():
                            known[key] = val
                            sem = sem_grp[key[1]] if key[0] == "g" else sem_eng[key[1]]
                            eng.wait_ge(sem, val)
                        ins = o.fn(eng)
                        if o.is_dma:
                            ins.then_inc(sem_grp[o.group], o.inc)
                        elif o.flag:
                            ins.then_inc(sem_eng[o.eng], 1)
                    if engname == "sp":
                        for g in final_groups:
                            eng.wait_ge(sem_grp[g], self.group_inc[g] * self.group_counts[g])
                return body

            attr = {"pe": "tensor", "act": "scalar", "dve": "vector", "pool": "gpsimd", "sp": "sync"}
            for e, lst in per_eng.items():
                getattr(block, attr[e])(make(e, lst))


class Arena:
    def __init__(self, ap, nwords):
        self.ap = ap
        self.n = nwords
        self.off = 0

    def _view(self, v, shape):
        if len(shape) == 2:
            return v
        if len(shape) == 3:
            return v.rearrange("p (a b) -> p a b", a=shape[1], b=shape[2])
        if len(shape) == 4:
            return v.rearrange("p (a b c) -> p a b c", a=shape[1], b=shape[2], c=shape[3])
        raise ValueError(shape)

    def f32(self, shape):
        n = int(np.prod(shape[1:]))
        assert self.off + n <= self.n, ("arena overflow", self.off, n)
        v = self.ap[:, self.off:self.off + n]
        self.off += n
        return self._view(v, shape)

    def bf16(self, shape):
        n = int(np.prod(shape[1:]))
        w = (n + 1) // 2
        assert self.off + w <= self.n, ("arena overflow", self.off, w)
        v = self.ap[:, self.off:self.off + w].bitcast(BF16)[:, 0:n]
        self.off += w
        return self._view(v, shape)


def fchunk(X):
    return X.rearrange("(kc p) t -> p kc t", p=128)


def build_program():
    nc = bass.Bass("TRN2", target_bir_lowering=False)
    p = Prog(nc)
    T = {}
    GROUPS = [[0, 1, 2, 3], [4, 5, 6, 7]]

    def dram(name, shape, dtype, kind):
        if kind == "cc":
            T[name] = nc.dram_tensor(name, list(shape), dtype).ap()
        else:
            k = {"in": "ExternalInput", "out": "ExternalOutput", "tmp": "Internal"}[kind]
            T[name] = nc.dram_tensor(name, list(shape), dtype, kind=k).ap()
        return T[name]

    layers = [0, 1]
    for nm, shp in (("cT", [128, 16]), ("g_normT", [128, 2, 2, 16]), ("conv_wT", [128, 2, 3, 88]),
                    ("conv_bT", [128, 2, 88]), ("g_finalT", [128, 16]), ("gqk", [128, 4]), ("hmask", [128, 2]), ("sel", [128, 8]),
                    ("w_adaS", [2, D, 3072]), ("b_adaS", [128, 48]), ("xT", [D, NT]), ("a_w_in", [D, 2 * D]), ("a_w_out", [D, D]),
                    ("w_sT", [128, D]), ("g_v_bc", [128, D]), ("b_s_bc", [128, D]),
                    ("f_w_up0", [D, 2 * DFF]), ("f_w_down0", [DFF, D]), ("f_w_up1", [D, 2 * DFF]), ("f_w_down1", [DFF, D]),
                    ("w_qk", [D, 2560]), ("rperm", [128, 128]), ("w_v", [D, 512]), ("cosT", [128, NT]), ("sinT", [128, NT]),
                    ("b_w_o", [D, D])):
        dram(nm, shp, F32, "in")
    for nm, shp in (("Wb_in", [D, 2 * D]), ("Wb_out", [D, D]), ("Wb_up0", [22, 128, 8192]), ("Wb_down0", [16, 128, NF * 128]),
                    ("Wb_up1", [22, 128, 8192]), ("Wb_down1", [16, 128, NF * 128]), ("Wb_qk", [D, 2560]),
                    ("Wb_v", [D, 512]), ("Wb_o", [D, D]), ("QT", [16, 128, NT]), ("OT", [16, 128, NT])):
        dram(nm, shp, BF16, "tmp")
    for nm in ("x1T", "x2T", "x3T", "x4T"):
        dram(nm, [D, NT], F32, "tmp")
    dram("outT", [D, NT], F32, "out")
    dram("KVin", [128, 128, 128], F32, "cc")
    dram("KVout", [128, 512, 128], F32, "cc")
    dram("A_in", [128, 48], F32, "cc")
    dram("A_out", [512, 48], F32, "cc")
    for l in (0, 1):
        dram("E_in%d" % l, [128, 32], F32, "cc")
        dram("E_out%d" % l, [512, 32], F32, "cc")
    KVin_bf = T["KVin"].bitcast(BF16)
    KVout_bf = T["KVout"].bitcast(BF16)

    with contextlib.ExitStack() as st:
        arena_t = st.enter_context(nc.sbuf_tensor("arena", [128, 49152], F32))
        ps = st.enter_context(nc.psum_tensor("ps", [128, 8, 512], F32))
        A = Arena(arena_t[:, :], 49152)

        ones_bf = A.bf16([128, 128])
        ones_f = A.f32([128, 128])
        eps_t = A.f32([128, 1])
        zero_t = A.f32([128, 1])
        cT = A.f32([128, 16])
        cond = A.f32([128, 16])
        modT = A.f32([128, 2, 96])
        gmul = A.f32([128, 2, 2, 16])
        b_adaT = A.f32([128, 2, 96])
        g_normT = A.f32([128, 2, 2, 16])
        conv_wT = A.f32([128, 2, 3, 88])
        conv_bT = A.f32([128, 2, 88])
        g_finalT = A.f32([128, 16])
        gqk = A.f32([128, 4])
        hmask = A.f32([128, 2])
        sel = A.f32([128, 8])
        halo_sb = A.f32([128, 16, 2])
        w_sT = A.bf16([128, 16, 128])
        PERSIST = A.off

        p.op("dve", lambda e: e.memset(ones_bf, 1.0), writes=["ones_bf"])
        p.op("dve", lambda e: e.memset(ones_f, 1.0), writes=["ones_f"])
        p.op("dve", lambda e: e.memset(eps_t, EPS), writes=["eps"])
        p.op("dve", lambda e: e.memset(zero_t, 0.0), writes=["zero"])
        for nm, dst in (("cT", cT), ("g_normT", g_normT), ("conv_wT", conv_wT),
                        ("conv_bT", conv_bT), ("g_finalT", g_finalT), ("gqk", gqk), ("hmask", hmask), ("sel", sel)):
            p.dma("sp", "const", lambda e, nm=nm, dst=dst: e.dma_start(out=dst, in_=T[nm]), writes=["c_" + nm])
        CONST_KEYS = ["c_cT", "c_b_adaT", "c_g_normT", "c_conv_wT", "c_conv_bT", "c_g_finalT", "c_gqk", "c_hmask",
                      "ones_bf", "ones_f", "eps", "zero"]

        p.dma("pool", "wsT", lambda e: e.dma_start(out=w_sT, in_=T["w_sT"].rearrange("p (g i) -> p g i", g=16)), writes=["w_sT"])

        def convert(src, dst, key):
            R, C = T[src].shape
            tot = R * C
            per = tot // 128
            sv = T[src].rearrange("(p r) c -> p (r c)", p=128)
            dv = T[dst].rearrange("(p r) c -> p (r c)", p=128)
            CH = 8192
            grp = "cv_" + key
            p.bg_groups.add(grp)
            for a in range(0, per, CH):
                b = min(per, a + CH)
                p.dma("pool", grp, lambda e, a=a, b=b: e.dma_start(out=dv[:, a:b], in_=sv[:, a:b]), writes=["W_" + key])

        def convert_up(l):
            grp = "cv_up%d" % l
            p.bg_groups.add(grp)
            S5 = T["f_w_up%d" % l].rearrange("(kc p) (wh fb c) -> fb p wh kc c", p=128, wh=2, fb=22, c=256)
            for fb in range(22):
                p.dma("pool", grp, lambda e, fb=fb: e.dma_start(
                    out=T["Wb_up%d" % l][fb].rearrange("p (wh kc c) -> p wh kc c", wh=2, kc=16, c=256), in_=S5[fb]),
                    writes=["W_up%d" % l])

        def convert_down(l):
            grp = "cv_down%d" % l
            p.bg_groups.add(grp)
            S4 = T["f_w_down%d" % l].rearrange("(f p) (dc c) -> dc p f c", p=128, c=128)
            for dc in range(16):
                p.dma("pool", grp, lambda e, dc=dc: e.dma_start(
                    out=T["Wb_down%d" % l][dc].rearrange("p (f c) -> p f c", f=NF, c=128), in_=S4[dc]),
                    writes=["W_down%d" % l])

        def convert_first():
            convert("a_w_in", "Wb_in", "in")
            convert("a_w_out", "Wb_out", "out")

        def convert_early():
            convert_up(0)
            convert_down(0)

        def convert_late():
            convert("w_v", "Wb_v", "v")
            convert("w_qk", "Wb_qk", "qk")
            convert("b_w_o", "Wb_o", "o")
            convert_up(1)
            convert_down(1)

        convert_first()

        def phase_ada_all():
            A.off = PERSIST
            wbl = [A.f32([128, 16, 512]) for _ in range(2)]
            modS = A.f32([128, 48])
            b_adaS = A.f32([128, 48])
            p.dma("sp", "const", lambda e: e.dma_start(out=b_adaS, in_=T["b_adaS"]), writes=["b_adaS"])
            p.op("act", lambda e: e.activation(out=cond, in_=cT, func=AF.Silu), reads=["c_cT"], writes=["cond"])
            blocks = [(l, cb) for l in range(2) for cb in range(6)]
            Wl = [fchunk(T["w_adaS"][l]) for l in range(2)]

            def load(i):
                l, cb = blocks[i]
                s = i % 2
                p.dma("sp", ("ada", s), lambda e, l=l, cb=cb, s=s: e.dma_start(out=wbl[s], in_=Wl[l][:, :, cb * 512:(cb + 1) * 512]),
                      writes=[("adaw", s)])
            load(0)
            for i, (l, cb) in enumerate(blocks):
                if i + 1 < len(blocks):
                    load(i + 1)
                s = i % 2
                for j in range(4):
                    col = l * 24 + cb * 4 + j
                    for kc in range(KC):
                        p.op("pe", lambda e, s=s, j=j, kc=kc, col=col: e.matmul(
                            ps[:, 0, col:col + 1], lhsT=wbl[s][:, kc, j * 128:(j + 1) * 128], rhs=cond[:, kc:kc + 1],
                            start=(kc == 0), stop=(kc == KC - 1)),
                            reads=[("adaw", s), "cond"], writes=[("ps", 0)])
            p.op("dve", lambda e: e.tensor_tensor(out=modS, in0=ps[:, 0, 0:48], in1=b_adaS, op=ALU.add),
                 reads=[("ps", 0), "b_adaS"], writes=["modS"])
            p.dma("sp", "ain", lambda e: e.dma_start(out=T["A_in"], in_=modS), reads=["modS"], writes=["A_in"])
            p.dma("pool", "cc_ada", lambda e: e.collective_compute(
                "AllGather", ALU.bypass, replica_groups=GROUPS, ins=[T["A_in"]], outs=[T["A_out"]]),
                reads=["A_in"], writes=["A_out"], inc=1)
            p.dma("sp", "aout", lambda e: e.dma_start(
                out=modT.rearrange("p l (r j) -> p l r j", r=4, j=24),
                in_=T["A_out"].rearrange("(r p) (l j) -> p l r j", p=128, l=2)),
                reads=["A_out"], writes=["modT"])
            for l in range(2):
                for s_ in range(2):
                    sc0 = 16 + 48 * s_
                    p.op("dve", lambda e, l=l, s_=s_, sc0=sc0: e.scalar_tensor_tensor(
                        out=gmul[:, l, s_, :], in0=modT[:, l, sc0:sc0 + 16], scalar=1.0, in1=g_normT[:, l, s_, :],
                        op0=ALU.add, op1=ALU.mult), reads=["modT", "c_g_normT"], writes=["gmul"])
            p.barrier()

        def norm_mod(xin, N, sq, psb, rstd, tmps, hT, gm, sh, xkey, hkey, plain_out=None, sqkey="sq"):
            p.op("act", lambda e: e.activation(out=sq[:, :, 0:N], in_=xin[:, :, 0:N], func=AF.Square),
                 reads=[xkey], writes=[sqkey])
            for kc in range(KC):
                p.op("pe", lambda e, kc=kc: e.matmul(ps[:, psb, 0:N], lhsT=ones_bf, rhs=sq[:, kc, 0:N],
                                                     start=(kc == 0), stop=(kc == KC - 1)),
                     reads=[sqkey, "ones_bf"], writes=[("ps", psb)])
            p.op("act", lambda e: e.activation(out=rstd[:, 0:N], in_=ps[:, psb, 0:N], func=AF.Sqrt, bias=eps_t[:, 0:1],
                                               scale=1.0 / D), reads=[("ps", psb), "eps"], writes=["rstd"])
            p.op("dve", lambda e: e.reciprocal(out=rstd[:, 0:N], in_=rstd[:, 0:N]), reads=["rstd"], writes=["rstd"])
            for kc in range(KC):
                if plain_out is not None:
                    p.op("dve", lambda e, kc=kc: e.scalar_tensor_tensor(
                        out=plain_out[:, kc, 0:N], in0=xin[:, kc, 0:N], scalar=gm[:, kc:kc + 1], in1=rstd[:, 0:N],
                        op0=ALU.mult, op1=ALU.mult), reads=[xkey, "rstd", "c_g_finalT"], writes=[hkey])
                    continue
                tb = tmps[kc % len(tmps)]
                tk = ("nm_tmp", kc % len(tmps))
                p.op("dve", lambda e, kc=kc, tb=tb: e.scalar_tensor_tensor(
                    out=tb[:, 0:N], in0=xin[:, kc, 0:N], scalar=gm[:, kc:kc + 1], in1=rstd[:, 0:N],
                    op0=ALU.mult, op1=ALU.mult), reads=[xkey, "rstd", "gmul"], writes=[tk])
                p.op("act", lambda e, kc=kc, tb=tb: e.activation(
                    out=hT[:, kc, 0:N], in_=tb[:, 0:N], func=AF.Identity, bias=sh[:, kc:kc + 1], scale=1.0),
                    reads=[tk, "modT"], writes=[hkey])

        class WStream:
            def __init__(self, name, slots, items, wkeys):
                self.name = name
                self.slots = slots
                self.items = items
                self.wkeys = wkeys
                self.next = 0

            def issue_upto(self, k):
                while self.next <= min(k, len(self.items) - 1):
                    i = self.next
                    s = i % len(self.slots)
                    for (o_ap, i_ap) in self.items[i](self.slots[s]):
                        p.dma("sp", (self.name, s), lambda e, o_ap=o_ap, i_ap=i_ap: e.dma_start(out=o_ap, in_=i_ap),
                              reads=self.wkeys, writes=[(self.name, s)])
                    self.next += 1

            def use(self, i):
                self.issue_upto(i + len(self.slots) - 1)
                s = i % len(self.slots)
                return self.slots[s], (self.name, s)

        def proj_residual(rhsT, rhskey, ws, ws_base, xsrc, xdst, gt, tok0, N, xr, ob, dkey, banks):
            XS = fchunk(xsrc)
            XD = fchunk(xdst)
            pend = []
            for ob_ in range(4):
                wblk, wk = ws.use(ws_base + ob_)
                for j in range(4):
                    dc = ob_ * 4 + j
                    bk = banks[dc % len(banks)]
                    rs = dc % 2
                    p.dma("sp", ("xr", rs), lambda e, dc=dc, rs=rs: e.dma_start(out=xr[rs][:, 0:N], in_=XS[:, dc, tok0:tok0 + N]),
                          writes=[("xr", rs)])
                    while pend:
                        pend.pop(0)()
                    for g in range(KC):
                        p.op("pe", lambda e, g=g, j=j, bk=bk, wblk=wblk: e.matmul(
                            ps[:, bk, 0:N], lhsT=wblk[:, g, j * 128:(j + 1) * 128], rhs=rhsT[:, g, 0:N],
                            start=(g == 0), stop=(g == KC - 1)), reads=[wk, rhskey], writes=[("ps", bk)])
                    p.op("dve", lambda e, dc=dc, bk=bk, rs=rs: e.scalar_tensor_tensor(
                        out=ob[rs][:, 0:N], in0=ps[:, bk, 0:N], scalar=gt[:, dc:dc + 1], in1=xr[rs][:, 0:N],
                        op0=ALU.mult, op1=ALU.add), reads=[("ps", bk), ("xr", rs), "modT"], writes=[("ob", rs)])
                    pend.append(lambda dc=dc, rs=rs: p.dma("sp", ("xo", rs), lambda e: e.dma_start(
                        out=XD[:, dc, tok0:tok0 + N], in_=ob[rs][:, 0:N]), reads=[("ob", rs)]))
            while pend:
                pend.pop(0)()

        def colblocks(Wb, c0, nblk, width=512):
            V = fchunk(Wb)
            return [(lambda slot, i=i: [(slot[:, :, 0:width], V[:, :, c0 + i * width:c0 + (i + 1) * width])]) for i in range(nblk)]

        def phase_gmlp():
            A.off = PERSIST
            xt = A.f32([128, 16, 512])
            bufA = A.bf16([128, 16, 512])
            bufB = A.bf16([128, 16, 512])
            vn = A.bf16([128, 4, 2048])
            rstd = A.f32([128, 512])
            tmps = [A.f32([128, 512]) for _ in range(2)]
            vts = [A.f32([128, 512]) for _ in range(2)]
            junk = A.bf16([128, 512])
            ssq = A.f32([128, 16])
            ssv = A.f32([128, 4])
            svts = [A.f32([128, 512]) for _ in range(2)]
            g_v_bc = A.f32([128, 2048])
            b_s_bc = A.f32([128, 2048])
            wsl = [A.bf16([128, 16, 512]) for _ in range(3)]
            xr = [A.f32([128, 512]) for _ in range(2)]
            ob = [A.f32([128, 512]) for _ in range(2)]
            p.dma("sp", "const", lambda e: e.dma_start(out=g_v_bc, in_=T["g_v_bc"]), writes=["g_v_bc"])
            p.dma("sp", "const", lambda e: e.dma_start(out=b_s_bc, in_=T["b_s_bc"]), writes=["b_s_bc"])
            ntile = NT // 512
            items = []
            for tt in range(ntile):
                items += colblocks(T["Wb_in"], 2048, 4) + colblocks(T["Wb_in"], 0, 4) + colblocks(T["Wb_out"], 0, 4)
            ws = WStream("gw", wsl, items, ["W_in", "W_out"])
            XS = fchunk(T["xT"])
            sh1 = modT[:, 0, 0:16]
            gt1 = modT[:, 0, 32:48]
            for tt in range(ntile):
                tok0 = tt * 512
                p.dma("sp", "xt", lambda e, tok0=tok0: e.dma_start(out=xt, in_=XS[:, :, tok0:tok0 + 512]),
                      reads=[("X", "xT")], writes=["xt"])
                norm_mod(xt, 512, bufA, 7, rstd, tmps, bufB, gmul[:, 0, 0, :], sh1, "xt", "bufB", sqkey="bufA")
                base = tt * 12
                for vb in range(4):
                    wblk, wk = ws.use(base + vb)
                    for tcn in range(4):
                        u = vb * 4 + tcn
                        bk = u % 4
                        for kc in range(KC):
                            p.op("pe", lambda e, kc=kc, tcn=tcn, bk=bk, wblk=wblk: e.matmul(
                                ps[:, bk, :], lhsT=bufB[:, kc, tcn * 128:(tcn + 1) * 128], rhs=wblk[:, kc, :],
                                start=(kc == 0), stop=(kc == KC - 1)), reads=[wk, "bufB"], writes=[("ps", bk)])
                        vt = vts[u % 2]
                        vk = ("vt", u % 2)
                        p.op("act", lambda e, bk=bk, vt=vt: e.activation(out=vt, in_=ps[:, bk, :], func=AF.Gelu),
                             reads=[("ps", bk)], writes=[vk])
                        p.op("dve", lambda e, vt=vt, tcn=tcn, vb=vb: e.scalar_tensor_tensor(
                            out=junk, in0=vt, scalar=1.0, in1=vt, op0=ALU.mult, op1=ALU.mult,
                            accum_out=ssq[:, tcn * 4 + vb:tcn * 4 + vb + 1]), reads=[vk], writes=["junk", "ssq"])
                        p.op("act", lambda e, vt=vt, tcn=tcn, vb=vb: e.activation(out=vn[:, tcn, vb * 512:(vb + 1) * 512], in_=vt, func=AF.Copy),
                             reads=[vk], writes=[("vraw", tcn)])
                for tcn in range(4):
                    p.op("dve", lambda e, tcn=tcn: e.tensor_reduce(out=ssv[:, tcn:tcn + 1], in_=ssq[:, tcn * 4:(tcn + 1) * 4],
                                                                   axis=mybir.AxisListType.X, op=ALU.add),
                         reads=["ssq"], writes=["ssv"])
                p.op("act", lambda e: e.activation(out=ssv, in_=ssv, func=AF.Sqrt, bias=eps_t[:, 0:1], scale=1.0 / D),
                     reads=["ssv", "eps"], writes=["ssv"])
                p.op("dve", lambda e: e.reciprocal(out=ssv, in_=ssv), reads=["ssv"], writes=["ssv"])
                for tcn in range(4):
                    p.op("dve", lambda e, tcn=tcn: e.scalar_tensor_tensor(
                        out=vn[:, tcn, :], in0=vn[:, tcn, :], scalar=ssv[:, tcn:tcn + 1], in1=g_v_bc,
                        op0=ALU.mult, op1=ALU.mult), reads=[("vraw", tcn), "ssv", "g_v_bc"], writes=[("vn", tcn)])
                for ub in range(4):
                    wblk, wk = ws.use(base + 4 + ub)
                    for j in range(4):
                        uc = ub * 4 + j
                        bk = uc % 4
                        for kc in range(KC):
                            p.op("pe", lambda e, kc=kc, j=j, bk=bk, wblk=wblk: e.matmul(
                                ps[:, bk, :], lhsT=wblk[:, kc, j * 128:(j + 1) * 128], rhs=bufB[:, kc, :],
                                start=(kc == 0), stop=(kc == KC - 1)), reads=[wk, "bufB"], writes=[("ps", bk)])
                        p.op("act", lambda e, bk=bk, uc=uc: e.activation(out=bufA[:, uc, :], in_=ps[:, bk, :], func=AF.Gelu),
                             reads=[("ps", bk)], writes=["bufA"])
                for tcn in range(4):
                    for gq in range(4):
                        u = tcn * 4 + gq
                        bk = 4 + u % 3
                        for gi in range(4):
                            g = gq * 4 + gi
                            p.op("pe", lambda e, g=g, gi=gi, bk=bk, tcn=tcn: e.matmul(
                                ps[:, bk, gi * 128:(gi + 1) * 128], lhsT=vn[:, tcn, g * 128:(g + 1) * 128], rhs=w_sT[:, g, :],
                                start=True, stop=True), reads=[("vn", tcn), "w_sT"], writes=[("ps", bk)])
                        sv = svts[u % 2]
                        sk = ("svt", u % 2)
                        p.op("dve", lambda e, bk=bk, sv=sv, gq=gq: e.tensor_tensor(
                            out=sv, in0=ps[:, bk, :], in1=b_s_bc[:, gq * 512:(gq + 1) * 512], op=ALU.add),
                            reads=[("ps", bk), "b_s_bc"], writes=[sk])
                        p.op("dve", lambda e, sv=sv, gq=gq, tcn=tcn: e.tensor_tensor(
                            out=bufB[:, gq * 4:(gq + 1) * 4, tcn * 128:(tcn + 1) * 128],
                            in0=sv.rearrange("p (a b) -> p a b", a=4, b=128),
                            in1=bufA[:, gq * 4:(gq + 1) * 4, tcn * 128:(tcn + 1) * 128], op=ALU.mult),
                            reads=[sk, "bufA"], writes=["bufB"])
                proj_residual(bufB, "bufB", ws, base + 8, T["xT"], T["x1T"], gt1, tok0, 512, xr, ob, "x1", [0, 1, 2, 3])
            p.barrier()

        def phase_ffn(l, xsrc_name, xdst_name):
            A.off = PERSIST
            act = A.bf16([128, 2, NF, SUBW])
            act_off_end = A.off
            h2T = A.bf16([128, 2, 16, SUBW + 2])
            rstd = A.f32([128, 512])
            tmps = [A.f32([128, 512]) for _ in range(2)]
            tg = [A.f32([128, 2, SUBW]) for _ in range(2)]
            tv = [A.f32([128, 2, SUBW]) for _ in range(2)]
            wsl = [A.bf16([128, 2, 16, 256]) for _ in range(3)]
            xr = [A.f32([128, 2, SUBW]) for _ in range(2)]
            ob = [A.f32([128, 2, SUBW]) for _ in range(2)]
            save = A.off
            A.off = PERSIST
            xin = A.f32([128, 16, SUBW + 2])
            sq = A.bf16([128, 16, SUBW + 2])
            assert A.off <= act_off_end
            A.off = save
            XS = fchunk(T[xsrc_name])
            XD = fchunk(T[xdst_name])
            Wup = T["Wb_up%d" % l]
            Wdn = T["Wb_down%d" % l]
            sh2 = modT[:, l, 48:64]
            gt2 = modT[:, l, 80:96]
            gm2 = gmul[:, l, 1, :]
            nsup = NSUBT // 2
            items = []
            for s_ in range(nsup):
                for fb in range(NF // 2):
                    items.append(lambda slot, fb=fb: [(slot.rearrange("p a k c -> p (a k c)"), Wup[fb])])
                for dc in range(KC):
                    items.append(lambda slot, dc=dc: [(slot.rearrange("p a k c -> p (a k c)")[:, 0:NF * 128], Wdn[dc])])
            ws = WStream("fw", wsl, items, ["W_up%d" % l, "W_down%d" % l])
            per_sup = NF // 2 + KC
            unit = 0
            for sp_ in range(nsup):
                toks = []
                for sub in range(2):
                    s_ = sp_ * 2 + sub
                    tok0 = min(s_ * SUBW, NT - SUBW)
                    toks.append(tok0)
                    if s_ == 0:
                        p.dma("sp", "xin", lambda e: e.dma_start(out=xin[:, :, 1:SUBW + 2], in_=XS[:, :, 0:SUBW + 1]),
                              reads=[("X", xsrc_name)], writes=["actreg"])
                        p.op("dve", lambda e: e.tensor_copy(out=xin[:, :, 0:1], in_=halo_sb[:, :, 0:1]),
                             reads=["halo_sb"], writes=["actreg"])
                    elif s_ == NSUBT - 1:
                        p.dma("sp", "xin", lambda e, tok0=tok0: e.dma_start(out=xin[:, :, 0:SUBW + 1], in_=XS[:, :, tok0 - 1:NT]),
                              reads=[("X", xsrc_name)], writes=["actreg"])
                        p.op("dve", lambda e: e.tensor_copy(out=xin[:, :, SUBW + 1:SUBW + 2], in_=halo_sb[:, :, 1:2]),
                             reads=["halo_sb"], writes=["actreg"])
                    else:
                        p.dma("sp", "xin", lambda e, tok0=tok0: e.dma_start(out=xin, in_=XS[:, :, tok0 - 1:tok0 + SUBW + 1]),
                              reads=[("X", xsrc_name)], writes=["actreg"])
                    norm_mod(xin, SUBW + 2, sq, 7, rstd, tmps, h2T[:, sub, :, :], gm2, sh2, "actreg", ("h2T", sub), sqkey="actreg")
                    if s_ == 0:
                        p.op("dve", lambda e, sub=sub: e.tensor_scalar(out=h2T[:, sub, :, 0:1], in0=h2T[:, sub, :, 0:1],
                                                                        scalar1=hmask[:, 0:1], scalar2=None, op0=ALU.mult),
                             reads=[("h2T", sub), "c_hmask"], writes=[("h2T", sub)])
                    if s_ == NSUBT - 1:
                        p.op("dve", lambda e, sub=sub: e.tensor_scalar(out=h2T[:, sub, :, SUBW + 1:SUBW + 2],
                                                                        in0=h2T[:, sub, :, SUBW + 1:SUBW + 2],
                                                                        scalar1=hmask[:, 1:2], scalar2=None, op0=ALU.mult),
                             reads=[("h2T", sub), "c_hmask"], writes=[("h2T", sub)])
                base = sp_ * per_sup
                for f in range(NF):
                    wblk, wk = ws.use(base + f // 2)
                    fl = f % 2
                    for which in range(2):
                        slot = unit % 3
                        unit += 1
                        b0 = 2 * slot
                        for kc in range(KC):
                            for sub in range(2):
                                p.op("pe", lambda e, kc=kc, sub=sub, b0=b0, which=which, fl=fl, wblk=wblk: e.matmul(
                                    ps[:, b0 + sub, 0:SUBW + 2], lhsT=wblk[:, which, kc, fl * 128:(fl + 1) * 128],
                                    rhs=h2T[:, sub, kc, :], start=(kc == 0), stop=(kc == KC - 1)),
                                    reads=[wk, ("h2T", 0), ("h2T", 1)], writes=[("ps", b0), ("ps", b0 + 1)])
                        tb = (tg if which == 0 else tv)[f % 2]
                        tk = ("tg" if which == 0 else "tv", f % 2)
                        fc = which * NF + f
                        pk = [("ps", b0), ("ps", b0 + 1)]
                        p.op("act", lambda e, b0=b0, tb=tb, fc=fc: e.activation(
                            out=tb, in_=ps[:, b0:b0 + 2, 1:SUBW + 1], func=AF.Identity,
                            bias=conv_bT[:, l, fc:fc + 1], scale=conv_wT[:, l, 1, fc:fc + 1]),
                            reads=pk + ["c_conv_wT", "c_conv_bT"], writes=[tk])
                        p.op("dve", lambda e, b0=b0, tb=tb, fc=fc: e.scalar_tensor_tensor(
                            out=tb, in0=ps[:, b0:b0 + 2, 0:SUBW], scalar=conv_wT[:, l, 0, fc:fc + 1], in1=tb,
                            op0=ALU.mult, op1=ALU.add), reads=pk + [tk], writes=[tk])
                        p.op("dve", lambda e, b0=b0, tb=tb, fc=fc: e.scalar_tensor_tensor(
                            out=tb, in0=ps[:, b0:b0 + 2, 2:SUBW + 2], scalar=conv_wT[:, l, 2, fc:fc + 1], in1=tb,
                            op0=ALU.mult, op1=ALU.add), reads=pk + [tk], writes=[tk])
                        if which == 0:
                            p.op("act", lambda e, tb=tb: e.activation(out=tb, in_=tb, func=AF.Gelu), reads=[tk], writes=[tk])
                    p.op("dve", lambda e, f=f: e.tensor_tensor(out=act[:, :, f, :], in0=tg[f % 2], in1=tv[f % 2], op=ALU.mult),
                         reads=[("tg", f % 2), ("tv", f % 2)], writes=["actreg"])
                pend = []
                for dc in range(KC):
                    wblk, wk = ws.use(base + NF // 2 + dc)
                    wd = wblk.rearrange("p a k c -> p (a k c)")[:, 0:NF * 128].rearrange("p (f c) -> p f c", f=NF, c=128)
                    b0 = 6 if dc % 2 == 0 else 4
                    rs = dc % 2
                    for sub in range(2):
                        p.dma("sp", ("xr", rs), lambda e, dc=dc, rs=rs, sub=sub, tk_=toks[sub]: e.dma_start(
                            out=xr[rs][:, sub, :], in_=XS[:, dc, tk_:tk_ + SUBW]),
                            writes=[("xr", rs)])
                    while pend:
                        pend.pop(0)()
                    for f in range(NF):
                        for sub in range(2):
                            p.op("pe", lambda e, f=f, sub=sub, b0=b0, wd=wd: e.matmul(
                                ps[:, b0 + sub, 0:SUBW], lhsT=wd[:, f, :], rhs=act[:, sub, f, :],
                                start=(f == 0), stop=(f == NF - 1)), reads=[wk, "actreg"], writes=[("ps", b0), ("ps", b0 + 1)])
                    p.op("dve", lambda e, dc=dc, b0=b0, rs=rs: e.scalar_tensor_tensor(
                        out=ob[rs], in0=ps[:, b0:b0 + 2, 0:SUBW], scalar=gt2[:, dc:dc + 1], in1=xr[rs],
                        op0=ALU.mult, op1=ALU.add), reads=[("ps", b0), ("ps", b0 + 1), ("xr", rs), "modT"], writes=[("ob", rs)])
                    for sub in range(2):
                        pend.append(lambda dc=dc, rs=rs, sub=sub, tk_=toks[sub]: p.dma("sp", ("xo", rs), lambda e: e.dma_start(
                            out=XD[:, dc, tk_:tk_ + SUBW], in_=ob[rs][:, sub, :]), reads=[("ob", rs)]))
                while pend:
                    pend.pop(0)()
            p.barrier()

        def phase_qkv():
            A.off = PERSIST
            xt = A.f32([128, 16, 512])
            sq = A.bf16([128, 16, 512])
            hT = A.bf16([128, 16, 512])
            rstd = A.f32([128, 512])
            tmps = [A.f32([128, 512]) for _ in range(2)]
            cs = A.f32([128, 512])
            sn = A.f32([128, 512])
            t1 = [A.f32([128, 512]) for _ in range(3)]
            t2 = [A.f32([128, 512]) for _ in range(3)]
            sqh = [A.bf16([128, 512]) for _ in range(3)]
            rs_ = [A.f32([128, 512]) for _ in range(3)]
            qr = [A.bf16([128, 512]) for _ in range(3)]
            vb_ = [A.bf16([128, 512]) for _ in range(2)]
            wsl = [A.bf16([128, 2, 16, 256]) for _ in range(3)]
            ntile = NT // 512
            Vv = fchunk(T["Wb_v"])
            Vqk = fchunk(T["Wb_qk"])
            rp_f = A.f32([128, 128])
            rp_b = A.bf16([128, 128])
            qb = [A.bf16([128, 512]) for _ in range(3)]
            p.dma("sp", "const", lambda e: e.dma_start(out=rp_f, in_=T["rperm"]), writes=["rp_f"])
            p.op("dve", lambda e: e.tensor_copy(out=rp_b, in_=rp_f), reads=["rp_f"], writes=["rp_b"])
            items = []
            for tt in range(ntile):
                items.append(lambda slot: [(slot.rearrange("p a k c -> p (a k c)").rearrange("p (k c) -> p k c", k=16, c=512), Vv[:, :, :])])
                for hb in range(10):
                    items.append(lambda slot, hb=hb: [(slot[:, 0, :, :], Vqk[:, :, hb * 256:(hb + 1) * 256])])
            ws = WStream("qw", wsl, items, ["W_v", "W_qk"])
            XS = fchunk(T["x2T"])
            sh1 = modT[:, 1, 0:16]
            KV5 = KVin_bf.rearrange("(kv h c) p f -> kv h c p f", kv=2, h=4, c=16)

            def kv_collectives(tt):
                for h in range(4):
                    for j in range(2):
                        for kv in range(2):
                            c = kv * 64 + h * 16 + 2 * tt + j
                            p.dma("pool", "cc_kv", lambda e, c=c: e.collective_compute(
                                "AllGather", ALU.bypass, replica_groups=GROUPS, ins=[T["KVin"][c]], outs=[T["KVout"][c]]),
                                reads=[("KVin", c)], writes=["KVout"], inc=1)
            hcnt = 0
            for tt in range(ntile):
                tok0 = tt * 512
                p.dma("sp", "xt", lambda e, tok0=tok0: e.dma_start(out=xt, in_=XS[:, :, tok0:tok0 + 512]),
                      reads=[("X", "x2T")], writes=["xt"])
                p.dma("sp", "cs", lambda e, tok0=tok0: e.dma_start(out=cs, in_=T["cosT"][:, tok0:tok0 + 512]), writes=["cs"])
                p.dma("sp", "cs", lambda e, tok0=tok0: e.dma_start(out=sn, in_=T["sinT"][:, tok0:tok0 + 512]), writes=["sn"])
                norm_mod(xt, 512, sq, 7, rstd, tmps, hT, gmul[:, 1, 0, :], sh1, "xt", "hT")
                base = tt * 11
                wblk, wk = ws.use(base)
                wv = wblk.rearrange("p a k c -> p (a k c)").rearrange("p (k c) -> p k c", k=16, c=512)
                for tcn in range(4):
                    bk = 4 + tcn % 2
                    for kc in range(KC):
                        p.op("pe", lambda e, kc=kc, tcn=tcn, bk=bk, wv=wv: e.matmul(
                            ps[:, bk, :], lhsT=hT[:, kc, tcn * 128:(tcn + 1) * 128], rhs=wv[:, kc, :],
                            start=(kc == 0), stop=(kc == KC - 1)), reads=[wk, "hT"], writes=[("ps", bk)])
                    vs = tcn % 2
                    p.op("act", lambda e, bk=bk, vs=vs: e.activation(out=vb_[vs], in_=ps[:, bk, :], func=AF.Copy),
                         reads=[("ps", bk)], writes=[("vb", vs)])
                    vc = tt * 2 + tcn // 2
                    off = (tcn % 2) * 128
                    p.dma("sp", ("vst", vs), lambda e, vs=vs, vc=vc, off=off: e.dma_start(
                        out=KV5[1, :, vc, :, off:off + 128].rearrange("h p d -> p h d"),
                        in_=vb_[vs].rearrange("p (h d) -> p h d", h=4, d=128)),
                        reads=[("vb", vs)], writes=[("KVin", 64 + h_ * 16 + vc) for h_ in range(4)])
                for hb in range(10):
                    wblk, wk = ws.use(base + 1 + hb)
                    for hh in range(2):
                        head = hb * 2 + hh
                        hs = hcnt % 3
                        hcnt += 1
                        bq = 0 + 2 * hs
                        bp = 1 + 2 * hs
                        for kc in range(KC):
                            p.op("pe", lambda e, kc=kc, hh=hh, bq=bq, wblk=wblk: e.matmul(
                                ps[:, bq, :], lhsT=wblk[:, 0, kc, hh * 128:(hh + 1) * 128], rhs=hT[:, kc, :],
                                start=(kc == 0), stop=(kc == KC - 1)), reads=[wk, "hT"], writes=[("ps", bq)])
                        p.op("act", lambda e, bq=bq, hs=hs: e.activation(out=qb[hs], in_=ps[:, bq, :], func=AF.Copy),
                             reads=[("ps", bq)], writes=[("qb", hs)])
                        p.op("pe", lambda e, bp=bp, hs=hs: e.matmul(ps[:, bp, :], lhsT=rp_b, rhs=qb[hs], start=True, stop=True),
                             reads=[("qb", hs), "rp_b"], writes=[("ps", bp)])
                        p.op("act", lambda e, bq=bq, hs=hs: e.activation(out=sqh[hs], in_=ps[:, bq, :], func=AF.Square),
                             reads=[("ps", bq)], writes=[("sqh", hs)])
                        bs = 6
                        p.op("pe", lambda e, hs=hs, bs=bs: e.matmul(ps[:, bs, :], lhsT=ones_bf, rhs=sqh[hs], start=True, stop=True),
                             reads=[("sqh", hs), "ones_bf"], writes=[("ps", bs)])
                        p.op("act", lambda e, hs=hs, bs=bs: e.activation(out=rs_[hs], in_=ps[:, bs, :], func=AF.Sqrt,
                                                                          bias=eps_t[:, 0:1], scale=1.0 / 128),
                             reads=[("ps", bs), "eps"], writes=[("rs", hs)])
                        p.op("dve", lambda e, hs=hs: e.reciprocal(out=rs_[hs], in_=rs_[hs]), reads=[("rs", hs)], writes=[("rs", hs)])
                        gc = 0 if head < 16 else 2
                        p.op("dve", lambda e, hs=hs, bq=bq, gc=gc: e.scalar_tensor_tensor(
                            out=t1[hs], in0=ps[:, bq, :], scalar=gqk[:, gc:gc + 1], in1=cs, op0=ALU.mult, op1=ALU.mult),
                            reads=[("ps", bq), "c_gqk", "cs"], writes=[("t1", hs)])
                        p.op("dve", lambda e, hs=hs, bp=bp, gc=gc: e.scalar_tensor_tensor(
                            out=t2[hs], in0=ps[:, bp, :], scalar=gqk[:, gc + 1:gc + 2], in1=sn, op0=ALU.mult, op1=ALU.mult),
                            reads=[("ps", bp), "c_gqk", "sn"], writes=[("t2", hs)])
                        p.op("dve", lambda e, hs=hs: e.tensor_tensor(out=t1[hs], in0=t1[hs], in1=t2[hs], op=ALU.add),
                             reads=[("t1", hs), ("t2", hs)], writes=[("t1", hs)])
                        p.op("dve", lambda e, hs=hs: e.tensor_tensor(out=qr[hs], in0=t1[hs], in1=rs_[hs], op=ALU.mult),
                             reads=[("t1", hs), ("rs", hs)], writes=[("qr", hs)])
                        if head < 16:
                            p.dma("sp", ("qst", hs), lambda e, hs=hs, head=head, tok0=tok0: e.dma_start(
                                out=T["QT"][head, :, tok0:tok0 + 512], in_=qr[hs]), reads=[("qr", hs)])
                        else:
                            c0 = (head - 16) * 16 + 2 * tt
                            p.dma("sp", ("qst", hs), lambda e, hs=hs, c0=c0: e.dma_start(
                                out=KVin_bf[c0:c0 + 2].rearrange("c d t -> d c t"),
                                in_=qr[hs].rearrange("p (c t) -> p c t", c=2, t=256)),
                                reads=[("qr", hs)], writes=[("KVin", c0), ("KVin", c0 + 1)])
                if tt >= 1:
                    kv_collectives(tt - 1)
            kv_collectives(ntile - 1)
            p.barrier()

        def phase_attn():
            A.off = PERSIST
            KTs = [A.bf16([128, 4, NT]) for _ in range(2)]
            Vs = [A.bf16([128, 4, 32, 128]) for _ in range(2)]
            Qs = [A.bf16([128, 4, 128]) for _ in range(3)]
            Ps = [A.bf16([128, 2, 512]) for _ in range(4)]
            acc2 = [A.f32([128, 2, 512]) for _ in range(2)]
            accs = [A.f32([128, 512]) for _ in range(2)]
            rinv = [A.f32([128, 512]) for _ in range(2)]
            osb = [A.bf16([128, 4, 128]) for _ in range(2)]
            scale = 1.0 / math.sqrt(128.0)
            def load_kv(h):
                s = h % 2
                for r in range(4):
                    p.dma("sp", ("kv", s), lambda e, h=h, s=s, r=r: e.dma_start(
                        out=KTs[s][:, r, :].rearrange("p (c t) -> p c t", c=16, t=256),
                        in_=KVout_bf[h * 16:(h + 1) * 16, r * 128:(r + 1) * 128, :].rearrange("c d t -> d c t")),
                        reads=["KVout"], writes=[("K", s)])
                    p.dma("sp", ("kv", s), lambda e, h=h, s=s, r=r: e.dma_start(
                        out=Vs[s][:, r, :, :].rearrange("p k d -> p (k d)").rearrange("p (c f) -> p c f", c=16, f=256),
                        in_=KVout_bf[64 + h * 16:64 + (h + 1) * 16, r * 128:(r + 1) * 128, :].rearrange("c q f -> q c f")),
                        reads=["KVout"], writes=[("V", s)])

            nq = NT // 128
            qi_all = [(h, qt) for h in range(4) for qt in range(nq)]

            def load_q(i):
                h, qt = qi_all[i]
                s = i % 3
                p.dma("sp", ("q", s), lambda e, h=h, qt=qt, s=s: e.dma_start(
                    out=Qs[s], in_=T["QT"][4 * h:4 * h + 4, :, qt * 128:(qt + 1) * 128].rearrange("g d q -> d g q")),
                    reads=["QT"], writes=[("Q", s)])

            load_kv(0)
            load_q(0)
            load_q(1)
            npair = 64
            slot_ctr = [0]
            pairs = [(i, kp) for i in range(len(qi_all)) for kp in range(npair)]
            sslot = {}
            NSL = 2
            ACCK = [("ps", 6), ("ps", 7)]

            def emit_S(n):
                i, kp = pairs[n]
                h, qt = qi_all[i]
                s = h % 2
                qs = i % 3
                if kp == 0:
                    if qt == 0 and h + 1 < 4:
                        load_kv(h + 1)
                    if i + 2 < len(qi_all):
                        load_q(i + 2)
                Kt = KTs[s].rearrange("p r t -> p (r t)")
                Qt = Qs[qs].rearrange("p g q -> p (g q)")
                sb = slot_ctr[0] % NSL
                slot_ctr[0] += 1
                sslot[n] = sb
                for j in range(2):
                    ktile = 2 * kp + j
                    p.op("pe", lambda e, sb=sb, j=j, ktile=ktile, Kt=Kt, Qt=Qt: e.matmul(
                        ps[:, 2 * sb + j, :], lhsT=Kt[:, ktile * 128:(ktile + 1) * 128], rhs=Qt, start=True, stop=True),
                        reads=[("K", s), ("Q", qs)], writes=[("ps", 2 * sb), ("ps", 2 * sb + 1)])

            def emit_rest(n):
                i, kp = pairs[n]
                h, qt = qi_all[i]
                s = h % 2
                Vt = Vs[s].rearrange("p r k d -> p (r k) d")
                ob_ = 4 + i % 2
                sb = sslot.pop(n)
                pb = n % 4
                p.op("act", lambda e, sb=sb, pb=pb: e.activation(
                    out=Ps[pb], in_=ps[:, 2 * sb:2 * sb + 2, :], func=AF.Exp, scale=scale),
                    reads=[("ps", 2 * sb), ("ps", 2 * sb + 1)], writes=[("P", pb)])
                for j in range(2):
                    ktile = 2 * kp + j
                    p.op("pe", lambda e, pb=pb, j=j, ktile=ktile, Vt=Vt, ob_=ob_, kp=kp: e.matmul(
                        ps[:, ob_, :], lhsT=Vt[:, ktile, :], rhs=Ps[pb][:, j, :],
                        start=(kp == 0 and j == 0), stop=(kp == npair - 1 and j == 1)),
                        reads=[("V", s), ("P", pb)], writes=[("ps", ob_)])
                a2p = acc2[i % 2]
                apk = ("acc2p", i % 2)
                if kp % 4 == 1:
                    if kp == 1:
                        p.op("pool", lambda e, pb=pb, a2p=a2p: e.tensor_copy(out=a2p, in_=Ps[pb]), reads=[("P", pb)], writes=[apk])
                    else:
                        p.op("pool", lambda e, pb=pb, a2p=a2p: e.tensor_tensor(out=a2p, in0=a2p, in1=Ps[pb], op=ALU.add),
                             reads=[("P", pb), apk], writes=[apk])
                elif kp == 0:
                    p.op("dve", lambda e, pb=pb: e.tensor_copy(out=ps[:, 6:8, :], in_=Ps[pb]), reads=[("P", pb)], writes=ACCK)
                else:
                    p.op("dve", lambda e, pb=pb: e.tensor_tensor(out=ps[:, 6:8, :], in0=ps[:, 6:8, :], in1=Ps[pb], op=ALU.add),
                         reads=[("P", pb)] + ACCK, writes=ACCK)
                if kp == npair - 1:
                    r_ = i % 2
                    p.op("dve", lambda e, r_=r_: e.tensor_copy(out=rinv[r_], in_=ps[:, 6, :]), reads=ACCK, writes=[("rinv", r_)])
                    p.op("dve", lambda e, r_=r_: e.tensor_tensor(out=accs[r_], in0=ps[:, 7, :], in1=rinv[r_], op=ALU.add),
                         reads=ACCK + [("rinv", r_)], writes=[("accs", r_)])
                    p.op("dve", lambda e, r_=r_, a2p=a2p: e.tensor_tensor(out=accs[r_], in0=accs[r_], in1=a2p[:, 0, :], op=ALU.add),
                         reads=[apk, ("accs", r_)], writes=[("accs", r_)])
                    p.op("dve", lambda e, r_=r_, a2p=a2p: e.tensor_tensor(out=accs[r_], in0=accs[r_], in1=a2p[:, 1, :], op=ALU.add),
                         reads=[apk, ("accs", r_)], writes=[("accs", r_)])

            def emit_final(i):
                h, qt = qi_all[i]
                r_ = i % 2
                ob_ = 4 + i % 2
                sbk = 2 * (slot_ctr[0] % NSL)
                slot_ctr[0] += 1
                p.op("pe", lambda e, r_=r_, sbk=sbk: e.matmul(ps[:, sbk, :], lhsT=ones_f, rhs=accs[r_], start=True, stop=True),
                     reads=[("accs", r_), "ones_f"], writes=[("ps", sbk), ("ps", sbk + 1)])
                p.op("dve", lambda e, r_=r_, sbk=sbk: e.reciprocal(out=rinv[r_], in_=ps[:, sbk, :]),
                     reads=[("ps", sbk), ("ps", sbk + 1)], writes=[("rinv", r_)])
                p.op("dve", lambda e, r_=r_, ob_=ob_: e.tensor_tensor(
                    out=osb[r_].rearrange("p g q -> p (g q)"), in0=ps[:, ob_, :], in1=rinv[r_], op=ALU.mult),
                    reads=[("ps", ob_), ("rinv", r_)], writes=[("osb", r_)])
                p.dma("sp", ("ost", r_), lambda e, r_=r_, h=h, qt=qt: e.dma_start(
                    out=T["OT"][4 * h:4 * h + 4, :, qt * 128:(qt + 1) * 128].rearrange("g d q -> d g q"), in_=osb[r_]),
                    reads=[("osb", r_)])

            emit_S(0)
            pending_final = None
            for n in range(len(pairs)):
                if n + 1 < len(pairs):
                    emit_S(n + 1)
                emit_rest(n)
                if pending_final is not None:
                    emit_final(pending_final)
                    pending_final = None
                if pairs[n][1] == npair - 1:
                    pending_final = pairs[n][0]
            if pending_final is not None:
                emit_final(pending_final)
            p.barrier()

        def phase_wo():
            A.off = PERSIST
            oT = [A.bf16([128, 16, 512]) for _ in range(2)]
            wsl = [A.bf16([128, 16, 512]) for _ in range(3)]
            xr = [A.f32([128, 512]) for _ in range(2)]
            ob = [A.f32([128, 512]) for _ in range(2)]
            ntile = NT // 512
            items = []
            for tt in range(ntile):
                items += colblocks(T["Wb_o"], 0, 4)
            ws = WStream("ow", wsl, items, ["W_o"])
            gt1 = modT[:, 1, 32:48]
            OTv = T["OT"].rearrange("h d t -> d h t")
            for tt in range(ntile):
                tok0 = tt * 512
                s = tt % 2
                p.dma("sp", ("oT", s), lambda e, s=s, tok0=tok0: e.dma_start(out=oT[s], in_=OTv[:, :, tok0:tok0 + 512]),
                      reads=["OT"], writes=[("oT", s)])
                proj_residual(oT[s], ("oT", s), ws, tt * 4, T["x2T"], T["x3T"], gt1, tok0, 512, xr, ob, "x3", [0, 1, 2, 3])
            p.barrier()

        def phase_final():
            A.off = PERSIST
            xt = [A.f32([128, 16, 512]) for _ in range(2)]
            sq = A.bf16([128, 16, 512])
            rstd = A.f32([128, 512])
            yo = [A.f32([128, 16, 512]) for _ in range(2)]
            XS = fchunk(T["x4T"])
            XD = fchunk(T["outT"])
            ntile = NT // 512
            for tt in range(ntile):
                tok0 = tt * 512
                s = tt % 2
                p.dma("sp", ("fx", s), lambda e, s=s, tok0=tok0: e.dma_start(out=xt[s], in_=XS[:, :, tok0:tok0 + 512]),
                      reads=[("X", "x4T")], writes=[("fxt", s)])
                norm_mod(xt[s], 512, sq, 7, rstd, None, None, g_finalT, None, ("fxt", s), ("fyo", s), plain_out=yo[s])
                p.dma("sp", ("fo", s), lambda e, s=s, tok0=tok0: e.dma_start(out=XD[:, :, tok0:tok0 + 512], in_=yo[s]),
                      reads=[("fyo", s)])
            p.barrier()

        def phase_halo(l, xname):
            A.off = PERSIST
            eo = A.f32([128, 4, 16, 2])
            X = fchunk(T[xname])
            Ein = T["E_in%d" % l]
            Eout = T["E_out%d" % l]
            Ein3 = Ein.rearrange("p (k e) -> p k e", e=2)
            p.dma("sp", "ein", lambda e: e.dma_start(out=Ein3[:, :, 0:1], in_=X[:, :, 0:1], allow_slow_non_contiguous=True),
                  reads=[("X", xname)], writes=["E_in"])
            p.dma("sp", "ein", lambda e: e.dma_start(out=Ein3[:, :, 1:2], in_=X[:, :, NT - 1:NT], allow_slow_non_contiguous=True),
                  reads=[("X", xname)], writes=["E_in"])
            p.dma("pool", "cc_halo", lambda e: e.collective_compute(
                "AllGather", ALU.bypass, replica_groups=GROUPS, ins=[Ein], outs=[Eout]),
                reads=["E_in"], writes=["E_out"], inc=1)
            p.dma("sp", "eo", lambda e: e.dma_start(out=eo, in_=Eout.rearrange("(r p) (k e) -> p r k e", p=128, e=2)),
                  reads=["E_out"], writes=["eo"])
            for side in range(2):
                src_e = 1 - side
                for r in range(4):
                    sc = sel[:, side * 4 + r:side * 4 + r + 1]
                    if r == 0:
                        p.op("dve", lambda e, side=side, src_e=src_e, sc=sc: e.tensor_scalar(
                            out=halo_sb[:, :, side], in0=eo[:, 0, :, src_e], scalar1=sc, scalar2=None, op0=ALU.mult),
                            reads=["eo", "c_sel"], writes=["halo_sb"])
                    else:
                        p.op("dve", lambda e, side=side, src_e=src_e, sc=sc, r=r: e.scalar_tensor_tensor(
                            out=halo_sb[:, :, side], in0=eo[:, r, :, src_e], scalar=sc, in1=halo_sb[:, :, side],
                            op0=ALU.mult, op1=ALU.add), reads=["eo", "c_sel", "halo_sb"], writes=["halo_sb"])
            p.barrier()

        p.barrier()
        phase_ada_all()
        convert_early()
        phase_gmlp()
        phase_halo(0, "x1T")
        convert_late()
        phase_ffn(0, "x1T", "x2T")
        phase_qkv()
        phase_attn()
        phase_wo()
        phase_halo(1, "x3T")
        phase_ffn(1, "x3T", "x4T")
        phase_final()
        final_groups = [g for g in p.group_counts if g not in p.bg_groups]
        p.emit(final_groups=final_groups)
    return nc, T


def _rope_tables(t0):
    S = 16384
    s = np.arange(t0, t0 + NT)
    row = (s // 64).astype(np.float32)
    col = (s % 64).astype(np.float32)
    inv = (10000.0 ** (-np.arange(0, 64, 2, dtype=np.float32) / 64.0)).astype(np.float32)
    ang_r = (row[:, None] * inv[None, :]).astype(np.float32)
    ang_c = (col[:, None] * inv[None, :]).astype(np.float32)
    cr, sr, cc, sc = np.cos(ang_r), np.sin(ang_r), np.cos(ang_c), np.sin(ang_c)
    cosT = np.concatenate([cr, cr, cc, cc], axis=1).T
    sinT = np.concatenate([-sr, sr, -sc, sc], axis=1).T
    return np.ascontiguousarray(cosT, dtype=np.float32), np.ascontiguousarray(sinT, dtype=np.float32)


def _partner_idx(n):
    d = np.arange(n)
    w = d % 64
    return np.where(w < 32, d + 32, d - 32)


_PROG_CACHE = {}


def _get_prog():
    if "p" not in _PROG_CACHE:
        _PROG_CACHE["p"] = build_program()
    return _PROG_CACHE["p"]


def _prep(x, c, w_ada, b_ada, g_norm, g_final, a_w_in, a_g_v, a_w_s, a_b_s, a_w_out,
          b_w_qkv, b_g_q, b_g_k, b_w_o, f_w_up, f_conv_w, f_conv_b, f_w_down):
    f = lambda a: np.ascontiguousarray(np.asarray(a), dtype=np.float32)
    x, c, w_ada, b_ada, g_norm, g_final = f(x), f(c), f(w_ada), f(b_ada), f(g_norm), f(g_final)
    a_w_in, a_g_v, a_w_s, a_b_s, a_w_out = f(a_w_in), f(a_g_v), f(a_w_s), f(a_b_s), f(a_w_out)
    b_w_qkv, b_g_q, b_g_k, b_w_o = f(b_w_qkv), f(b_g_q), f(b_g_k), f(b_w_o)
    f_w_up, f_conv_w, f_conv_b, f_w_down = f(f_w_up), f(f_conv_w), f(f_conv_b), f(f_w_down)

    shared = {
        "g_normT": f(g_norm.reshape(2, 2, 16, 128).transpose(3, 0, 1, 2)),
        "conv_wT": f(f_conv_w.reshape(2, 3, 88, 128).transpose(3, 0, 1, 2)),
        "conv_bT": f(f_conv_b.reshape(2, 88, 128).transpose(2, 0, 1)),
        "g_finalT": f(g_final.reshape(16, 128).T),
        "a_w_in": a_w_in[0], "a_w_out": a_w_out[0],
        "w_sT": f(a_w_s[0].transpose(2, 0, 1).reshape(128, 2048)),
        "g_v_bc": f(np.broadcast_to(a_g_v[0][None, :], (128, 2048))),
        "b_s_bc": f(np.broadcast_to(a_b_s[0].reshape(1, 2048), (128, 2048))),
        "f_w_up0": f_w_up[0], "f_w_up1": f_w_up[1], "f_w_down0": f_w_down[0], "f_w_down1": f_w_down[1],
        "b_w_o": b_w_o[0],
    }
    wqk = f(b_w_qkv[0][:, 0:2560])
    pidx = _partner_idx(2560)
    shared["w_qk"] = wqk
    rp = np.zeros((128, 128), np.float32)
    pp = _partner_idx(128)
    rp[pp, np.arange(128)] = 1.0
    shared["rperm"] = rp
    shared["w_v"] = f(b_w_qkv[0][:, 2560:3072])
    p128 = _partner_idx(128)
    shared["gqk"] = f(np.stack([b_g_q[0], b_g_q[0][p128], b_g_k[0], b_g_k[0][p128]], axis=1))

    cores = []
    for ci in range(8):
        b, r = ci // 4, ci % 4
        t0 = r * NT
        m = dict(shared)
        m["xT"] = f(x[b, t0:t0 + NT, :].T)
        m["cT"] = f(c[b].reshape(16, 128).T)
        cs, sn = _rope_tables(t0)
        m["cosT"], m["sinT"] = cs, sn
        m["hmask"] = f(np.broadcast_to(np.array([[0.0 if r == 0 else 1.0, 0.0 if r == 3 else 1.0]], np.float32), (128, 2)))
        sl = np.zeros((128, 8), np.float32)
        if r > 0:
            sl[:, r - 1] = 1.0
        if r < 3:
            sl[:, 4 + r + 1] = 1.0
        m["sel"] = sl
        m["w_adaS"] = f(w_ada[:, :, r * 3072:(r + 1) * 3072])
        m["b_adaS"] = f(b_ada[:, r * 3072:(r + 1) * 3072].reshape(2, 24, 128).transpose(2, 0, 1).reshape(128, 48))
        cores.append(m)

    return cores


IN_NAMES = ["cT", "g_normT", "conv_wT", "conv_bT", "g_finalT", "gqk", "hmask", "sel", "w_adaS", "b_adaS", "xT",
            "a_w_in", "a_w_out", "w_sT", "g_v_bc", "b_s_bc", "f_w_up0", "f_w_down0", "f_w_up1", "f_w_down1",
            "w_qk", "rperm", "w_v", "cosT", "sinT", "b_w_o"]


def kernel(**inputs):
    cores = _prep(**inputs)
    nc, T = _get_prog()
    maps = [{n: m[n] for n in IN_NAMES} for m in cores]
    res = run_bass_kernel_spmd(nc, maps, core_ids=list(range(8)))
    out = np.empty((2, 16384, D), np.float32)
    for ci in range(8):
        b, r = ci // 4, ci % 4
        out[b, r * NT:(r + 1) * NT, :] = res.results[ci]["outT"].T
    return out
```

```python
import contextlib
import math
import numpy as np
import ml_dtypes
import concourse.bass as bass
import concourse.mybir as mybir
from concourse.bass_utils import run_bass_kernel_spmd

F32 = mybir.dt.float32
BF16 = mybir.dt.bfloat16
AF = mybir.ActivationFunctionType
ALU = mybir.AluOpType

D = 2048
KC = 16
NT = 4096
DFF = 5632
NF = 44
EPS = 1e-6
SUBW = 410
NSUBT = 10
COMPUTE = ("pe", "act", "dve", "pool")


class Op:
    __slots__ = ("eng", "fn", "raw", "oth", "flag", "idx", "group", "gcount", "is_dma", "inc")

    def __init__(self, eng, fn, is_dma, group):
        self.eng = eng
        self.fn = fn
        self.raw = set()
        self.oth = set()
        self.flag = False
        self.idx = 0
        self.group = group
        self.gcount = 0
        self.is_dma = is_dma
        self.inc = 16


def _ekey(o):
    return ("dma", o.group) if o.is_dma else o.eng


class Prog:
    def __init__(self, nc):
        self.nc = nc
        self.ops = []
        self.last_writer = {}
        self.readers = {}
        self.group_counts = {}
        self.group_last = {}
        self.last_on_eng = {}
        self.pending_barrier = {}
        self.bg_groups = set()
        self.group_inc = {}

    def _add(self, op, reads, writes):
        for r in reads:
            w = self.last_writer.get(r)
            if w is not None:
                op.raw.add(w)
        for wkey in writes:
            w = self.last_writer.get(wkey)
            if w is not None:
                op.oth.add(w)
            for rd in self.readers.get(wkey, {}).values():
                op.oth.add(rd)
        pb = self.pending_barrier.pop(op.eng, None)
        if pb:
            op.raw.update(pb)
        op.raw.discard(op)
        op.oth.discard(op)
        op.oth -= op.raw
        for r in reads:
            self.readers.setdefault(r, {})[_ekey(op)] = op
        for wkey in writes:
            self.last_writer[wkey] = op
            self.readers[wkey] = {}
        self.ops.append(op)
        self.last_on_eng[op.eng] = op
        return op

    def op(self, eng, fn, reads=(), writes=()):
        return self._add(Op(eng, fn, False, None), reads, writes)

    def dma(self, queue, group, fn, reads=(), writes=(), inc=16):
        o = Op(queue, fn, True, group)
        o.inc = inc
        self.group_inc[group] = inc
        self.group_counts[group] = self.group_counts.get(group, 0) + 1
        o.gcount = self.group_counts[group]
        self.group_last[group] = o
        return self._add(o, reads, writes)

    def barrier(self):
        lasts = set()
        for e, o in self.last_on_eng.items():
            if not o.is_dma:
                lasts.add(o)
        for g, o in self.group_last.items():
            if g not in self.bg_groups:
                lasts.add(o)
        for e in ("pe", "act", "dve", "pool", "sp"):
            self.pending_barrier[e] = set(lasts)

    def emit(self, final_groups=()):
        nc = self.nc
        ops = self.ops

        def needed(o, d, is_raw):
            if d.is_dma or o.is_dma:
                return True
            if d.eng != o.eng:
                return True
            if o.eng == "pe":
                return False
            return is_raw

        for o in ops:
            for d in o.raw:
                if needed(o, d, True) and not d.is_dma:
                    d.flag = True
            for d in o.oth:
                if needed(o, d, False) and not d.is_dma:
                    d.flag = True
        counts = {e: 0 for e in COMPUTE}
        for o in ops:
            if not o.is_dma and o.flag:
                counts[o.eng] += 1
                o.idx = counts[o.eng]
        groups = sorted(self.group_counts, key=str)
        with contextlib.ExitStack() as st:
            sem_eng = {e: st.enter_context(nc.semaphore("m_" + e)) for e in COMPUTE}
            sem_grp = {g: st.enter_context(nc.semaphore("g%d" % i)) for i, g in enumerate(groups)}
            block = st.enter_context(nc.Block())
            per_eng = {"sp": []}
            for o in ops:
                per_eng.setdefault(o.eng, []).append(o)

            def make(engname, lst):
                def body(eng):
                    known = {}
                    for o in lst:
                        waits = {}
                        for is_raw, ds in ((True, o.raw), (False, o.oth)):
                            for d in ds:
                                if not needed(o, d, is_raw):
                                    continue
                                if d.is_dma:
                                    key = ("g", d.group)
                                    val = d.inc * d.gcount
                                else:
                                    key = ("e", d.eng)
                                    val = d.idx
                                if known.get(key, 0) >= val:
                                    continue
                                if waits.get(key, 0) < val:
                                    waits[key] = val
                        for key, val in waits.items():
                            known[key] = val
                            sem = sem_grp[key[1]] if key[0] == "g" else sem_eng[key[1]]
                            eng.wait_ge(sem, val)
                        ins = o.fn(eng)
                        if o.is_dma:
                            ins.then_inc(sem_grp[o.group], o.inc)
                        elif o.flag:
                            ins.then_inc(sem_eng[o.eng], 1)
                    if engname == "sp":
                        for g in final_groups:
                            eng.wait_ge(sem_grp[g], self.group_inc[g] * self.group_counts[g])
                return body

            attr = {"pe": "tensor", "act": "scalar", "dve": "vector", "pool": "gpsimd", "sp": "sync"}
            for e, lst in per_eng.items():
                getattr(block, attr[e])(make(e, lst))


class Arena:
    def __init__(self, ap, nwords):
        self.ap = ap
        self.n = nwords
        self.off = 0

    def _view(self, v, shape):
        if len(shape) == 2:
            return v
        if len(shape) == 3:
            return v.rearrange("p (a b) -> p a b", a=shape[1], b=shape[2])
        if len(shape) == 4:
            return v.rearrange("p (a b c) -> p a b c", a=shape[1], b=shape[2], c=shape[3])
        raise ValueError(shape)

    def f32(self, shape):
        n = int(np.prod(shape[1:]))
        assert self.off + n <= self.n, ("arena overflow", self.off, n)
        v = self.ap[:, self.off:self.off + n]
        self.off += n
        return self._view(v, shape)

    def bf16(self, shape):
        n = int(np.prod(shape[1:]))
        w = (n + 1) // 2
        assert self.off + w <= self.n, ("arena overflow", self.off, w)
        v = self.ap[:, self.off:self.off + w].bitcast(BF16)[:, 0:n]
        self.off += w
        return self._view(v, shape)


def fchunk(X):
    return X.rearrange("(kc p) t -> p kc t", p=128)


def build_program():
    nc = bass.Bass("TRN2", target_bir_lowering=False)
    p = Prog(nc)
    T = {}
    GROUPS = [[0, 1, 2, 3], [4, 5, 6, 7]]

    def dram(name, shape, dtype, kind):
        if kind == "cc":
            T[name] = nc.dram_tensor(name, list(shape), dtype).ap()
        else:
            k = {"in": "ExternalInput", "out": "ExternalOutput", "tmp": "Internal"}[kind]
            T[name] = nc.dram_tensor(name, list(shape), dtype, kind=k).ap()
        return T[name]

    layers = [0, 1]
    for nm, shp in (("cT", [128, 16]), ("g_normT", [128, 2, 2, 16]), ("conv_wT", [128, 2, 3, 88]),
                    ("conv_bT", [128, 2, 88]), ("g_finalT", [128, 16]), ("gqk", [128, 4]), ("hmask", [128, 2]), ("sel", [128, 8]),
                    ("w_adaS", [2, D, 3072]), ("b_adaS", [128, 48]), ("xT", [D, NT]), ("a_w_in", [D, 2 * D]), ("a_w_out", [D, D]),
                    ("w_sT", [128, D]), ("g_v_bc", [128, D]), ("b_s_bc", [128, D]),
                    ("f_w_up0", [D, 2 * DFF]), ("f_w_down0", [DFF, D]), ("f_w_up1", [D, 2 * DFF]), ("f_w_down1", [DFF, D]),
                    ("w_qk", [D, 2560]), ("rperm", [128, 128]), ("w_v", [D, 512]), ("cosT", [128, NT]), ("sinT", [128, NT]),
                    ("b_w_o", [D, D])):
        dram(nm, shp, F32, "in")
    for nm, shp in (("Wb_in", [D, 2 * D]), ("Wb_out", [D, D]), ("Wb_up0", [22, 128, 8192]), ("Wb_down0", [16, 128, NF * 128]),
                    ("Wb_up1", [22, 128, 8192]), ("Wb_down1", [16, 128, NF * 128]), ("Wb_qk", [D, 2560]),
                    ("Wb_v", [D, 512]), ("Wb_o", [D, D]), ("QT", [16, 128, NT]), ("OT", [16, 128, NT])):
        dram(nm, shp, BF16, "tmp")
    for nm in ("x1T", "x2T", "x3T", "x4T"):
        dram(nm, [D, NT], F32, "tmp")
    dram("outT", [D, NT], F32, "out")
    dram("KVin", [128, 128, 128], F32, "cc")
    dram("KVout", [128, 512, 128], F32, "cc")
    dram("A_in", [128, 48], F32, "cc")
    dram("A_out", [512, 48], F32, "cc")
    for l in (0, 1):
        dram("E_in%d" % l, [128, 32], F32, "cc")
        dram("E_out%d" % l, [512, 32], F32, "cc")
    KVin_bf = T["KVin"].bitcast(BF16)
    KVout_bf = T["KVout"].bitcast(BF16)

    with contextlib.ExitStack() as st:
        arena_t = st.enter_context(nc.sbuf_tensor("arena", [128, 49152], F32))
        ps = st.enter_context(nc.psum_tensor("ps", [128, 8, 512], F32))
        A = Arena(arena_t[:, :], 49152)

        ones_bf = A.bf16([128, 128])
        ones_f = A.f32([128, 128])
        eps_t = A.f32([128, 1])
        zero_t = A.f32([128, 1])
        cT = A.f32([128, 16])
        cond = A.f32([128, 16])
        modT = A.f32([128, 2, 96])
        gmul = A.f32([128, 2, 2, 16])
        b_adaT = A.f32([128, 2, 96])
        g_normT = A.f32([128, 2, 2, 16])
        conv_wT = A.f32([128, 2, 3, 88])
        conv_bT = A.f32([128, 2, 88])
        g_finalT = A.f32([128, 16])
        gqk = A.f32([128, 4])
        hmask = A.f32([128, 2])
        sel = A.f32([128, 8])
        halo_sb = A.f32([128, 16, 2])
        w_sT = A.bf16([128, 16, 128])
        PERSIST = A.off

        p.op("dve", lambda e: e.memset(ones_bf, 1.0), writes=["ones_bf"])
        p.op("dve", lambda e: e.memset(ones_f, 1.0), writes=["ones_f"])
        p.op("dve", lambda e: e.memset(eps_t, EPS), writes=["eps"])
        p.op("dve", lambda e: e.memset(zero_t, 0.0), writes=["zero"])
        for nm, dst in (("cT", cT), ("g_normT", g_normT), ("conv_wT", conv_wT),
                        ("conv_bT", conv_bT), ("g_finalT", g_finalT), ("gqk", gqk), ("hmask", hmask), ("sel", sel)):
            p.dma("sp", "const", lambda e, nm=nm, dst=dst: e.dma_start(out=dst, in_=T[nm]), writes=["c_" + nm])
        CONST_KEYS = ["c_cT", "c_b_adaT", "c_g_normT", "c_conv_wT", "c_conv_bT", "c_g_finalT", "c_gqk", "c_hmask",
                      "ones_bf", "ones_f", "eps", "zero"]

        p.dma("pool", "wsT", lambda e: e.dma_start(out=w_sT, in_=T["w_sT"].rearrange("p (g i) -> p g i", g=16)), writes=["w_sT"])

        def convert(src, dst, key):
            R, C = T[src].shape
            tot = R * C
            per = tot // 128
            sv = T[src].rearrange("(p r) c -> p (r c)", p=128)
            dv = T[dst].rearrange("(p r) c -> p (r c)", p=128)
            CH = 8192
            grp = "cv_" + key
            p.bg_groups.add(grp)
            for a in range(0, per, CH):
                b = min(per, a + CH)
                p.dma("pool", grp, lambda e, a=a, b=b: e.dma_start(out=dv[:, a:b], in_=sv[:, a:b]), writes=["W_" + key])

        def convert_up(l):
            grp = "cv_up%d" % l
            p.bg_groups.add(grp)
            S5 = T["f_w_up%d" % l].rearrange("(kc p) (wh fb c) -> fb p wh kc c", p=128, wh=2, fb=22, c=256)
            for fb in range(22):
                p.dma("pool", grp, lambda e, fb=fb: e.dma_start(
                    out=T["Wb_up%d" % l][fb].rearrange("p (wh kc c) -> p wh kc c", wh=2, kc=16, c=256), in_=S5[fb]),
                    writes=["W_up%d" % l])

        def convert_down(l):
            grp = "cv_down%d" % l
            p.bg_groups.add(grp)
            S4 = T["f_w_down%d" % l].rearrange("(f p) (dc c) -> dc p f c", p=128, c=128)
            for dc in range(16):
                p.dma("pool", grp, lambda e, dc=dc: e.dma_start(
                    out=T["Wb_down%d" % l][dc].rearrange("p (f c) -> p f c", f=NF, c=128), in_=S4[dc]),
                    writes=["W_down%d" % l])

        def convert_first():
            convert("a_w_in", "Wb_in", "in")
            convert("a_w_out", "Wb_out", "out")

        def convert_early():
            convert_up(0)
            convert_down(0)

        def convert_late():
            convert("w_v", "Wb_v", "v")
            convert("w_qk", "Wb_qk", "qk")
            convert("b_w_o", "Wb_o", "o")
            convert_up(1)
            convert_down(1)

        convert_first()

        def phase_ada_all():
            A.off = PERSIST
            wbl = [A.f32([128, 16, 512]) for _ in range(2)]
            modS = A.f32([128, 48])
            b_adaS = A.f32([128, 48])
            p.dma("sp", "const", lambda e: e.dma_start(out=b_adaS, in_=T["b_adaS"]), writes=["b_adaS"])
            p.op("act", lambda e: e.activation(out=cond, in_=cT, func=AF.Silu), reads=["c_cT"], writes=["cond"])
            blocks = [(l, cb) for l in range(2) for cb in range(6)]
            Wl = [fchunk(T["w_adaS"][l]) for l in range(2)]

            def load(i):
                l, cb = blocks[i]
                s = i % 2
                p.dma("sp", ("ada", s), lambda e, l=l, cb=cb, s=s: e.dma_start(out=wbl[s], in_=Wl[l][:, :, cb * 512:(cb + 1) * 512]),
                      writes=[("adaw", s)])
            load(0)
            for i, (l, cb) in enumerate(blocks):
                if i + 1 < len(blocks):
                    load(i + 1)
                s = i % 2
                for j in range(4):
                    col = l * 24 + cb * 4 + j
                    for kc in range(KC):
                        p.op("pe", lambda e, s=s, j=j, kc=kc, col=col: e.matmul(
                            ps[:, 0, col:col + 1], lhsT=wbl[s][:, kc, j * 128:(j + 1) * 128], rhs=cond[:, kc:kc + 1],
                            start=(kc == 0), stop=(kc == KC - 1)),
                            reads=[("adaw", s), "cond"], writes=[("ps", 0)])
            p.op("dve", lambda e: e.tensor_tensor(out=modS, in0=ps[:, 0, 0:48], in1=b_adaS, op=ALU.add),
                 reads=[("ps", 0), "b_adaS"], writes=["modS"])
            p.dma("sp", "ain", lambda e: e.dma_start(out=T["A_in"], in_=modS), reads=["modS"], writes=["A_in"])
            p.dma("pool", "cc_ada", lambda e: e.collective_compute(
                "AllGather", ALU.bypass, replica_groups=GROUPS, ins=[T["A_in"]], outs=[T["A_out"]]),
                reads=["A_in"], writes=["A_out"], inc=1)
            p.dma("sp", "aout", lambda e: e.dma_start(
                out=modT.rearrange("p l (r j) -> p l r j", r=4, j=24),
                in_=T["A_out"].rearrange("(r p) (l j) -> p l r j", p=128, l=2)),
                reads=["A_out"], writes=["modT"])
            for l in range(2):
                for s_ in range(2):
                    sc0 = 16 + 48 * s_
                    p.op("dve", lambda e, l=l, s_=s_, sc0=sc0: e.scalar_tensor_tensor(
                        out=gmul[:, l, s_, :], in0=modT[:, l, sc0:sc0 + 16], scalar=1.0, in1=g_normT[:, l, s_, :],
                        op0=ALU.add, op1=ALU.mult), reads=["modT", "c_g_normT"], writes=["gmul"])
            p.barrier()

        def norm_mod(xin, N, sq, psb, rstd, tmps, hT, gm, sh, xkey, hkey, plain_out=None, sqkey="sq"):
            p.op("act", lambda e: e.activation(out=sq[:, :, 0:N], in_=xin[:, :, 0:N], func=AF.Square),
                 reads=[xkey], writes=[sqkey])
            for kc in range(KC):
                p.op("pe", lambda e, kc=kc: e.matmul(ps[:, psb, 0:N], lhsT=ones_bf, rhs=sq[:, kc, 0:N],
                                                     start=(kc == 0), stop=(kc == KC - 1)),
                     reads=[sqkey, "ones_bf"], writes=[("ps", psb)])
            p.op("act", lambda e: e.activation(out=rstd[:, 0:N], in_=ps[:, psb, 0:N], func=AF.Sqrt, bias=eps_t[:, 0:1],
                                               scale=1.0 / D), reads=[("ps", psb), "eps"], writes=["rstd"])
            p.op("dve", lambda e: e.reciprocal(out=rstd[:, 0:N], in_=rstd[:, 0:N]), reads=["rstd"], writes=["rstd"])
            for kc in range(KC):
                if plain_out is not None:
                    p.op("dve", lambda e, kc=kc: e.scalar_tensor_tensor(
                        out=plain_out[:, kc, 0:N], in0=xin[:, kc, 0:N], scalar=gm[:, kc:kc + 1], in1=rstd[:, 0:N],
                        op0=ALU.mult, op1=ALU.mult), reads=[xkey, "rstd", "c_g_finalT"], writes=[hkey])
                    continue
                tb = tmps[kc % len(tmps)]
                tk = ("nm_tmp", kc % len(tmps))
                p.op("dve", lambda e, kc=kc, tb=tb: e.scalar_tensor_tensor(
                    out=tb[:, 0:N], in0=xin[:, kc, 0:N], scalar=gm[:, kc:kc + 1], in1=rstd[:, 0:N],
                    op0=ALU.mult, op1=ALU.mult), reads=[xkey, "rstd", "gmul"], writes=[tk])
                p.op("act", lambda e, kc=kc, tb=tb: e.activation(
                    out=hT[:, kc, 0:N], in_=tb[:, 0:N], func=AF.Identity, bias=sh[:, kc:kc + 1], scale=1.0),
                    reads=[tk, "modT"], writes=[hkey])

        class WStream:
            def __init__(self, name, slots, items, wkeys):
                self.name = name
                self.slots = slots
                self.items = items
                self.wkeys = wkeys
                self.next = 0

            def issue_upto(self, k):
                while self.next <= min(k, len(self.items) - 1):
                    i = self.next
                    s = i % len(self.slots)
                    for (o_ap, i_ap) in self.items[i](self.slots[s]):
                        p.dma("sp", (self.name, s), lambda e, o_ap=o_ap, i_ap=i_ap: e.dma_start(out=o_ap, in_=i_ap),
                              reads=self.wkeys, writes=[(self.name, s)])
                    self.next += 1

            def use(self, i):
                self.issue_upto(i + len(self.slots) - 1)
                s = i % len(self.slots)
                return self.slots[s], (self.name, s)

        def proj_residual(rhsT, rhskey, ws, ws_base, xsrc, xdst, gt, tok0, N, xr, ob, dkey, banks):
            XS = fchunk(xsrc)
            XD = fchunk(xdst)
            pend = []
            for ob_ in range(4):
                wblk, wk = ws.use(ws_base + ob_)
                for j in range(4):
                    dc = ob_ * 4 + j
                    bk = banks[dc % len(banks)]
                    rs = dc % 2
                    p.dma("sp", ("xr", rs), lambda e, dc=dc, rs=rs: e.dma_start(out=xr[rs][:, 0:N], in_=XS[:, dc, tok0:tok0 + N]),
                          writes=[("xr", rs)])
                    while pend:
                        pend.pop(0)()
                    for g in range(KC):
                        p.op("pe", lambda e, g=g, j=j, bk=bk, wblk=wblk: e.matmul(
                            ps[:, bk, 0:N], lhsT=wblk[:, g, j * 128:(j + 1) * 128], rhs=rhsT[:, g, 0:N],
                            start=(g == 0), stop=(g == KC - 1)), reads=[wk, rhskey], writes=[("ps", bk)])
                    p.op("dve", lambda e, dc=dc, bk=bk, rs=rs: e.scalar_tensor_tensor(
                        out=ob[rs][:, 0:N], in0=ps[:, bk, 0:N], scalar=gt[:, dc:dc + 1], in1=xr[rs][:, 0:N],
                        op0=ALU.mult, op1=ALU.add), reads=[("ps", bk), ("xr", rs), "modT"], writes=[("ob", rs)])
                    pend.append(lambda dc=dc, rs=rs: p.dma("sp", ("xo", rs), lambda e: e.dma_start(
                        out=XD[:, dc, tok0:tok0 + N], in_=ob[rs][:, 0:N]), reads=[("ob", rs)]))
            while pend:
                pend.pop(0)()

        def colblocks(Wb, c0, nblk, width=512):
            V = fchunk(Wb)
            return [(lambda slot, i=i: [(slot[:, :, 0:width], V[:, :, c0 + i * width:c0 + (i + 1) * width])]) for i in range(nblk)]

        def phase_gmlp():
            A.off = PERSIST
            xt = A.f32([128, 16, 512])
            bufA = A.bf16([128, 16, 512])
            bufB = A.bf16([128, 16, 512])
            vn = A.bf16([128, 4, 2048])
            rstd = A.f32([128, 512])
            tmps = [A.f32([128, 512]) for _ in range(2)]
            vts = [A.f32([128, 512]) for _ in range(2)]
            junk = A.bf16([128, 512])
            ssq = A.f32([128, 16])
            ssv = A.f32([128, 4])
            svts = [A.f32([128, 512]) for _ in range(2)]
            g_v_bc = A.f32([128, 2048])
            b_s_bc = A.f32([128, 2048])
            wsl = [A.bf16([128, 16, 512]) for _ in range(3)]
            xr = [A.f32([128, 512]) for _ in range(2)]
            ob = [A.f32([128, 512]) for _ in range(2)]
            p.dma("sp", "const", lambda e: e.dma_start(out=g_v_bc, in_=T["g_v_bc"]), writes=["g_v_bc"])
            p.dma("sp", "const", lambda e: e.dma_start(out=b_s_bc, in_=T["b_s_bc"]), writes=["b_s_bc"])
            ntile = NT // 512
            items = []
            for tt in range(ntile):
                items += colblocks(T["Wb_in"], 2048, 4) + colblocks(T["Wb_in"], 0, 4) + colblocks(T["Wb_out"], 0, 4)
            ws = WStream("gw", wsl, items, ["W_in", "W_out"])
            XS = fchunk(T["xT"])
            sh1 = modT[:, 0, 0:16]
            gt1 = modT[:, 0, 32:48]
            for tt in range(ntile):
                tok0 = tt * 512
                p.dma("sp", "xt", lambda e, tok0=tok0: e.dma_start(out=xt, in_=XS[:, :, tok0:tok0 + 512]),
                      reads=[("X", "xT")], writes=["xt"])
                norm_mod(xt, 512, bufA, 7, rstd, tmps, bufB, gmul[:, 0, 0, :], sh1, "xt", "bufB", sqkey="bufA")
                base = tt * 12
                for vb in range(4):
                    wblk, wk = ws.use(base + vb)
                    for tcn in range(4):
                        u = vb * 4 + tcn
                        bk = u % 4
                        for kc in range(KC):
                            p.op("pe", lambda e, kc=kc, tcn=tcn, bk=bk, wblk=wblk: e.matmul(
                                ps[:, bk, :], lhsT=bufB[:, kc, tcn * 128:(tcn + 1) * 128], rhs=wblk[:, kc, :],
                                start=(kc == 0), stop=(kc == KC - 1)), reads=[wk, "bufB"], writes=[("ps", bk)])
                        vt = vts[u % 2]
                        vk = ("vt", u % 2)
                        p.op("act", lambda e, bk=bk, vt=vt: e.activation(out=vt, in_=ps[:, bk, :], func=AF.Gelu),
                             reads=[("ps", bk)], writes=[vk])
                        p.op("dve", lambda e, vt=vt, tcn=tcn, vb=vb: e.scalar_tensor_tensor(
                            out=junk, in0=vt, scalar=1.0, in1=vt, op0=ALU.mult, op1=ALU.mult,
                            accum_out=ssq[:, tcn * 4 + vb:tcn * 4 + vb + 1]), reads=[vk], writes=["junk", "ssq"])
                        p.op("act", lambda e, vt=vt, tcn=tcn, vb=vb: e.activation(out=vn[:, tcn, vb * 512:(vb + 1) * 512], in_=vt, func=AF.Copy),
                             reads=[vk], writes=[("vraw", tcn)])
                for tcn in range(4):
                    p.op("dve", lambda e, tcn=tcn: e.tensor_reduce(out=ssv[:, tcn:tcn + 1], in_=ssq[:, tcn * 4:(tcn + 1) * 4],
                                                                   axis=mybir.AxisListType.X, op=ALU.add),
                         reads=["ssq"], writes=["ssv"])
                p.op("act", lambda e: e.activation(out=ssv, in_=ssv, func=AF.Sqrt, bias=eps_t[:, 0:1], scale=1.0 / D),
                     reads=["ssv", "eps"], writes=["ssv"])
                p.op("dve", lambda e: e.reciprocal(out=ssv, in_=ssv), reads=["ssv"], writes=["ssv"])
                for tcn in range(4):
                    p.op("dve", lambda e, tcn=tcn: e.scalar_tensor_tensor(
                        out=vn[:, tcn, :], in0=vn[:, tcn, :], scalar=ssv[:, tcn:tcn + 1], in1=g_v_bc,
                        op0=ALU.mult, op1=ALU.mult), reads=[("vraw", tcn), "ssv", "g_v_bc"], writes=[("vn", tcn)])
                for ub in range(4):
                    wblk, wk = ws.use(base + 4 + ub)
                    for j in range(4):
                        uc = ub * 4 + j
                        bk = uc % 4
                        for kc in range(KC):
                            p.op("pe", lambda e, kc=kc, j=j, bk=bk, wblk=wblk: e.matmul(
                                ps[:, bk, :], lhsT=wblk[:, kc, j * 128:(j + 1) * 128], rhs=bufB[:, kc, :],
                                start=(kc == 0), stop=(kc == KC - 1)), reads=[wk, "bufB"], writes=[("ps", bk)])
                        p.op("act", lambda e, bk=bk, uc=uc: e.activation(out=bufA[:, uc, :], in_=ps[:, bk, :], func=AF.Gelu),
                             reads=[("ps", bk)], writes=["bufA"])
                for tcn in range(4):
                    for gq in range(4):
                        u = tcn * 4 + gq
                        bk = 4 + u % 3
                        for gi in range(4):
                            g = gq * 4 + gi
                            p.op("pe", lambda e, g=g, gi=gi, bk=bk, tcn=tcn: e.matmul(
                                ps[:, bk, gi * 128:(gi + 1) * 128], lhsT=vn[:, tcn, g * 128:(g + 1) * 128], rhs=w_sT[:, g, :],
                                start=True, stop=True), reads=[("vn", tcn), "w_sT"], writes=[("ps", bk)])
                        sv = svts[u % 2]
                        sk = ("svt", u % 2)
                        p.op("dve", lambda e, bk=bk, sv=sv, gq=gq: e.tensor_tensor(
                            out=sv, in0=ps[:, bk, :], in1=b_s_bc[:, gq * 512:(gq + 1) * 512], op=ALU.add),
                            reads=[("ps", bk), "b_s_bc"], writes=[sk])
                        p.op("dve", lambda e, sv=sv, gq=gq, tcn=tcn: e.tensor_tensor(
                            out=bufB[:, gq * 4:(gq + 1) * 4, tcn * 128:(tcn + 1) * 128],
                            in0=sv.rearrange("p (a b) -> p a b", a=4, b=128),
                            in1=bufA[:, gq * 4:(gq + 1) * 4, tcn * 128:(tcn + 1) * 128], op=ALU.mult),
                            reads=[sk, "bufA"], writes=["bufB"])
                proj_residual(bufB, "bufB", ws, base + 8, T["xT"], T["x1T"], gt1, tok0, 512, xr, ob, "x1", [0, 1, 2, 3])
            p.barrier()

        def phase_ffn(l, xsrc_name, xdst_name):
            A.off = PERSIST
            act = A.bf16([128, 2, NF, SUBW])
            act_off_end = A.off
            h2T = A.bf16([128, 2, 16, SUBW + 2])
            rstd = A.f32([128, 512])
            tmps = [A.f32([128, 512]) for _ in range(2)]
            tg = [A.f32([128, 2, SUBW]) for _ in range(2)]
            tv = [A.f32([128, 2, SUBW]) for _ in range(2)]
            wsl = [A.bf16([128, 2, 16, 256]) for _ in range(3)]
            xr = [A.f32([128, 2, SUBW]) for _ in range(2)]
            ob = [A.f32([128, 2, SUBW]) for _ in range(2)]
            save = A.off
            A.off = PERSIST
            xin = A.f32([128, 16, SUBW + 2])
            sq = A.bf16([128, 16, SUBW + 2])
            assert A.off <= act_off_end
            A.off = save
            XS = fchunk(T[xsrc_name])
            XD = fchunk(T[xdst_name])
            Wup = T["Wb_up%d" % l]
            Wdn = T["Wb_down%d" % l]
            sh2 = modT[:, l, 48:64]
            gt2 = modT[:, l, 80:96]
            gm2 = gmul[:, l, 1, :]
            nsup = NSUBT // 2
            items = []
            for s_ in range(nsup):
                for fb in range(NF // 2):
                    items.append(lambda slot, fb=fb: [(slot.rearrange("p a k c -> p (a k c)"), Wup[fb])])
                for dc in range(KC):
                    items.append(lambda slot, dc=dc: [(slot.rearrange("p a k c -> p (a k c)")[:, 0:NF * 128], Wdn[dc])])
            ws = WStream("fw", wsl, items, ["W_up%d" % l, "W_down%d" % l])
            per_sup = NF // 2 + KC
            unit = 0
            for sp_ in range(nsup):
                toks = []
                for sub in range(2):
                    s_ = sp_ * 2 + sub
                    tok0 = min(s_ * SUBW, NT - SUBW)
                    toks.append(tok0)
                    if s_ == 0:
                        p.dma("sp", "xin", lambda e: e.dma_start(out=xin[:, :, 1:SUBW + 2], in_=XS[:, :, 0:SUBW + 1]),
                              reads=[("X", xsrc_name)], writes=["actreg"])
                        p.op("dve", lambda e: e.tensor_copy(out=xin[:, :, 0:1], in_=halo_sb[:, :, 0:1]),
                             reads=["halo_sb"], writes=["actreg"])
                    elif s_ == NSUBT - 1:
                        p.dma("sp", "xin", lambda e, tok0=tok0: e.dma_start(out=xin[:, :, 0:SUBW + 1], in_=XS[:, :, tok0 - 1:NT]),
                              reads=[("X", xsrc_name)], writes=["actreg"])
                        p.op("dve", lambda e: e.tensor_copy(out=xin[:, :, SUBW + 1:SUBW + 2], in_=halo_sb[:, :, 1:2]),
                             reads=["halo_sb"], writes=["actreg"])
                    else:
                        p.dma("sp", "xin", lambda e, tok0=tok0: e.dma_start(out=xin, in_=XS[:, :, tok0 - 1:tok0 + SUBW + 1]),
                              reads=[("X", xsrc_name)], writes=["actreg"])
                    norm_mod(xin, SUBW + 2, sq, 7, rstd, tmps, h2T[:, sub, :, :], gm2, sh2, "actreg", ("h2T", sub), sqkey="actreg")
                    if s_ == 0:
                        p.op("dve", lambda e, sub=sub: e.tensor_scalar(out=h2T[:, sub, :, 0:1], in0=h2T[:, sub, :, 0:1],
                                                                        scalar1=hmask[:, 0:1], scalar2=None, op0=ALU.mult),
                             reads=[("h2T", sub), "c_hmask"], writes=[("h2T", sub)])
                    if s_ == NSUBT - 1:
                        p.op("dve", lambda e, sub=sub: e.tensor_scalar(out=h2T[:, sub, :, SUBW + 1:SUBW + 2],
                                                                        in0=h2T[:, sub, :, SUBW + 1:SUBW + 2],
                                                                        scalar1=hmask[:, 1:2], scalar2=None, op0=ALU.mult),
                             reads=[("h2T", sub), "c_hmask"], writes=[("h2T", sub)])
                base = sp_ * per_sup
                for f in range(NF):
                    wblk, wk = ws.use(base + f // 2)
                    fl = f % 2
                    for which in range(2):
                        slot = unit % 3
                        unit += 1
                        b0 = 2 * slot
                        for kc in range(KC):
                            for sub in range(2):
                                p.op("pe", lambda e, kc=kc, sub=sub, b0=b0, which=which, fl=fl, wblk=wblk: e.matmul(
                                    ps[:, b0 + sub, 0:SUBW + 2], lhsT=wblk[:, which, kc, fl * 128:(fl + 1) * 128],
                                    rhs=h2T[:, sub, kc, :], start=(kc == 0), stop=(kc == KC - 1)),
                                    reads=[wk, ("h2T", 0), ("h2T", 1)], writes=[("ps", b0), ("ps", b0 + 1)])
                        tb = (tg if which == 0 else tv)[f % 2]
                        tk = ("tg" if which == 0 else "tv", f % 2)
                        fc = which * NF + f
                        pk = [("ps", b0), ("ps", b0 + 1)]
                        p.op("act", lambda e, b0=b0, tb=tb, fc=fc: e.activation(
                            out=tb, in_=ps[:, b0:b0 + 2, 1:SUBW + 1], func=AF.Identity,
                            bias=conv_bT[:, l, fc:fc + 1], scale=conv_wT[:, l, 1, fc:fc + 1]),
                            reads=pk + ["c_conv_wT", "c_conv_bT"], writes=[tk])
                        p.op("dve", lambda e, b0=b0, tb=tb, fc=fc: e.scalar_tensor_tensor(
                            out=tb, in0=ps[:, b0:b0 + 2, 0:SUBW], scalar=conv_wT[:, l, 0, fc:fc + 1], in1=tb,
                            op0=ALU.mult, op1=ALU.add), reads=pk + [tk], writes=[tk])
                        p.op("dve", lambda e, b0=b0, tb=tb, fc=fc: e.scalar_tensor_tensor(
                            out=tb, in0=ps[:, b0:b0 + 2, 2:SUBW + 2], scalar=conv_wT[:, l, 2, fc:fc + 1], in1=tb,
                            op0=ALU.mult, op1=ALU.add), reads=pk + [tk], writes=[tk])
                        if which == 0:
                            p.op("act", lambda e, tb=tb: e.activation(out=tb, in_=tb, func=AF.Gelu), reads=[tk], writes=[tk])
                    p.op("dve", lambda e, f=f: e.tensor_tensor(out=act[:, :, f, :], in0=tg[f % 2], in1=tv[f % 2], op=ALU.mult),
                         reads=[("tg", f % 2), ("tv", f % 2)], writes=["actreg"])
                pend = []
                for dc in range(KC):
                    wblk, wk = ws.use(base + NF // 2 + dc)
                    wd = wblk.rearrange("p a k c -> p (a k c)")[:, 0:NF * 128].rearrange("p (f c) -> p f c", f=NF, c=128)
                    b0 = 6 if dc % 2 == 0 else 4
                    rs = dc % 2
                    for sub in range(2):
                        p.dma("sp", ("xr", rs), lambda e, dc=dc, rs=rs, sub=sub, tk_=toks[sub]: e.dma_start(
                            out=xr[rs][:, sub, :], in_=XS[:, dc, tk_:tk_ + SUBW]),
                            writes=[("xr", rs)])
                    while pend:
                        pend.pop(0)()
                    for f in range(NF):
                        for sub in range(2):
                            p.op("pe", lambda e, f=f, sub=sub, b0=b0, wd=wd: e.matmul(
                                ps[:, b0 + sub, 0:SUBW], lhsT=wd[:, f, :], rhs=act[:, sub, f, :],
                                start=(f == 0), stop=(f == NF - 1)), reads=[wk, "actreg"], writes=[("ps", b0), ("ps", b0 + 1)])
                    p.op("dve", lambda e, dc=dc, b0=b0, rs=rs: e.scalar_tensor_tensor(
                        out=ob[rs], in0=ps[:, b0:b0 + 2, 0:SUBW], scalar=gt2[:, dc:dc + 1], in1=xr[rs],
                        op0=ALU.mult, op1=ALU.add), reads=[("ps", b0), ("ps", b0 + 1), ("xr", rs), "modT"], writes=[("ob", rs)])
                    for sub in range(2):
                        pend.append(lambda dc=dc, rs=rs, sub=sub, tk_=toks[sub]: p.dma("sp", ("xo", rs), lambda e: e.dma_start(
                            out=XD[:, dc, tk_:tk_ + SUBW], in_=ob[rs][:, sub, :]), reads=[("ob", rs)]))
                while pend:
                    pend.pop(0)()
            p.barrier()

        def phase_qkv():
            A.off = PERSIST
            xt = A.f32([128, 16, 512])
            sq = A.bf16([128, 16, 512])
            hT = A.bf16([128, 16, 512])
            rstd = A.f32([128, 512])
            tmps = [A.f32([128, 512]) for _ in range(2)]
            cs = A.f32([128, 512])
            sn = A.f32([128, 512])
            t1 = [A.f32([128, 512]) for _ in range(3)]
            t2 = [A.f32([128, 512]) for _ in range(3)]
            sqh = [A.bf16([128, 512]) for _ in range(3)]
            rs_ = [A.f32([128, 512]) for _ in range(3)]
            qr = [A.bf16([128, 512]) for _ in range(3)]
            vb_ = [A.bf16([128, 512]) for _ in range(2)]
            wsl = [A.bf16([128, 2, 16, 256]) for _ in range(3)]
            ntile = NT // 512
            Vv = fchunk(T["Wb_v"])
            Vqk = fchunk(T["Wb_qk"])
            rp_f = A.f32([128, 128])
            rp_b = A.bf16([128, 128])
            qb = [A.bf16([128, 512]) for _ in range(3)]
            p.dma("sp", "const", lambda e: e.dma_start(out=rp_f, in_=T["rperm"]), writes=["rp_f"])
            p.op("dve", lambda e: e.tensor_copy(out=rp_b, in_=rp_f), reads=["rp_f"], writes=["rp_b"])
            items = []
            for tt in range(ntile):
                items.append(lambda slot: [(slot.rearrange("p a k c -> p (a k c)").rearrange("p (k c) -> p k c", k=16, c=512), Vv[:, :, :])])
                for hb in range(10):
                    items.append(lambda slot, hb=hb: [(slot[:, 0, :, :], Vqk[:, :, hb * 256:(hb + 1) * 256])])
            ws = WStream("qw", wsl, items, ["W_v", "W_qk"])
            XS = fchunk(T["x2T"])
            sh1 = modT[:, 1, 0:16]
            KV5 = KVin_bf.rearrange("(kv h c) p f -> kv h c p f", kv=2, h=4, c=16)

            def kv_collectives(tt):
                for h in range(4):
                    for j in range(2):
                        for kv in range(2):
                            c = kv * 64 + h * 16 + 2 * tt + j
                            p.dma("pool", "cc_kv", lambda e, c=c: e.collective_compute(
                                "AllGather", ALU.bypass, replica_groups=GROUPS, ins=[T["KVin"][c]], outs=[T["KVout"][c]]),
                                reads=[("KVin", c)], writes=["KVout"], inc=1)
            hcnt = 0
            for tt in range(ntile):
                tok0 = tt * 512
                p.dma("sp", "xt", lambda e, tok0=tok0: e.dma_start(out=xt, in_=XS[:, :, tok0:tok0 + 512]),
                      reads=[("X", "x2T")], writes=["xt"])
                p.dma("sp", "cs", lambda e, tok0=tok0: e.dma_start(out=cs, in_=T["cosT"][:, tok0:tok0 + 512]), writes=["cs"])
                p.dma("sp", "cs", lambda e, tok0=tok0: e.dma_start(out=sn, in_=T["sinT"][:, tok0:tok0 + 512]), writes=["sn"])
                norm_mod(xt, 512, sq, 7, rstd, tmps, hT, gmul[:, 1, 0, :], sh1, "xt", "hT")
                base = tt * 11
                wblk, wk = ws.use(base)
                wv = wblk.rearrange("p a k c -> p (a k c)").rearrange("p (k c) -> p k c", k=16, c=512)
                for tcn in range(4):
                    bk = 4 + tcn % 2
                    for kc in range(KC):
                        p.op("pe", lambda e, kc=kc, tcn=tcn, bk=bk, wv=wv: e.matmul(
                            ps[:, bk, :], lhsT=hT[:, kc, tcn * 128:(tcn + 1) * 128], rhs=wv[:, kc, :],
                            start=(kc == 0), stop=(kc == KC - 1)), reads=[wk, "hT"], writes=[("ps", bk)])
                    vs = tcn % 2
                    p.op("act", lambda e, bk=bk, vs=vs: e.activation(out=vb_[vs], in_=ps[:, bk, :], func=AF.Copy),
                         reads=[("ps", bk)], writes=[("vb", vs)])
                    vc = tt * 2 + tcn // 2
                    off = (tcn % 2) * 128
                    p.dma("sp", ("vst", vs), lambda e, vs=vs, vc=vc, off=off: e.dma_start(
                        out=KV5[1, :, vc, :, off:off + 128].rearrange("h p d -> p h d"),
                        in_=vb_[vs].rearrange("p (h d) -> p h d", h=4, d=128)),
                        reads=[("vb", vs)], writes=[("KVin", 64 + h_ * 16 + vc) for h_ in range(4)])
                for hb in range(10):
                    wblk, wk = ws.use(base + 1 + hb)
                    for hh in range(2):
                        head = hb * 2 + hh
                        hs = hcnt % 3
                        hcnt += 1
                        bq = 0 + 2 * hs
                        bp = 1 + 2 * hs
                        for kc in range(KC):
                            p.op("pe", lambda e, kc=kc, hh=hh, bq=bq, wblk=wblk: e.matmul(
                                ps[:, bq, :], lhsT=wblk[:, 0, kc, hh * 128:(hh + 1) * 128], rhs=hT[:, kc, :],
                                start=(kc == 0), stop=(kc == KC - 1)), reads=[wk, "hT"], writes=[("ps", bq)])
                        p.op("act", lambda e, bq=bq, hs=hs: e.activation(out=qb[hs], in_=ps[:, bq, :], func=AF.Copy),
                             reads=[("ps", bq)], writes=[("qb", hs)])
                        p.op("pe", lambda e, bp=bp, hs=hs: e.matmul(ps[:, bp, :], lhsT=rp_b, rhs=qb[hs], start=True, stop=True),
                             reads=[("qb", hs), "rp_b"], writes=[("ps", bp)])
                        p.op("act", lambda e, bq=bq, hs=hs: e.activation(out=sqh[hs], in_=ps[:, bq, :], func=AF.Square),
                             reads=[("ps", bq)], writes=[("sqh", hs)])
                        bs = 6
                        p.op("pe", lambda e, hs=hs, bs=bs: e.matmul(ps[:, bs, :], lhsT=ones_bf, rhs=sqh[hs], start=True, stop=True),
                             reads=[("sqh", hs), "ones_bf"], writes=[("ps", bs)])
                        p.op("act", lambda e, hs=hs, bs=bs: e.activation(out=rs_[hs], in_=ps[:, bs, :], func=AF.Sqrt,
                                                                          bias=eps_t[:, 0:1], scale=1.0 / 128),
                             reads=[("ps", bs), "eps"], writes=[("rs", hs)])
                        p.op("dve", lambda e, hs=hs: e.reciprocal(out=rs_[hs], in_=rs_[hs]), reads=[("rs", hs)], writes=[("rs", hs)])
                        gc = 0 if head < 16 else 2
                        p.op("dve", lambda e, hs=hs, bq=bq, gc=gc: e.scalar_tensor_tensor(
                            out=t1[hs], in0=ps[:, bq, :], scalar=gqk[:, gc:gc + 1], in1=cs, op0=ALU.mult, op1=ALU.mult),
                            reads=[("ps", bq), "c_gqk", "cs"], writes=[("t1", hs)])
                        p.op("dve", lambda e, hs=hs, bp=bp, gc=gc: e.scalar_tensor_tensor(
                            out=t2[hs], in0=ps[:, bp, :], scalar=gqk[:, gc + 1:gc + 2], in1=sn, op0=ALU.mult, op1=ALU.mult),
                            reads=[("ps", bp), "c_gqk", "sn"], writes=[("t2", hs)])
                        p.op("dve", lambda e, hs=hs: e.tensor_tensor(out=t1[hs], in0=t1[hs], in1=t2[hs], op=ALU.add),
                             reads=[("t1", hs), ("t2", hs)], writes=[("t1", hs)])
                        p.op("dve", lambda e, hs=hs: e.tensor_tensor(out=qr[hs], in0=t1[hs], in1=rs_[hs], op=ALU.mult),
                             reads=[("t1", hs), ("rs", hs)], writes=[("qr", hs)])
                        if head < 16:
                            p.dma("sp", ("qst", hs), lambda e, hs=hs, head=head, tok0=tok0: e.dma_start(
                                out=T["QT"][head, :, tok0:tok0 + 512], in_=qr[hs]), reads=[("qr", hs)])
                        else:
                            c0 = (head - 16) * 16 + 2 * tt
                            p.dma("sp", ("qst", hs), lambda e, hs=hs, c0=c0: e.dma_start(
                                out=KVin_bf[c0:c0 + 2].rearrange("c d t -> d c t"),
                                in_=qr[hs].rearrange("p (c t) -> p c t", c=2, t=256)),
                                reads=[("qr", hs)], writes=[("KVin", c0), ("KVin", c0 + 1)])
                if tt >= 1:
                    kv_collectives(tt - 1)
            kv_collectives(ntile - 1)
            p.barrier()

        def phase_attn():
            A.off = PERSIST
            KTs = [A.bf16([128, 4, NT]) for _ in range(2)]
            Vs = [A.bf16([128, 4, 32, 128]) for _ in range(2)]
            Qs = [A.bf16([128, 4, 128]) for _ in range(3)]
            Ps = [A.bf16([128, 2, 512]) for _ in range(4)]
            acc2 = [A.f32([128, 2, 512]) for _ in range(2)]
            accs = [A.f32([128, 512]) for _ in range(2)]
            rinv = [A.f32([128, 512]) for _ in range(2)]
            osb = [A.bf16([128, 4, 128]) for _ in range(2)]
            scale = 1.0 / math.sqrt(128.0)
            def load_kv(h):
                s = h % 2
                for r in range(4):
                    p.dma("sp", ("kv", s), lambda e, h=h, s=s, r=r: e.dma_start(
                        out=KTs[s][:, r, :].rearrange("p (c t) -> p c t", c=16, t=256),
                        in_=KVout_bf[h * 16:(h + 1) * 16, r * 128:(r + 1) * 128, :].rearrange("c d t -> d c t")),
                        reads=["KVout"], writes=[("K", s)])
                    p.dma("sp", ("kv", s), lambda e, h=h, s=s, r=r: e.dma_start(
                        out=Vs[s][:, r, :, :].rearrange("p k d -> p (k d)").rearrange("p (c f) -> p c f", c=16, f=256),
                        in_=KVout_bf[64 + h * 16:64 + (h + 1) * 16, r * 128:(r + 1) * 128, :].rearrange("c q f -> q c f")),
                        reads=["KVout"], writes=[("V", s)])

            nq = NT // 128
            qi_all = [(h, qt) for h in range(4) for qt in range(nq)]

            def load_q(i):
                h, qt = qi_all[i]
                s = i % 3
                p.dma("sp", ("q", s), lambda e, h=h, qt=qt, s=s: e.dma_start(
                    out=Qs[s], in_=T["QT"][4 * h:4 * h + 4, :, qt * 128:(qt + 1) * 128].rearrange("g d q -> d g q")),
                    reads=["QT"], writes=[("Q", s)])

            load_kv(0)
            load_q(0)
            load_q(1)
            npair = 64
            slot_ctr = [0]
            pairs = [(i, kp) for i in range(len(qi_all)) for kp in range(npair)]
            sslot = {}
            NSL = 2
            ACCK = [("ps", 6), ("ps", 7)]

            def emit_S(n):
                i, kp = pairs[n]
                h, qt = qi_all[i]
                s = h % 2
                qs = i % 3
                if kp == 0:
                    if qt == 0 and h + 1 < 4:
                        load_kv(h + 1)
                    if i + 2 < len(qi_all):
                        load_q(i + 2)
                Kt = KTs[s].rearrange("p r t -> p (r t)")
                Qt = Qs[qs].rearrange("p g q -> p (g q)")
                sb = slot_ctr[0] % NSL
                slot_ctr[0] += 1
                sslot[n] = sb
                for j in range(2):
                    ktile = 2 * kp + j
                    p.op("pe", lambda e, sb=sb, j=j, ktile=ktile, Kt=Kt, Qt=Qt: e.matmul(
                        ps[:, 2 * sb + j, :], lhsT=Kt[:, ktile * 128:(ktile + 1) * 128], rhs=Qt, start=True, stop=True),
                        reads=[("K", s), ("Q", qs)], writes=[("ps", 2 * sb), ("ps", 2 * sb + 1)])

            def emit_rest(n):
                i, kp = pairs[n]
                h, qt = qi_all[i]
                s = h % 2
                Vt = Vs[s].rearrange("p r k d -> p (r k) d")
                ob_ = 4 + i % 2
                sb = sslot.pop(n)
                pb = n % 4
                p.op("act", lambda e, sb=sb, pb=pb: e.activation(
                    out=Ps[pb], in_=ps[:, 2 * sb:2 * sb + 2, :], func=AF.Exp, scale=scale),
                    reads=[("ps", 2 * sb), ("ps", 2 * sb + 1)], writes=[("P", pb)])
                a2p = acc2[i % 2]
                apk = ("acc2p", i % 2)
                if kp % 4 == 1:
                    if kp == 1:
                        p.op("pool", lambda e, pb=pb, a2p=a2p: e.tensor_copy(out=a2p, in_=Ps[pb]), reads=[("P", pb)], writes=[apk])
                    else:
                        p.op("pool", lambda e, pb=pb, a2p=a2p: e.tensor_tensor(out=a2p, in0=a2p, in1=Ps[pb], op=ALU.add),
                             reads=[("P", pb), apk], writes=[apk])
                elif kp == 0:
                    p.op("dve", lambda e, pb=pb: e.tensor_copy(out=ps[:, 6:8, :], in_=Ps[pb]), reads=[("P", pb)], writes=ACCK)
                else:
                    p.op("dve", lambda e, pb=pb: e.tensor_tensor(out=ps[:, 6:8, :], in0=ps[:, 6:8, :], in1=Ps[pb], op=ALU.add),
                         reads=[("P", pb)] + ACCK, writes=ACCK)
                if kp == npair - 1:
                    r_ = i % 2
                    p.op("dve", lambda e, r_=r_: e.tensor_copy(out=rinv[r_], in_=ps[:, 6, :]), reads=ACCK, writes=[("rinv", r_)])
                    p.op("dve", lambda e, r_=r_: e.tensor_tensor(out=accs[r_], in0=ps[:, 7, :], in1=rinv[r_], op=ALU.add),
                         reads=ACCK + [("rinv", r_)], writes=[("accs", r_)])
                    p.op("dve", lambda e, r_=r_, a2p=a2p: e.tensor_tensor(out=accs[r_], in0=accs[r_], in1=a2p[:, 0, :], op=ALU.add),
                         reads=[apk, ("accs", r_)], writes=[("accs", r_)])
                    p.op("dve", lambda e, r_=r_, a2p=a2p: e.tensor_tensor(out=accs[r_], in0=accs[r_], in1=a2p[:, 1, :], op=ALU.add),
                         reads=[apk, ("accs", r_)], writes=[("accs", r_)])

            def emit_PV(n):
                i, kp = pairs[n]
                h, qt = qi_all[i]
                s = h % 2
                Vt = Vs[s].rearrange("p r k d -> p (r k) d")
                ob_ = 4 + i % 2
                pb = n % 4
                for j in range(2):
                    ktile = 2 * kp + j
                    p.op("pe", lambda e, pb=pb, j=j, ktile=ktile, Vt=Vt, ob_=ob_, kp=kp: e.matmul(
                        ps[:, ob_, :], lhsT=Vt[:, ktile, :], rhs=Ps[pb][:, j, :],
                        start=(kp == 0 and j == 0), stop=(kp == npair - 1 and j == 1)),
                        reads=[("V", s), ("P", pb)], writes=[("ps", ob_)])

            def emit_final(i):
                h, qt = qi_all[i]
                r_ = i % 2
                ob_ = 4 + i % 2
                sbk = 2 * (slot_ctr[0] % NSL)
                p.op("pe", lambda e, r_=r_, sbk=sbk: e.matmul(ps[:, sbk, :], lhsT=ones_f, rhs=accs[r_], start=True, stop=True),
                     reads=[("accs", r_), "ones_f"], writes=[("ps", sbk), ("ps", sbk + 1)])
                p.op("dve", lambda e, r_=r_, sbk=sbk: e.reciprocal(out=rinv[r_], in_=ps[:, sbk, :]),
                     reads=[("ps", sbk), ("ps", sbk + 1)], writes=[("rinv", r_)])
                p.op("dve", lambda e, r_=r_, ob_=ob_: e.tensor_tensor(
                    out=osb[r_].rearrange("p g q -> p (g q)"), in0=ps[:, ob_, :], in1=rinv[r_], op=ALU.mult),
                    reads=[("ps", ob_), ("rinv", r_)], writes=[("osb", r_)])
                p.dma("sp", ("ost", r_), lambda e, r_=r_, h=h, qt=qt: e.dma_start(
                    out=T["OT"][4 * h:4 * h + 4, :, qt * 128:(qt + 1) * 128].rearrange("g d q -> d g q"), in_=osb[r_]),
                    reads=[("osb", r_)])

            emit_S(0)
            pending_final = None
            NP_ = len(pairs)
            for n in range(NP_ + 1):
                if n + 1 < NP_:
                    emit_S(n + 1)
                if n < NP_:
                    emit_rest(n)
                if pending_final is not None:
                    emit_final(pending_final)
                    pending_final = None
                if n >= 1:
                    emit_PV(n - 1)
                    if pairs[n - 1][1] == npair - 1:
                        pending_final = pairs[n - 1][0]
            if pending_final is not None:
                emit_final(pending_final)
            p.barrier()

        def phase_wo():
            A.off = PERSIST
            oT = [A.bf16([128, 16, 512]) for _ in range(2)]
            wsl = [A.bf16([128, 16, 512]) for _ in range(3)]
            xr = [A.f32([128, 512]) for _ in range(2)]
            ob = [A.f32([128, 512]) for _ in range(2)]
            ntile = NT // 512
            items = []
            for tt in range(ntile):
                items += colblocks(T["Wb_o"], 0, 4)
            ws = WStream("ow", wsl, items, ["W_o"])
            gt1 = modT[:, 1, 32:48]
            OTv = T["OT"].rearrange("h d t -> d h t")
            for tt in range(ntile):
                tok0 = tt * 512
                s = tt % 2
                p.dma("sp", ("oT", s), lambda e, s=s, tok0=tok0: e.dma_start(out=oT[s], in_=OTv[:, :, tok0:tok0 + 512]),
                      reads=["OT"], writes=[("oT", s)])
                proj_residual(oT[s], ("oT", s), ws, tt * 4, T["x2T"], T["x3T"], gt1, tok0, 512, xr, ob, "x3", [0, 1, 2, 3])
            p.barrier()

        def phase_final():
            A.off = PERSIST
            xt = [A.f32([128, 16, 512]) for _ in range(2)]
            sq = A.bf16([128, 16, 512])
            rstd = A.f32([128, 512])
            yo = [A.f32([128, 16, 512]) for _ in range(2)]
            XS = fchunk(T["x4T"])
            XD = fchunk(T["outT"])
            ntile = NT // 512
            for tt in range(ntile):
                tok0 = tt * 512
                s = tt % 2
                p.dma("sp", ("fx", s), lambda e, s=s, tok0=tok0: e.dma_start(out=xt[s], in_=XS[:, :, tok0:tok0 + 512]),
                      reads=[("X", "x4T")], writes=[("fxt", s)])
                norm_mod(xt[s], 512, sq, 7, rstd, None, None, g_finalT, None, ("fxt", s), ("fyo", s), plain_out=yo[s])
                p.dma("sp", ("fo", s), lambda e, s=s, tok0=tok0: e.dma_start(out=XD[:, :, tok0:tok0 + 512], in_=yo[s]),
                      reads=[("fyo", s)])
            p.barrier()

        def phase_halo(l, xname):
            A.off = PERSIST
            eo = A.f32([128, 4, 16, 2])
            X = fchunk(T[xname])
            Ein = T["E_in%d" % l]
            Eout = T["E_out%d" % l]
            Ein3 = Ein.rearrange("p (k e) -> p k e", e=2)
            p.dma("sp", "ein", lambda e: e.dma_start(out=Ein3[:, :, 0:1], in_=X[:, :, 0:1], allow_slow_non_contiguous=True),
                  reads=[("X", xname)], writes=["E_in"])
            p.dma("sp", "ein", lambda e: e.dma_start(out=Ein3[:, :, 1:2], in_=X[:, :, NT - 1:NT], allow_slow_non_contiguous=True),
                  reads=[("X", xname)], writes=["E_in"])
            p.dma("pool", "cc_halo", lambda e: e.collective_compute(
                "AllGather", ALU.bypass, replica_groups=GROUPS, ins=[Ein], outs=[Eout]),
                reads=["E_in"], writes=["E_out"], inc=1)
            p.dma("sp", "eo", lambda e: e.dma_start(out=eo, in_=Eout.rearrange("(r p) (k e) -> p r k e", p=128, e=2)),
                  reads=["E_out"], writes=["eo"])
            for side in range(2):
                src_e = 1 - side
                for r in range(4):
                    sc = sel[:, side * 4 + r:side * 4 + r + 1]
                    if r == 0:
                        p.op("dve", lambda e, side=side, src_e=src_e, sc=sc: e.tensor_scalar(
                            out=halo_sb[:, :, side], in0=eo[:, 0, :, src_e], scalar1=sc, scalar2=None, op0=ALU.mult),
                            reads=["eo", "c_sel"], writes=["halo_sb"])
                    else:
                        p.op("dve", lambda e, side=side, src_e=src_e, sc=sc, r=r: e.scalar_tensor_tensor(
                            out=halo_sb[:, :, side], in0=eo[:, r, :, src_e], scalar=sc, in1=halo_sb[:, :, side],
                            op0=ALU.mult, op1=ALU.add), reads=["eo", "c_sel", "halo_sb"], writes=["halo_sb"])
            p.barrier()

        p.barrier()
        phase_ada_all()
        convert_early()
        phase_gmlp()
        phase_halo(0, "x1T")
        convert_late()
        phase_ffn(0, "x1T", "x2T")
        phase_qkv()
        phase_attn()
        phase_wo()
        phase_halo(1, "x3T")
        phase_ffn(1, "x3T", "x4T")
        phase_final()
        final_groups = [g for g in p.group_counts if g not in p.bg_groups]
        p.emit(final_groups=final_groups)
    return nc, T


def _rope_tables(t0):
    S = 16384
    s = np.arange(t0, t0 + NT)
    row = (s // 64).astype(np.float32)
    col = (s % 64).astype(np.float32)
    inv = (10000.0 ** (-np.arange(0, 64, 2, dtype=np.float32) / 64.0)).astype(np.float32)
    ang_r = (row[:, None] * inv[None, :]).astype(np.float32)
    ang_c = (col[:, None] * inv[None, :]).astype(np.float32)
    cr, sr, cc, sc = np.cos(ang_r), np.sin(ang_r), np.cos(ang_c), np.sin(ang_c)
    cosT = np.concatenate([cr, cr, cc, cc], axis=1).T
    sinT = np.concatenate([-sr, sr, -sc, sc], axis=1).T
    return np.ascontiguousarray(cosT, dtype=np.float32), np.ascontiguousarray(sinT, dtype=np.float32)


def _partner_idx(n):
    d = np.arange(n)
    w = d % 64
    return np.where(w < 32, d + 32, d - 32)


_PROG_CACHE = {}


def _get_prog():
    if "p" not in _PROG_CACHE:
        _PROG_CACHE["p"] = build_program()
    return _PROG_CACHE["p"]


def _prep(x, c, w_ada, b_ada, g_norm, g_final, a_w_in, a_g_v, a_w_s, a_b_s, a_w_out,
          b_w_qkv, b_g_q, b_g_k, b_w_o, f_w_up, f_conv_w, f_conv_b, f_w_down):
    f = lambda a: np.ascontiguousarray(np.asarray(a), dtype=np.float32)
    x, c, w_ada, b_ada, g_norm, g_final = f(x), f(c), f(w_ada), f(b_ada), f(g_norm), f(g_final)
    a_w_in, a_g_v, a_w_s, a_b_s, a_w_out = f(a_w_in), f(a_g_v), f(a_w_s), f(a_b_s), f(a_w_out)
    b_w_qkv, b_g_q, b_g_k, b_w_o = f(b_w_qkv), f(b_g_q), f(b_g_k), f(b_w_o)
    f_w_up, f_conv_w, f_conv_b, f_w_down = f(f_w_up), f(f_conv_w), f(f_conv_b), f(f_w_down)

    shared = {
        "g_normT": f(g_norm.reshape(2, 2, 16, 128).transpose(3, 0, 1, 2)),
        "conv_wT": f(f_conv_w.reshape(2, 3, 88, 128).transpose(3, 0, 1, 2)),
        "conv_bT": f(f_conv_b.reshape(2, 88, 128).transpose(2, 0, 1)),
        "g_finalT": f(g_final.reshape(16, 128).T),
        "a_w_in": a_w_in[0], "a_w_out": a_w_out[0],
        "w_sT": f(a_w_s[0].transpose(2, 0, 1).reshape(128, 2048)),
        "g_v_bc": f(np.broadcast_to(a_g_v[0][None, :], (128, 2048))),
        "b_s_bc": f(np.broadcast_to(a_b_s[0].reshape(1, 2048), (128, 2048))),
        "f_w_up0": f_w_up[0], "f_w_up1": f_w_up[1], "f_w_down0": f_w_down[0], "f_w_down1": f_w_down[1],
        "b_w_o": b_w_o[0],
    }
    wqk = f(b_w_qkv[0][:, 0:2560])
    pidx = _partner_idx(2560)
    shared["w_qk"] = wqk
    rp = np.zeros((128, 128), np.float32)
    pp = _partner_idx(128)
    rp[pp, np.arange(128)] = 1.0
    shared["rperm"] = rp
    shared["w_v"] = f(b_w_qkv[0][:, 2560:3072])
    p128 = _partner_idx(128)
    shared["gqk"] = f(np.stack([b_g_q[0], b_g_q[0][p128], b_g_k[0], b_g_k[0][p128]], axis=1))

    cores = []
    for ci in range(8):
        b, r = ci // 4, ci % 4
        t0 = r * NT
        m = dict(shared)
        m["xT"] = f(x[b, t0:t0 + NT, :].T)
        m["cT"] = f(c[b].reshape(16, 128).T)
        cs, sn = _rope_tables(t0)
        m["cosT"], m["sinT"] = cs, sn
        m["hmask"] = f(np.broadcast_to(np.array([[0.0 if r == 0 else 1.0, 0.0 if r == 3 else 1.0]], np.float32), (128, 2)))
        sl = np.zeros((128, 8), np.float32)
        if r > 0:
            sl[:, r - 1] = 1.0
        if r < 3:
            sl[:, 4 + r + 1] = 1.0
        m["sel"] = sl
        m["w_adaS"] = f(w_ada[:, :, r * 3072:(r + 1) * 3072])
        m["b_adaS"] = f(b_ada[:, r * 3072:(r + 1) * 3072].reshape(2, 24, 128).transpose(2, 0, 1).reshape(128, 48))
        cores.append(m)

    return cores


IN_NAMES = ["cT", "g_normT", "conv_wT", "conv_bT", "g_finalT", "gqk", "hmask", "sel", "w_adaS", "b_adaS", "xT",
            "a_w_in", "a_w_out", "w_sT", "g_v_bc", "b_s_bc", "f_w_up0", "f_w_down0", "f_w_up1", "f_w_down1",
            "w_qk", "rperm", "w_v", "cosT", "sinT", "b_w_o"]


def kernel(**inputs):
    cores = _prep(**inputs)
    nc, T = _get_prog()
    maps = [{n: m[n] for n in IN_NAMES} for m in cores]
    res = run_bass_kernel_spmd(nc, maps, core_ids=list(range(8)))
    out = np.empty((2, 16384, D), np.float32)
    for ci in range(8):
        b, r = ci // 4, ci % 4
        out[b, r * NT:(r + 1) * NT, :] = res.results[ci]["outT"].T
    return out
```

```python
import contextlib
import math
import numpy as np
import ml_dtypes
import concourse.bass as bass
import concourse.mybir as mybir
from concourse.bass_utils import run_bass_kernel_spmd

F32 = mybir.dt.float32
BF16 = mybir.dt.bfloat16
AF = mybir.ActivationFunctionType
ALU = mybir.AluOpType

D = 2048
KC = 16
NT = 4096
DFF = 5632
NF = 44
EPS = 1e-6
SUBW = 410
NSUBT = 10
COMPUTE = ("pe", "act", "dve", "pool")


class Op:
    __slots__ = ("eng", "fn", "raw", "oth", "flag", "idx", "group", "gcount", "is_dma", "inc")

    def __init__(self, eng, fn, is_dma, group):
        self.eng = eng
        self.fn = fn
        self.raw = set()
        self.oth = set()
        self.flag = False
        self.idx = 0
        self.group = group
        self.gcount = 0
        self.is_dma = is_dma
        self.inc = 16


def _ekey(o):
    return ("dma", o.group) if o.is_dma else o.eng


class Prog:
    def __init__(self, nc):
        self.nc = nc
        self.ops = []
        self.last_writer = {}
        self.readers = {}
        self.group_counts = {}
        self.group_last = {}
        self.last_on_eng = {}
        self.pending_barrier = {}
        self.bg_groups = set()
        self.group_inc = {}

    def _add(self, op, reads, writes):
        for r in reads:
            w = self.last_writer.get(r)
            if w is not None:
                op.raw.add(w)
        for wkey in writes:
            w = self.last_writer.get(wkey)
            if w is not None:
                op.oth.add(w)
            for rd in self.readers.get(wkey, {}).values():
                op.oth.add(rd)
        pb = self.pending_barrier.pop(op.eng, None)
        if pb:
            op.raw.update(pb)
        op.raw.discard(op)
        op.oth.discard(op)
        op.oth -= op.raw
        for r in reads:
            self.readers.setdefault(r, {})[_ekey(op)] = op
        for wkey in writes:
            self.last_writer[wkey] = op
            self.readers[wkey] = {}
        self.ops.append(op)
        self.last_on_eng[op.eng] = op
        return op

    def op(self, eng, fn, reads=(), writes=()):
        return self._add(Op(eng, fn, False, None), reads, writes)

    def dma(self, queue, group, fn, reads=(), writes=(), inc=16):
        o = Op(queue, fn, True, group)
        o.inc = inc
        self.group_inc[group] = inc
        self.group_counts[group] = self.group_counts.get(group, 0) + 1
        o.gcount = self.group_counts[group]
        self.group_last[group] = o
        return self._add(o, reads, writes)

    def barrier(self):
        lasts = set()
        for e, o in self.last_on_eng.items():
            if not o.is_dma:
                lasts.add(o)
        for g, o in self.group_last.items():
            if g not in self.bg_groups:
                lasts.add(o)
        for e in ("pe", "act", "dve", "pool", "sp"):
            self.pending_barrier[e] = set(lasts)

    def emit(self, final_groups=()):
        nc = self.nc
        ops = self.ops

        def needed(o, d, is_raw):
            if d.is_dma or o.is_dma:
                return True
            if d.eng != o.eng:
                return True
            if o.eng == "pe":
                return False
            return is_raw

        for o in ops:
            for d in o.raw:
                if needed(o, d, True) and not d.is_dma:
                    d.flag = True
            for d in o.oth:
                if needed(o, d, False) and not d.is_dma:
                    d.flag = True
        counts = {e: 0 for e in COMPUTE}
        for o in ops:
            if not o.is_dma and o.flag:
                counts[o.eng] += 1
                o.idx = counts[o.eng]
        groups = sorted(self.group_counts, key=str)
        with contextlib.ExitStack() as st:
            sem_eng = {e: st.enter_context(nc.semaphore("m_" + e)) for e in COMPUTE}
            sem_grp = {g: st.enter_context(nc.semaphore("g%d" % i)) for i, g in enumerate(groups)}
            block = st.enter_context(nc.Block())
            per_eng = {"sp": []}
            for o in ops:
                per_eng.setdefault(o.eng, []).append(o)

            def make(engname, lst):
                def body(eng):
                    known = {}
                    for o in lst:
                        waits = {}
                        for is_raw, ds in ((True, o.raw), (False, o.oth)):
                            for d in ds:
                                if not needed(o, d, is_raw):
                                    continue
                                if d.is_dma:
                                    key = ("g", d.group)
                                    val = d.inc * d.gcount
                                else:
                                    key = ("e", d.eng)
                                    val = d.idx
                                if known.get(key, 0) >= val:
                                    continue
                                if waits.get(key, 0) < val:
                                    waits[key] = val
                        for key, val in waits.items():
                            known[key] = val
                            sem = sem_grp[key[1]] if key[0] == "g" else sem_eng[key[1]]
                            eng.wait_ge(sem, val)
                        ins = o.fn(eng)
                        if o.is_dma:
                            ins.then_inc(sem_grp[o.group], o.inc)
                        elif o.flag:
                            ins.then_inc(sem_eng[o.eng], 1)
                    if engname == "sp":
                        for g in final_groups:
                            eng.wait_ge(sem_grp[g], self.group_inc[g] * self.group_counts[g])
                return body

            attr = {"pe": "tensor", "act": "scalar", "dve": "vector", "pool": "gpsimd", "sp": "sync"}
            for e, lst in per_eng.items():
                getattr(block, attr[e])(make(e, lst))


class Arena:
    def __init__(self, ap, nwords):
        self.ap = ap
        self.n = nwords
        self.off = 0

    def _view(self, v, shape):
        if len(shape) == 2:
            return v
        if len(shape) == 3:
            return v.rearrange("p (a b) -> p a b", a=shape[1], b=shape[2])
        if len(shape) == 4:
            return v.rearrange("p (a b c) -> p a b c", a=shape[1], b=shape[2], c=shape[3])
        raise ValueError(shape)

    def f32(self, shape):
        n = int(np.prod(shape[1:]))
        assert self.off + n <= self.n, ("arena overflow", self.off, n)
        v = self.ap[:, self.off:self.off + n]
        self.off += n
        return self._view(v, shape)

    def bf16(self, shape):
        n = int(np.prod(shape[1:]))
        w = (n + 1) // 2
        assert self.off + w <= self.n, ("arena overflow", self.off, w)
        v = self.ap[:, self.off:self.off + w].bitcast(BF16)[:, 0:n]
        self.off += w
        return self._view(v, shape)


def fchunk(X):
    return X.rearrange("(kc p) t -> p kc t", p=128)


def build_program():
    nc = bass.Bass("TRN2", target_bir_lowering=False)
    p = Prog(nc)
    T = {}
    GROUPS = [[0, 1, 2, 3], [4, 5, 6, 7]]

    def dram(name, shape, dtype, kind):
        if kind == "cc":
            T[name] = nc.dram_tensor(name, list(shape), dtype).ap()
        else:
            k = {"in": "ExternalInput", "out": "ExternalOutput", "tmp": "Internal"}[kind]
            T[name] = nc.dram_tensor(name, list(shape), dtype, kind=k).ap()
        return T[name]

    layers = [0, 1]
    for nm, shp in (("cT", [128, 16]), ("g_normT", [128, 2, 2, 16]), ("conv_wT", [128, 2, 3, 88]),
                    ("conv_bT", [128, 2, 88]), ("g_finalT", [128, 16]), ("gqk", [128, 4]), ("hmask", [128, 2]), ("sel", [128, 8]),
                    ("w_adaS", [2, D, 3072]), ("b_adaS", [128, 48]), ("xT", [D, NT]), ("a_w_in", [D, 2 * D]), ("a_w_out", [D, D]),
                    ("w_sT", [128, D]), ("g_v_bc", [128, D]), ("b_s_bc", [128, D]),
                    ("f_w_up0", [D, 2 * DFF]), ("f_w_down0", [DFF, D]), ("f_w_up1", [D, 2 * DFF]), ("f_w_down1", [DFF, D]),
                    ("w_qk", [D, 2560]), ("rperm", [128, 128]), ("w_v", [D, 512]), ("cosT", [128, NT]), ("sinT", [128, NT]),
                    ("b_w_o", [D, D])):
        dram(nm, shp, F32, "in")
    for nm, shp in (("Wb_in", [D, 2 * D]), ("Wb_out", [D, D]), ("Wb_up0", [22, 128, 8192]), ("Wb_down0", [16, 128, NF * 128]),
                    ("Wb_up1", [22, 128, 8192]), ("Wb_down1", [16, 128, NF * 128]), ("Wb_qk", [D, 2560]),
                    ("Wb_v", [D, 512]), ("Wb_o", [D, D]), ("QT", [16, 128, NT]), ("OT", [16, 128, NT])):
        dram(nm, shp, BF16, "tmp")
    for nm in ("x1T", "x2T", "x3T", "x4T"):
        dram(nm, [D, NT], F32, "tmp")
    dram("outT", [D, NT], F32, "out")
    dram("KVin", [128, 128, 128], F32, "cc")
    dram("KVout", [128, 512, 128], F32, "cc")
    dram("A_in", [128, 48], F32, "cc")
    dram("A_out", [512, 48], F32, "cc")
    for l in (0, 1):
        dram("E_in%d" % l, [128, 32], F32, "cc")
        dram("E_out%d" % l, [512, 32], F32, "cc")
    KVin_bf = T["KVin"].bitcast(BF16)
    KVout_bf = T["KVout"].bitcast(BF16)

    with contextlib.ExitStack() as st:
        arena_t = st.enter_context(nc.sbuf_tensor("arena", [128, 49152], F32))
        ps = st.enter_context(nc.psum_tensor("ps", [128, 8, 512], F32))
        A = Arena(arena_t[:, :], 49152)

        ones_bf = A.bf16([128, 128])
        ones_f = A.f32([128, 128])
        eps_t = A.f32([128, 1])
        zero_t = A.f32([128, 1])
        cT = A.f32([128, 16])
        cond = A.f32([128, 16])
        modT = A.f32([128, 2, 96])
        gmul = A.f32([128, 2, 2, 16])
        b_adaT = A.f32([128, 2, 96])
        g_normT = A.f32([128, 2, 2, 16])
        conv_wT = A.f32([128, 2, 3, 88])
        conv_bT = A.f32([128, 2, 88])
        g_finalT = A.f32([128, 16])
        gqk = A.f32([128, 4])
        hmask = A.f32([128, 2])
        sel = A.f32([128, 8])
        halo_sb = A.f32([128, 16, 2])
        w_sT = A.bf16([128, 16, 128])
        PERSIST = A.off

        p.op("dve", lambda e: e.memset(ones_bf, 1.0), writes=["ones_bf"])
        p.op("dve", lambda e: e.memset(ones_f, 1.0), writes=["ones_f"])
        p.op("dve", lambda e: e.memset(eps_t, EPS), writes=["eps"])
        p.op("dve", lambda e: e.memset(zero_t, 0.0), writes=["zero"])
        for nm, dst in (("cT", cT), ("g_normT", g_normT), ("conv_wT", conv_wT),
                        ("conv_bT", conv_bT), ("g_finalT", g_finalT), ("gqk", gqk), ("hmask", hmask), ("sel", sel)):
            p.dma("sp", "const", lambda e, nm=nm, dst=dst: e.dma_start(out=dst, in_=T[nm]), writes=["c_" + nm])
        CONST_KEYS = ["c_cT", "c_b_adaT", "c_g_normT", "c_conv_wT", "c_conv_bT", "c_g_finalT", "c_gqk", "c_hmask",
                      "ones_bf", "ones_f", "eps", "zero"]

        p.dma("pool", "wsT", lambda e: e.dma_start(out=w_sT, in_=T["w_sT"].rearrange("p (g i) -> p g i", g=16)), writes=["w_sT"])

        def convert(src, dst, key):
            R, C = T[src].shape
            tot = R * C
            per = tot // 128
            sv = T[src].rearrange("(p r) c -> p (r c)", p=128)
            dv = T[dst].rearrange("(p r) c -> p (r c)", p=128)
            CH = 8192
            grp = "cv_" + key
            p.bg_groups.add(grp)
            for a in range(0, per, CH):
                b = min(per, a + CH)
                p.dma("pool", grp, lambda e, a=a, b=b: e.dma_start(out=dv[:, a:b], in_=sv[:, a:b]), writes=["W_" + key])

        def convert_up(l):
            grp = "cv_up%d" % l
            p.bg_groups.add(grp)
            S5 = T["f_w_up%d" % l].rearrange("(kc p) (wh fb c) -> fb p wh kc c", p=128, wh=2, fb=22, c=256)
            for fb in range(22):
                p.dma("pool", grp, lambda e, fb=fb: e.dma_start(
                    out=T["Wb_up%d" % l][fb].rearrange("p (wh kc c) -> p wh kc c", wh=2, kc=16, c=256), in_=S5[fb]),
                    writes=["W_up%d" % l])

        def convert_down(l):
            grp = "cv_down%d" % l
            p.bg_groups.add(grp)
            S4 = T["f_w_down%d" % l].rearrange("(f p) (dc c) -> dc p f c", p=128, c=128)
            for dc in range(16):
                p.dma("pool", grp, lambda e, dc=dc: e.dma_start(
                    out=T["Wb_down%d" % l][dc].rearrange("p (f c) -> p f c", f=NF, c=128), in_=S4[dc]),
                    writes=["W_down%d" % l])

        def convert_first():
            convert("a_w_in", "Wb_in", "in")
            convert("a_w_out", "Wb_out", "out")

        def convert_early():
            convert_up(0)
            convert_down(0)

        def convert_late():
            convert("w_v", "Wb_v", "v")
            convert("w_qk", "Wb_qk", "qk")
            convert("b_w_o", "Wb_o", "o")
            convert_up(1)
            convert_down(1)

        convert_first()

        def phase_ada_all():
            A.off = PERSIST
            wbl = [A.f32([128, 16, 512]) for _ in range(2)]
            modS = A.f32([128, 48])
            b_adaS = A.f32([128, 48])
            p.dma("sp", "const", lambda e: e.dma_start(out=b_adaS, in_=T["b_adaS"]), writes=["b_adaS"])
            p.op("act", lambda e: e.activation(out=cond, in_=cT, func=AF.Silu), reads=["c_cT"], writes=["cond"])
            blocks = [(l, cb) for l in range(2) for cb in range(6)]
            Wl = [fchunk(T["w_adaS"][l]) for l in range(2)]

            def load(i):
                l, cb = blocks[i]
                s = i % 2
                p.dma("sp", ("ada", s), lambda e, l=l, cb=cb, s=s: e.dma_start(out=wbl[s], in_=Wl[l][:, :, cb * 512:(cb + 1) * 512]),
                      writes=[("adaw", s)])
            load(0)
            for i, (l, cb) in enumerate(blocks):
                if i + 1 < len(blocks):
                    load(i + 1)
                s = i % 2
                for j in range(4):
                    col = l * 24 + cb * 4 + j
                    for kc in range(KC):
                        p.op("pe", lambda e, s=s, j=j, kc=kc, col=col: e.matmul(
                            ps[:, 0, col:col + 1], lhsT=wbl[s][:, kc, j * 128:(j + 1) * 128], rhs=cond[:, kc:kc + 1],
                            start=(kc == 0), stop=(kc == KC - 1)),
                            reads=[("adaw", s), "cond"], writes=[("ps", 0)])
            p.op("dve", lambda e: e.tensor_tensor(out=modS, in0=ps[:, 0, 0:48], in1=b_adaS, op=ALU.add),
                 reads=[("ps", 0), "b_adaS"], writes=["modS"])
            p.dma("sp", "ain", lambda e: e.dma_start(out=T["A_in"], in_=modS), reads=["modS"], writes=["A_in"])
            p.dma("pool", "cc_ada", lambda e: e.collective_compute(
                "AllGather", ALU.bypass, replica_groups=GROUPS, ins=[T["A_in"]], outs=[T["A_out"]]),
                reads=["A_in"], writes=["A_out"], inc=1)
            p.dma("sp", "aout", lambda e: e.dma_start(
                out=modT.rearrange("p l (r j) -> p l r j", r=4, j=24),
                in_=T["A_out"].rearrange("(r p) (l j) -> p l r j", p=128, l=2)),
                reads=["A_out"], writes=["modT"])
            for l in range(2):
                for s_ in range(2):
                    sc0 = 16 + 48 * s_
                    p.op("dve", lambda e, l=l, s_=s_, sc0=sc0: e.scalar_tensor_tensor(
                        out=gmul[:, l, s_, :], in0=modT[:, l, sc0:sc0 + 16], scalar=1.0, in1=g_normT[:, l, s_, :],
                        op0=ALU.add, op1=ALU.mult), reads=["modT", "c_g_normT"], writes=["gmul"])
            p.barrier()

        def norm_mod(xin, N, sq, psb, rstd, tmps, hT, gm, sh, xkey, hkey, plain_out=None, sqkey="sq"):
            p.op("act", lambda e: e.activation(out=sq[:, :, 0:N], in_=xin[:, :, 0:N], func=AF.Square),
                 reads=[xkey], writes=[sqkey])
            for kc in range(KC):
                p.op("pe", lambda e, kc=kc: e.matmul(ps[:, psb, 0:N], lhsT=ones_bf, rhs=sq[:, kc, 0:N],
                                                     start=(kc == 0), stop=(kc == KC - 1)),
                     reads=[sqkey, "ones_bf"], writes=[("ps", psb)])
            p.op("act", lambda e: e.activation(out=rstd[:, 0:N], in_=ps[:, psb, 0:N], func=AF.Sqrt, bias=eps_t[:, 0:1],
                                               scale=1.0 / D), reads=[("ps", psb), "eps"], writes=["rstd"])
            p.op("dve", lambda e: e.reciprocal(out=rstd[:, 0:N], in_=rstd[:, 0:N]), reads=["rstd"], writes=["rstd"])
            for kc in range(KC):
                if plain_out is not None:
                    p.op("dve", lambda e, kc=kc: e.scalar_tensor_tensor(
                        out=plain_out[:, kc, 0:N], in0=xin[:, kc, 0:N], scalar=gm[:, kc:kc + 1], in1=rstd[:, 0:N],
                        op0=ALU.mult, op1=ALU.mult), reads=[xkey, "rstd", "c_g_finalT"], writes=[hkey])
                    continue
                tb = tmps[kc % len(tmps)]
                tk = ("nm_tmp", kc % len(tmps))
                p.op("dve", lambda e, kc=kc, tb=tb: e.scalar_tensor_tensor(
                    out=tb[:, 0:N], in0=xin[:, kc, 0:N], scalar=gm[:, kc:kc + 1], in1=rstd[:, 0:N],
                    op0=ALU.mult, op1=ALU.mult), reads=[xkey, "rstd", "gmul"], writes=[tk])
                p.op("act", lambda e, kc=kc, tb=tb: e.activation(
                    out=hT[:, kc, 0:N], in_=tb[:, 0:N], func=AF.Identity, bias=sh[:, kc:kc + 1], scale=1.0),
                    reads=[tk, "modT"], writes=[hkey])

        class WStream:
            def __init__(self, name, slots, items, wkeys):
                self.name = name
                self.slots = slots
                self.items = items
                self.wkeys = wkeys
                self.next = 0

            def issue_upto(self, k):
                while self.next <= min(k, len(self.items) - 1):
                    i = self.next
                    s = i % len(self.slots)
                    for (o_ap, i_ap) in self.items[i](self.slots[s]):
                        p.dma("sp", (self.name, s), lambda e, o_ap=o_ap, i_ap=i_ap: e.dma_start(out=o_ap, in_=i_ap),
                              reads=self.wkeys, writes=[(self.name, s)])
                    self.next += 1

            def use(self, i):
                self.issue_upto(i + len(self.slots) - 1)
                s = i % len(self.slots)
                return self.slots[s], (self.name, s)

        def proj_residual(rhsT, rhskey, ws, ws_base, xsrc, xdst, gt, tok0, N, xr, ob, dkey, banks):
            XS = fchunk(xsrc)
            XD = fchunk(xdst)
            pend = []
            for ob_ in range(4):
                wblk, wk = ws.use(ws_base + ob_)
                for j in range(4):
                    dc = ob_ * 4 + j
                    bk = banks[dc % len(banks)]
                    rs = dc % 2
                    p.dma("sp", ("xr", rs), lambda e, dc=dc, rs=rs: e.dma_start(out=xr[rs][:, 0:N], in_=XS[:, dc, tok0:tok0 + N]),
                          writes=[("xr", rs)])
                    while pend:
                        pend.pop(0)()
                    for g in range(KC):
                        p.op("pe", lambda e, g=g, j=j, bk=bk, wblk=wblk: e.matmul(
                            ps[:, bk, 0:N], lhsT=wblk[:, g, j * 128:(j + 1) * 128], rhs=rhsT[:, g, 0:N],
                            start=(g == 0), stop=(g == KC - 1)), reads=[wk, rhskey], writes=[("ps", bk)])
                    p.op("dve", lambda e, dc=dc, bk=bk, rs=rs: e.scalar_tensor_tensor(
                        out=ob[rs][:, 0:N], in0=ps[:, bk, 0:N], scalar=gt[:, dc:dc + 1], in1=xr[rs][:, 0:N],
                        op0=ALU.mult, op1=ALU.add), reads=[("ps", bk), ("xr", rs), "modT"], writes=[("ob", rs)])
                    pend.append(lambda dc=dc, rs=rs: p.dma("sp", ("xo", rs), lambda e: e.dma_start(
                        out=XD[:, dc, tok0:tok0 + N], in_=ob[rs][:, 0:N]), reads=[("ob", rs)]))
            while pend:
                pend.pop(0)()

        def colblocks(Wb, c0, nblk, width=512):
            V = fchunk(Wb)
            return [(lambda slot, i=i: [(slot[:, :, 0:width], V[:, :, c0 + i * width:c0 + (i + 1) * width])]) for i in range(nblk)]

        def phase_gmlp():
            A.off = PERSIST
            xt = A.f32([128, 16, 512])
            bufA = A.bf16([128, 16, 512])
            bufB = A.bf16([128, 16, 512])
            vn = A.bf16([128, 4, 2048])
            rstd = A.f32([128, 512])
            tmps = [A.f32([128, 512]) for _ in range(2)]
            vts = [A.f32([128, 512]) for _ in range(2)]
            junk = A.bf16([128, 512])
            ssq = A.f32([128, 16])
            ssv = A.f32([128, 4])
            svts = [A.f32([128, 512]) for _ in range(2)]
            g_v_bc = A.f32([128, 2048])
            b_s_bc = A.f32([128, 2048])
            wsl = [A.bf16([128, 16, 512]) for _ in range(3)]
            xr = [A.f32([128, 512]) for _ in range(2)]
            ob = [A.f32([128, 512]) for _ in range(2)]
            p.dma("sp", "const", lambda e: e.dma_start(out=g_v_bc, in_=T["g_v_bc"]), writes=["g_v_bc"])
            p.dma("sp", "const", lambda e: e.dma_start(out=b_s_bc, in_=T["b_s_bc"]), writes=["b_s_bc"])
            ntile = NT // 512
            items = []
            for tt in range(ntile):
                items += colblocks(T["Wb_in"], 2048, 4) + colblocks(T["Wb_in"], 0, 4) + colblocks(T["Wb_out"], 0, 4)
            ws = WStream("gw", wsl, items, ["W_in", "W_out"])
            XS = fchunk(T["xT"])
            sh1 = modT[:, 0, 0:16]
            gt1 = modT[:, 0, 32:48]
            for tt in range(ntile):
                tok0 = tt * 512
                p.dma("sp", "xt", lambda e, tok0=tok0: e.dma_start(out=xt, in_=XS[:, :, tok0:tok0 + 512]),
                      reads=[("X", "xT")], writes=["xt"])
                norm_mod(xt, 512, bufA, 7, rstd, tmps, bufB, gmul[:, 0, 0, :], sh1, "xt", "bufB", sqkey="bufA")
                base = tt * 12
                for vb in range(4):
                    wblk, wk = ws.use(base + vb)
                    for tcn in range(4):
                        u = vb * 4 + tcn
                        bk = u % 4
                        for kc in range(KC):
                            p.op("pe", lambda e, kc=kc, tcn=tcn, bk=bk, wblk=wblk: e.matmul(
                                ps[:, bk, :], lhsT=bufB[:, kc, tcn * 128:(tcn + 1) * 128], rhs=wblk[:, kc, :],
                                start=(kc == 0), stop=(kc == KC - 1)), reads=[wk, "bufB"], writes=[("ps", bk)])
                        vt = vts[u % 2]
                        vk = ("vt", u % 2)
                        p.op("act", lambda e, bk=bk, vt=vt: e.activation(out=vt, in_=ps[:, bk, :], func=AF.Gelu),
                             reads=[("ps", bk)], writes=[vk])
                        p.op("dve", lambda e, vt=vt, tcn=tcn, vb=vb: e.scalar_tensor_tensor(
                            out=junk, in0=vt, scalar=1.0, in1=vt, op0=ALU.mult, op1=ALU.mult,
                            accum_out=ssq[:, tcn * 4 + vb:tcn * 4 + vb + 1]), reads=[vk], writes=["junk", "ssq"])
                        p.op("act", lambda e, vt=vt, tcn=tcn, vb=vb: e.activation(out=vn[:, tcn, vb * 512:(vb + 1) * 512], in_=vt, func=AF.Copy),
                             reads=[vk], writes=[("vraw", tcn)])
                for tcn in range(4):
                    p.op("dve", lambda e, tcn=tcn: e.tensor_reduce(out=ssv[:, tcn:tcn + 1], in_=ssq[:, tcn * 4:(tcn + 1) * 4],
                                                                   axis=mybir.AxisListType.X, op=ALU.add),
                         reads=["ssq"], writes=["ssv"])
                p.op("act", lambda e: e.activation(out=ssv, in_=ssv, func=AF.Sqrt, bias=eps_t[:, 0:1], scale=1.0 / D),
                     reads=["ssv", "eps"], writes=["ssv"])
                p.op("dve", lambda e: e.reciprocal(out=ssv, in_=ssv), reads=["ssv"], writes=["ssv"])
                for tcn in range(4):
                    p.op("dve", lambda e, tcn=tcn: e.scalar_tensor_tensor(
                        out=vn[:, tcn, :], in0=vn[:, tcn, :], scalar=ssv[:, tcn:tcn + 1], in1=g_v_bc,
                        op0=ALU.mult, op1=ALU.mult), reads=[("vraw", tcn), "ssv", "g_v_bc"], writes=[("vn", tcn)])
                for ub in range(4):
                    wblk, wk = ws.use(base + 4 + ub)
                    for j in range(4):
                        uc = ub * 4 + j
                        bk = uc % 4
                        for kc in range(KC):
                            p.op("pe", lambda e, kc=kc, j=j, bk=bk, wblk=wblk: e.matmul(
                                ps[:, bk, :], lhsT=wblk[:, kc, j * 128:(j + 1) * 128], rhs=bufB[:, kc, :],
                                start=(kc == 0), stop=(kc == KC - 1)), reads=[wk, "bufB"], writes=[("ps", bk)])
                        p.op("act", lambda e, bk=bk, uc=uc: e.activation(out=bufA[:, uc, :], in_=ps[:, bk, :], func=AF.Gelu),
                             reads=[("ps", bk)], writes=["bufA"])
                for tcn in range(4):
                    for gq in range(4):
                        u = tcn * 4 + gq
                        bk = 4 + u % 3
                        for gi in range(4):
                            g = gq * 4 + gi
                            p.op("pe", lambda e, g=g, gi=gi, bk=bk, tcn=tcn: e.matmul(
                                ps[:, bk, gi * 128:(gi + 1) * 128], lhsT=vn[:, tcn, g * 128:(g + 1) * 128], rhs=w_sT[:, g, :],
                                start=True, stop=True), reads=[("vn", tcn), "w_sT"], writes=[("ps", bk)])
                        sv = svts[u % 2]
                        sk = ("svt", u % 2)
                        p.op("dve", lambda e, bk=bk, sv=sv, gq=gq: e.tensor_tensor(
                            out=sv, in0=ps[:, bk, :], in1=b_s_bc[:, gq * 512:(gq + 1) * 512], op=ALU.add),
                            reads=[("ps", bk), "b_s_bc"], writes=[sk])
                        p.op("dve", lambda e, sv=sv, gq=gq, tcn=tcn: e.tensor_tensor(
                            out=bufB[:, gq * 4:(gq + 1) * 4, tcn * 128:(tcn + 1) * 128],
                            in0=sv.rearrange("p (a b) -> p a b", a=4, b=128),
                            in1=bufA[:, gq * 4:(gq + 1) * 4, tcn * 128:(tcn + 1) * 128], op=ALU.mult),
                            reads=[sk, "bufA"], writes=["bufB"])
                proj_residual(bufB, "bufB", ws, base + 8, T["xT"], T["x1T"], gt1, tok0, 512, xr, ob, "x1", [0, 1, 2, 3])
            p.barrier()

        def phase_ffn(l, xsrc_name, xdst_name):
            A.off = PERSIST
            act = A.bf16([128, 2, NF, SUBW])
            act_off_end = A.off
            h2T = A.bf16([128, 2, 16, SUBW + 2])
            rstd = A.f32([128, 512])
            tmps = [A.f32([128, 512]) for _ in range(2)]
            tg = [A.f32([128, 2, SUBW]) for _ in range(2)]
            tv = [A.f32([128, 2, SUBW]) for _ in range(2)]
            wsl = [A.bf16([128, 2, 16, 256]) for _ in range(3)]
            xr = [A.f32([128, 2, SUBW]) for _ in range(2)]
            ob = [A.f32([128, 2, SUBW]) for _ in range(2)]
            save = A.off
            A.off = PERSIST
            xin = A.f32([128, 16, SUBW + 2])
            sq = A.bf16([128, 16, SUBW + 2])
            assert A.off <= act_off_end
            A.off = save
            XS = fchunk(T[xsrc_name])
            XD = fchunk(T[xdst_name])
            Wup = T["Wb_up%d" % l]
            Wdn = T["Wb_down%d" % l]
            sh2 = modT[:, l, 48:64]
            gt2 = modT[:, l, 80:96]
            gm2 = gmul[:, l, 1, :]
            nsup = NSUBT // 2
            items = []
            for s_ in range(nsup):
                for fb in range(NF // 2):
                    items.append(lambda slot, fb=fb: [(slot.rearrange("p a k c -> p (a k c)"), Wup[fb])])
                for dc in range(KC):
                    items.append(lambda slot, dc=dc: [(slot.rearrange("p a k c -> p (a k c)")[:, 0:NF * 128], Wdn[dc])])
            ws = WStream("fw", wsl, items, ["W_up%d" % l, "W_down%d" % l])
            per_sup = NF // 2 + KC
            unit = 0
            for sp_ in range(nsup):
                toks = []
                for sub in range(2):
                    s_ = sp_ * 2 + sub
                    tok0 = min(s_ * SUBW, NT - SUBW)
                    toks.append(tok0)
                    if s_ == 0:
                        p.dma("sp", "xin", lambda e: e.dma_start(out=xin[:, :, 1:SUBW + 2], in_=XS[:, :, 0:SUBW + 1]),
                              reads=[("X", xsrc_name)], writes=["actreg"])
                        p.op("dve", lambda e: e.tensor_copy(out=xin[:, :, 0:1], in_=halo_sb[:, :, 0:1]),
                             reads=["halo_sb"], writes=["actreg"])
                    elif s_ == NSUBT - 1:
                        p.dma("sp", "xin", lambda e, tok0=tok0: e.dma_start(out=xin[:, :, 0:SUBW + 1], in_=XS[:, :, tok0 - 1:NT]),
                              reads=[("X", xsrc_name)], writes=["actreg"])
                        p.op("dve", lambda e: e.tensor_copy(out=xin[:, :, SUBW + 1:SUBW + 2], in_=halo_sb[:, :, 1:2]),
                             reads=["halo_sb"], writes=["actreg"])
                    else:
                        p.dma("sp", "xin", lambda e, tok0=tok0: e.dma_start(out=xin, in_=XS[:, :, tok0 - 1:tok0 + SUBW + 1]),
                              reads=[("X", xsrc_name)], writes=["actreg"])
                    norm_mod(xin, SUBW + 2, sq, 7, rstd, tmps, h2T[:, sub, :, :], gm2, sh2, "actreg", ("h2T", sub), sqkey="actreg")
                    if s_ == 0:
                        p.op("dve", lambda e, sub=sub: e.tensor_scalar(out=h2T[:, sub, :, 0:1], in0=h2T[:, sub, :, 0:1],
                                                                        scalar1=hmask[:, 0:1], scalar2=None, op0=ALU.mult),
                             reads=[("h2T", sub), "c_hmask"], writes=[("h2T", sub)])
                    if s_ == NSUBT - 1:
                        p.op("dve", lambda e, sub=sub: e.tensor_scalar(out=h2T[:, sub, :, SUBW + 1:SUBW + 2],
                                                                        in0=h2T[:, sub, :, SUBW + 1:SUBW + 2],
                                                                        scalar1=hmask[:, 1:2], scalar2=None, op0=ALU.mult),
                             reads=[("h2T", sub), "c_hmask"], writes=[("h2T", sub)])
                base = sp_ * per_sup
                for f in range(NF):
                    wblk, wk = ws.use(base + f // 2)
                    fl = f % 2
                    for which in range(2):
                        slot = unit % 3
                        unit += 1
                        b0 = 2 * slot
                        for kc in range(KC):
                            for sub in range(2):
                                p.op("pe", lambda e, kc=kc, sub=sub, b0=b0, which=which, fl=fl, wblk=wblk: e.matmul(
                                    ps[:, b0 + sub, 0:SUBW + 2], lhsT=wblk[:, which, kc, fl * 128:(fl + 1) * 128],
                                    rhs=h2T[:, sub, kc, :], start=(kc == 0), stop=(kc == KC - 1)),
                                    reads=[wk, ("h2T", 0), ("h2T", 1)], writes=[("ps", b0), ("ps", b0 + 1)])
                        tb = (tg if which == 0 else tv)[f % 2]
                        tk = ("tg" if which == 0 else "tv", f % 2)
                        fc = which * NF + f
                        pk = [("ps", b0), ("ps", b0 + 1)]
                        p.op("act", lambda e, b0=b0, tb=tb, fc=fc: e.activation(
                            out=tb, in_=ps[:, b0:b0 + 2, 1:SUBW + 1], func=AF.Identity,
                            bias=conv_bT[:, l, fc:fc + 1], scale=conv_wT[:, l, 1, fc:fc + 1]),
                            reads=pk + ["c_conv_wT", "c_conv_bT"], writes=[tk])
                        p.op("dve", lambda e, b0=b0, tb=tb, fc=fc: e.scalar_tensor_tensor(
                            out=tb, in0=ps[:, b0:b0 + 2, 0:SUBW], scalar=conv_wT[:, l, 0, fc:fc + 1], in1=tb,
                            op0=ALU.mult, op1=ALU.add), reads=pk + [tk], writes=[tk])
                        p.op("dve", lambda e, b0=b0, tb=tb, fc=fc: e.scalar_tensor_tensor(
                            out=tb, in0=ps[:, b0:b0 + 2, 2:SUBW + 2], scalar=conv_wT[:, l, 2, fc:fc + 1], in1=tb,
                            op0=ALU.mult, op1=ALU.add), reads=pk + [tk], writes=[tk])
                        if which == 0:
                            p.op("act", lambda e, tb=tb: e.activation(out=tb, in_=tb, func=AF.Gelu), reads=[tk], writes=[tk])
                    p.op("dve", lambda e, f=f: e.tensor_tensor(out=act[:, :, f, :], in0=tg[f % 2], in1=tv[f % 2], op=ALU.mult),
                         reads=[("tg", f % 2), ("tv", f % 2)], writes=["actreg"])
                pend = []
                for dc in range(KC):
                    wblk, wk = ws.use(base + NF // 2 + dc)
                    wd = wblk.rearrange("p a k c -> p (a k c)")[:, 0:NF * 128].rearrange("p (f c) -> p f c", f=NF, c=128)
                    b0 = 6 if dc % 2 == 0 else 4
                    rs = dc % 2
                    for sub in range(2):
                        p.dma("sp", ("xr", rs), lambda e, dc=dc, rs=rs, sub=sub, tk_=toks[sub]: e.dma_start(
                            out=xr[rs][:, sub, :], in_=XS[:, dc, tk_:tk_ + SUBW]),
                            writes=[("xr", rs)])
                    while pend:
                        pend.pop(0)()
                    for f in range(NF):
                        for sub in range(2):
                            p.op("pe", lambda e, f=f, sub=sub, b0=b0, wd=wd: e.matmul(
                                ps[:, b0 + sub, 0:SUBW], lhsT=wd[:, f, :], rhs=act[:, sub, f, :],
                                start=(f == 0), stop=(f == NF - 1)), reads=[wk, "actreg"], writes=[("ps", b0), ("ps", b0 + 1)])
                    p.op("dve", lambda e, dc=dc, b0=b0, rs=rs: e.scalar_tensor_tensor(
                        out=ob[rs], in0=ps[:, b0:b0 + 2, 0:SUBW], scalar=gt2[:, dc:dc + 1], in1=xr[rs],
                        op0=ALU.mult, op1=ALU.add), reads=[("ps", b0), ("ps", b0 + 1), ("xr", rs), "modT"], writes=[("ob", rs)])
                    for sub in range(2):
                        pend.append(lambda dc=dc, rs=rs, sub=sub, tk_=toks[sub]: p.dma("sp", ("xo", rs), lambda e: e.dma_start(
                            out=XD[:, dc, tk_:tk_ + SUBW], in_=ob[rs][:, sub, :]), reads=[("ob", rs)]))
                while pend:
                    pend.pop(0)()
            p.barrier()

        def phase_qkv():
            A.off = PERSIST
            xt = A.f32([128, 16, 512])
            sq = A.bf16([128, 16, 512])
            hT = A.bf16([128, 16, 512])
            rstd = A.f32([128, 512])
            tmps = [A.f32([128, 512]) for _ in range(2)]
            cs = A.f32([128, 512])
            sn = A.f32([128, 512])
            t1 = [A.f32([128, 512]) for _ in range(3)]
            t2 = [A.f32([128, 512]) for _ in range(3)]
            sqh = [A.bf16([128, 512]) for _ in range(3)]
            rs_ = [A.f32([128, 512]) for _ in range(3)]
            qr = [A.bf16([128, 512]) for _ in range(3)]
            vb_ = [A.bf16([128, 512]) for _ in range(2)]
            wsl = [A.bf16([128, 2, 16, 256]) for _ in range(3)]
            ntile = NT // 512
            Vv = fchunk(T["Wb_v"])
            Vqk = fchunk(T["Wb_qk"])
            rp_f = A.f32([128, 128])
            rp_b = A.bf16([128, 128])
            qb = [A.bf16([128, 512]) for _ in range(3)]
            p.dma("sp", "const", lambda e: e.dma_start(out=rp_f, in_=T["rperm"]), writes=["rp_f"])
            p.op("dve", lambda e: e.tensor_copy(out=rp_b, in_=rp_f), reads=["rp_f"], writes=["rp_b"])
            items = []
            for tt in range(ntile):
                items.append(lambda slot: [(slot.rearrange("p a k c -> p (a k c)").rearrange("p (k c) -> p k c", k=16, c=512), Vv[:, :, :])])
                for hb in range(10):
                    items.append(lambda slot, hb=hb: [(slot[:, 0, :, :], Vqk[:, :, hb * 256:(hb + 1) * 256])])
            ws = WStream("qw", wsl, items, ["W_v", "W_qk"])
            XS = fchunk(T["x2T"])
            sh1 = modT[:, 1, 0:16]
            KV5 = KVin_bf.rearrange("(kv h c) p f -> kv h c p f", kv=2, h=4, c=16)

            def kv_collectives(tt):
                for h in range(4):
                    for j in range(2):
                        for kv in range(2):
                            c = kv * 64 + h * 16 + 2 * tt + j
                            p.dma("pool", "cc_kv", lambda e, c=c: e.collective_compute(
                                "AllGather", ALU.bypass, replica_groups=GROUPS, ins=[T["KVin"][c]], outs=[T["KVout"][c]]),
                                reads=[("KVin", c)], writes=["KVout"], inc=1)
            hcnt = 0
            for tt in range(ntile):
                tok0 = tt * 512
                p.dma("sp", "xt", lambda e, tok0=tok0: e.dma_start(out=xt, in_=XS[:, :, tok0:tok0 + 512]),
                      reads=[("X", "x2T")], writes=["xt"])
                p.dma("sp", "cs", lambda e, tok0=tok0: e.dma_start(out=cs, in_=T["cosT"][:, tok0:tok0 + 512]), writes=["cs"])
                p.dma("sp", "cs", lambda e, tok0=tok0: e.dma_start(out=sn, in_=T["sinT"][:, tok0:tok0 + 512]), writes=["sn"])
                norm_mod(xt, 512, sq, 7, rstd, tmps, hT, gmul[:, 1, 0, :], sh1, "xt", "hT")
                base = tt * 11
                wblk, wk = ws.use(base)
                wv = wblk.rearrange("p a k c -> p (a k c)").rearrange("p (k c) -> p k c", k=16, c=512)
                for tcn in range(4):
                    bk = 4 + tcn % 2
                    for kc in range(KC):
                        p.op("pe", lambda e, kc=kc, tcn=tcn, bk=bk, wv=wv: e.matmul(
                            ps[:, bk, :], lhsT=hT[:, kc, tcn * 128:(tcn + 1) * 128], rhs=wv[:, kc, :],
                            start=(kc == 0), stop=(kc == KC - 1)), reads=[wk, "hT"], writes=[("ps", bk)])
                    vs = tcn % 2
                    p.op("act", lambda e, bk=bk, vs=vs: e.activation(out=vb_[vs], in_=ps[:, bk, :], func=AF.Copy),
                         reads=[("ps", bk)], writes=[("vb", vs)])
                    vc = tt * 2 + tcn // 2
                    off = (tcn % 2) * 128
                    p.dma("sp", ("vst", vs), lambda e, vs=vs, vc=vc, off=off: e.dma_start(
                        out=KV5[1, :, vc, :, off:off + 128].rearrange("h p d -> p h d"),
                        in_=vb_[vs].rearrange("p (h d) -> p h d", h=4, d=128)),
                        reads=[("vb", vs)], writes=[("KVin", 64 + h_ * 16 + vc) for h_ in range(4)])
                for hb in range(10):
                    wblk, wk = ws.use(base + 1 + hb)
                    for hh in range(2):
                        head = hb * 2 + hh
                        hs = hcnt % 3
                        hcnt += 1
                        bq = 0 + 2 * hs
                        bp = 1 + 2 * hs
                        for kc in range(KC):
                            p.op("pe", lambda e, kc=kc, hh=hh, bq=bq, wblk=wblk: e.matmul(
                                ps[:, bq, :], lhsT=wblk[:, 0, kc, hh * 128:(hh + 1) * 128], rhs=hT[:, kc, :],
                                start=(kc == 0), stop=(kc == KC - 1)), reads=[wk, "hT"], writes=[("ps", bq)])
                        p.op("act", lambda e, bq=bq, hs=hs: e.activation(out=qb[hs], in_=ps[:, bq, :], func=AF.Copy),
                             reads=[("ps", bq)], writes=[("qb", hs)])
                        p.op("pe", lambda e, bp=bp, hs=hs: e.matmul(ps[:, bp, :], lhsT=rp_b, rhs=qb[hs], start=True, stop=True),
                             reads=[("qb", hs), "rp_b"], writes=[("ps", bp)])
                        p.op("act", lambda e, bq=bq, hs=hs: e.activation(out=sqh[hs], in_=ps[:, bq, :], func=AF.Square),
                             reads=[("ps", bq)], writes=[("sqh", hs)])
                        bs = 6
                        p.op("pe", lambda e, hs=hs, bs=bs: e.matmul(ps[:, bs, :], lhsT=ones_bf, rhs=sqh[hs], start=True, stop=True),
                             reads=[("sqh", hs), "ones_bf"], writes=[("ps", bs)])
                        p.op("act", lambda e, hs=hs, bs=bs: e.activation(out=rs_[hs], in_=ps[:, bs, :], func=AF.Sqrt,
                                                                          bias=eps_t[:, 0:1], scale=1.0 / 128),
                             reads=[("ps", bs), "eps"], writes=[("rs", hs)])
                        p.op("dve", lambda e, hs=hs: e.reciprocal(out=rs_[hs], in_=rs_[hs]), reads=[("rs", hs)], writes=[("rs", hs)])
                        gc = 0 if head < 16 else 2
                        p.op("dve", lambda e, hs=hs, bq=bq, gc=gc: e.scalar_tensor_tensor(
                            out=t1[hs], in0=ps[:, bq, :], scalar=gqk[:, gc:gc + 1], in1=cs, op0=ALU.mult, op1=ALU.mult),
                            reads=[("ps", bq), "c_gqk", "cs"], writes=[("t1", hs)])
                        p.op("dve", lambda e, hs=hs, bp=bp, gc=gc: e.scalar_tensor_tensor(
                            out=t2[hs], in0=ps[:, bp, :], scalar=gqk[:, gc + 1:gc + 2], in1=sn, op0=ALU.mult, op1=ALU.mult),
                            reads=[("ps", bp), "c_gqk", "sn"], writes=[("t2", hs)])
                        p.op("dve", lambda e, hs=hs: e.tensor_tensor(out=t1[hs], in0=t1[hs], in1=t2[hs], op=ALU.add),
                             reads=[("t1", hs), ("t2", hs)], writes=[("t1", hs)])
                        p.op("dve", lambda e, hs=hs: e.tensor_tensor(out=qr[hs], in0=t1[hs], in1=rs_[hs], op=ALU.mult),
                             reads=[("t1", hs), ("rs", hs)], writes=[("qr", hs)])
                        if head < 16:
                            p.dma("sp", ("qst", hs), lambda e, hs=hs, head=head, tok0=tok0: e.dma_start(
                                out=T["QT"][head, :, tok0:tok0 + 512], in_=qr[hs]), reads=[("qr", hs)])
                        else:
                            c0 = (head - 16) * 16 + 2 * tt
                            p.dma("sp", ("qst", hs), lambda e, hs=hs, c0=c0: e.dma_start(
                                out=KVin_bf[c0:c0 + 2].rearrange("c d t -> d c t"),
                                in_=qr[hs].rearrange("p (c t) -> p c t", c=2, t=256)),
                                reads=[("qr", hs)], writes=[("KVin", c0), ("KVin", c0 + 1)])
                if tt >= 1:
                    kv_collectives(tt - 1)
            kv_collectives(ntile - 1)
            p.barrier()

        def phase_attn():
            A.off = PERSIST
            KTs = [A.bf16([128, 4, NT]) for _ in range(2)]
            Vs = [A.bf16([128, 4, 32, 128]) for _ in range(2)]
            Qs = [A.bf16([128, 4, 128]) for _ in range(3)]
            Ps = [A.bf16([128, 2, 512]) for _ in range(4)]
            acc2 = [A.f32([128, 2, 512]) for _ in range(2)]
            accs = [A.f32([128, 512]) for _ in range(2)]
            rinv = [A.f32([128, 512]) for _ in range(2)]
            osb = [A.bf16([128, 4, 128]) for _ in range(2)]
            scale = 1.0 / math.sqrt(128.0)
            def load_kv(h):
                s = h % 2
                for r in range(4):
                    p.dma("sp", ("kv", s), lambda e, h=h, s=s, r=r: e.dma_start(
                        out=KTs[s][:, r, :].rearrange("p (c t) -> p c t", c=16, t=256),
                        in_=KVout_bf[h * 16:(h + 1) * 16, r * 128:(r + 1) * 128, :].rearrange("c d t -> d c t")),
                        reads=["KVout"], writes=[("K", s)])
                    p.dma("sp", ("kv", s), lambda e, h=h, s=s, r=r: e.dma_start(
                        out=Vs[s][:, r, :, :].rearrange("p k d -> p (k d)").rearrange("p (c f) -> p c f", c=16, f=256),
                        in_=KVout_bf[64 + h * 16:64 + (h + 1) * 16, r * 128:(r + 1) * 128, :].rearrange("c q f -> q c f")),
                        reads=["KVout"], writes=[("V", s)])

            nq = NT // 128
            qi_all = [(h, qt) for h in range(4) for qt in range(nq)]

            def load_q(i):
                h, qt = qi_all[i]
                s = i % 3
                p.dma("sp", ("q", s), lambda e, h=h, qt=qt, s=s: e.dma_start(
                    out=Qs[s], in_=T["QT"][4 * h:4 * h + 4, :, qt * 128:(qt + 1) * 128].rearrange("g d q -> d g q")),
                    reads=["QT"], writes=[("Q", s)])

            load_kv(0)
            load_q(0)
            load_q(1)
            npair = 64
            slot_ctr = [0]
            pairs = [(i, kp) for i in range(len(qi_all)) for kp in range(npair)]
            sslot = {}
            NSL = 2
            ACCK = [("ps", 6), ("ps", 7)]

            def emit_S(n):
                i, kp = pairs[n]
                h, qt = qi_all[i]
                s = h % 2
                qs = i % 3
                if kp == 0:
                    if i + 2 < len(qi_all):
                        load_q(i + 2)
                if kp == 4 and qt == 0 and h + 1 < 4:
                    load_kv(h + 1)
                Kt = KTs[s].rearrange("p r t -> p (r t)")
                Qt = Qs[qs].rearrange("p g q -> p (g q)")
                sb = slot_ctr[0] % NSL
                slot_ctr[0] += 1
                sslot[n] = sb
                for j in range(2):
                    ktile = 2 * kp + j
                    p.op("pe", lambda e, sb=sb, j=j, ktile=ktile, Kt=Kt, Qt=Qt: e.matmul(
                        ps[:, 2 * sb + j, :], lhsT=Kt[:, ktile * 128:(ktile + 1) * 128], rhs=Qt, start=True, stop=True),
                        reads=[("K", s), ("Q", qs)], writes=[("ps", 2 * sb), ("ps", 2 * sb + 1)])

            def emit_rest(n):
                i, kp = pairs[n]
                h, qt = qi_all[i]
                s = h % 2
                Vt = Vs[s].rearrange("p r k d -> p (r k) d")
                ob_ = 4 + i % 2
                sb = sslot.pop(n)
                pb = n % 4
                p.op("act", lambda e, sb=sb, pb=pb: e.activation(
                    out=Ps[pb], in_=ps[:, 2 * sb:2 * sb + 2, :], func=AF.Exp, scale=scale),
                    reads=[("ps", 2 * sb), ("ps", 2 * sb + 1)], writes=[("P", pb)])
                a2p = acc2[i % 2]
                apk = ("acc2p", i % 2)
                if kp % 4 == 1:
                    if kp == 1:
                        p.op("pool", lambda e, pb=pb, a2p=a2p: e.tensor_copy(out=a2p, in_=Ps[pb]), reads=[("P", pb)], writes=[apk])
                    else:
                        p.op("pool", lambda e, pb=pb, a2p=a2p: e.tensor_tensor(out=a2p, in0=a2p, in1=Ps[pb], op=ALU.add),
                             reads=[("P", pb), apk], writes=[apk])
                elif kp == 0:
                    p.op("dve", lambda e, pb=pb: e.tensor_copy(out=ps[:, 6:8, :], in_=Ps[pb]), reads=[("P", pb)], writes=ACCK)
                else:
                    p.op("dve", lambda e, pb=pb: e.tensor_tensor(out=ps[:, 6:8, :], in0=ps[:, 6:8, :], in1=Ps[pb], op=ALU.add),
                         reads=[("P", pb)] + ACCK, writes=ACCK)
                if kp == npair - 1:
                    r_ = i % 2
                    p.op("dve", lambda e, r_=r_: e.tensor_copy(out=rinv[r_], in_=ps[:, 6, :]), reads=ACCK, writes=[("rinv", r_)])
                    p.op("dve", lambda e, r_=r_: e.tensor_tensor(out=accs[r_], in0=ps[:, 7, :], in1=rinv[r_], op=ALU.add),
                         reads=ACCK + [("rinv", r_)], writes=[("accs", r_)])
                    p.op("dve", lambda e, r_=r_, a2p=a2p: e.tensor_tensor(out=accs[r_], in0=accs[r_], in1=a2p[:, 0, :], op=ALU.add),
                         reads=[apk, ("accs", r_)], writes=[("accs", r_)])
                    p.op("dve", lambda e, r_=r_, a2p=a2p: e.tensor_tensor(out=accs[r_], in0=accs[r_], in1=a2p[:, 1, :], op=ALU.add),
                         reads=[apk, ("accs", r_)], writes=[("accs", r_)])

            def emit_PV(n):
                i, kp = pairs[n]
                h, qt = qi_all[i]
                s = h % 2
                Vt = Vs[s].rearrange("p r k d -> p (r k) d")
                ob_ = 4 + i % 2
                pb = n % 4
                for j in range(2):
                    ktile = 2 * kp + j
                    p.op("pe", lambda e, pb=pb, j=j, ktile=ktile, Vt=Vt, ob_=ob_, kp=kp: e.matmul(
                        ps[:, ob_, :], lhsT=Vt[:, ktile, :], rhs=Ps[pb][:, j, :],
                        start=(kp == 0 and j == 0), stop=(kp == npair - 1 and j == 1)),
                        reads=[("V", s), ("P", pb)], writes=[("ps", ob_)])

            def emit_final(i):
                h, qt = qi_all[i]
                r_ = i % 2
                ob_ = 4 + i % 2
                sbk = 2 * (slot_ctr[0] % NSL)
                p.op("pe", lambda e, r_=r_, sbk=sbk: e.matmul(ps[:, sbk, :], lhsT=ones_f, rhs=accs[r_], start=True, stop=True),
                     reads=[("accs", r_), "ones_f"], writes=[("ps", sbk), ("ps", sbk + 1)])
                p.op("dve", lambda e, r_=r_, sbk=sbk: e.reciprocal(out=rinv[r_], in_=ps[:, sbk, :]),
                     reads=[("ps", sbk), ("ps", sbk + 1)], writes=[("rinv", r_)])
                p.op("dve", lambda e, r_=r_, ob_=ob_: e.tensor_tensor(
                    out=osb[r_].rearrange("p g q -> p (g q)"), in0=ps[:, ob_, :], in1=rinv[r_], op=ALU.mult),
                    reads=[("ps", ob_), ("rinv", r_)], writes=[("osb", r_)])
                p.dma("sp", ("ost", r_), lambda e, r_=r_, h=h, qt=qt: e.dma_start(
                    out=T["OT"][4 * h:4 * h + 4, :, qt * 128:(qt + 1) * 128].rearrange("g d q -> d g q"), in_=osb[r_]),
                    reads=[("osb", r_)])

            emit_S(0)
            pending_final = None
            NP_ = len(pairs)
            for n in range(NP_ + 1):
                if n + 1 < NP_:
                    emit_S(n + 1)
                if n < NP_:
                    emit_rest(n)
                if pending_final is not None:
                    emit_final(pending_final)
                    pending_final = None
                if n >= 1:
                    emit_PV(n - 1)
                    if pairs[n - 1][1] == npair - 1:
                        pending_final = pairs[n - 1][0]
            if pending_final is not None:
                emit_final(pending_final)
            p.barrier()

        def phase_wo():
            A.off = PERSIST
            oT = [A.bf16([128, 16, 512]) for _ in range(2)]
            wsl = [A.bf16([128, 16, 512]) for _ in range(3)]
            xr = [A.f32([128, 512]) for _ in range(2)]
            ob = [A.f32([128, 512]) for _ in range(2)]
            ntile = NT // 512
            items = []
            for tt in range(ntile):
                items += colblocks(T["Wb_o"], 0, 4)
            ws = WStream("ow", wsl, items, ["W_o"])
            gt1 = modT[:, 1, 32:48]
            OTv = T["OT"].rearrange("h d t -> d h t")
            for tt in range(ntile):
                tok0 = tt * 512
                s = tt % 2
                p.dma("sp", ("oT", s), lambda e, s=s, tok0=tok0: e.dma_start(out=oT[s], in_=OTv[:, :, tok0:tok0 + 512]),
                      reads=["OT"], writes=[("oT", s)])
                proj_residual(oT[s], ("oT", s), ws, tt * 4, T["x2T"], T["x3T"], gt1, tok0, 512, xr, ob, "x3", [0, 1, 2, 3])
            p.barrier()

        def phase_final():
            A.off = PERSIST
            xt = [A.f32([128, 16, 512]) for _ in range(2)]
            sq = A.bf16([128, 16, 512])
            rstd = A.f32([128, 512])
            yo = [A.f32([128, 16, 512]) for _ in range(2)]
            XS = fchunk(T["x4T"])
            XD = fchunk(T["outT"])
            ntile = NT // 512
            for tt in range(ntile):
                tok0 = tt * 512
                s = tt % 2
                p.dma("sp", ("fx", s), lambda e, s=s, tok0=tok0: e.dma_start(out=xt[s], in_=XS[:, :, tok0:tok0 + 512]),
                      reads=[("X", "x4T")], writes=[("fxt", s)])
                norm_mod(xt[s], 512, sq, 7, rstd, None, None, g_finalT, None, ("fxt", s), ("fyo", s), plain_out=yo[s])
                p.dma("sp", ("fo", s), lambda e, s=s, tok0=tok0: e.dma_start(out=XD[:, :, tok0:tok0 + 512], in_=yo[s]),
                      reads=[("fyo", s)])
            p.barrier()

        def phase_halo(l, xname):
            A.off = PERSIST
            eo = A.f32([128, 4, 16, 2])
            X = fchunk(T[xname])
            Ein = T["E_in%d" % l]
            Eout = T["E_out%d" % l]
            Ein3 = Ein.rearrange("p (k e) -> p k e", e=2)
            p.dma("sp", "ein", lambda e: e.dma_start(out=Ein3[:, :, 0:1], in_=X[:, :, 0:1], allow_slow_non_contiguous=True),
                  reads=[("X", xname)], writes=["E_in"])
            p.dma("sp", "ein", lambda e: e.dma_start(out=Ein3[:, :, 1:2], in_=X[:, :, NT - 1:NT], allow_slow_non_contiguous=True),
                  reads=[("X", xname)], writes=["E_in"])
            p.dma("pool", "cc_halo", lambda e: e.collective_compute(
                "AllGather", ALU.bypass, replica_groups=GROUPS, ins=[Ein], outs=[Eout]),
                reads=["E_in"], writes=["E_out"], inc=1)
            p.dma("sp", "eo", lambda e: e.dma_start(out=eo, in_=Eout.rearrange("(r p) (k e) -> p r k e", p=128, e=2)),
                  reads=["E_out"], writes=["eo"])
            for side in range(2):
                src_e = 1 - side
                for r in range(4):
                    sc = sel[:, side * 4 + r:side * 4 + r + 1]
                    if r == 0:
                        p.op("dve", lambda e, side=side, src_e=src_e, sc=sc: e.tensor_scalar(
                            out=halo_sb[:, :, side], in0=eo[:, 0, :, src_e], scalar1=sc, scalar2=None, op0=ALU.mult),
                            reads=["eo", "c_sel"], writes=["halo_sb"])
                    else:
                        p.op("dve", lambda e, side=side, src_e=src_e, sc=sc, r=r: e.scalar_tensor_tensor(
                            out=halo_sb[:, :, side], in0=eo[:, r, :, src_e], scalar=sc, in1=halo_sb[:, :, side],
                            op0=ALU.mult, op1=ALU.add), reads=["eo", "c_sel", "halo_sb"], writes=["halo_sb"])
            p.barrier()

        p.barrier()
        phase_ada_all()
        convert_early()
        phase_gmlp()
        phase_halo(0, "x1T")
        convert_late()
        phase_ffn(0, "x1T", "x2T")
        phase_qkv()
        phase_attn()
        phase_wo()
        phase_halo(1, "x3T")
        phase_ffn(1, "x3T", "x4T")
        phase_final()
        final_groups = [g for g in p.group_counts if g not in p.bg_groups]
        p.emit(final_groups=final_groups)
    return nc, T


def _rope_tables(t0):
    S = 16384
    s = np.arange(t0, t0 + NT)
    row = (s // 64).astype(np.float32)
    col = (s % 64).astype(np.float32)
    inv = (10000.0 ** (-np.arange(0, 64, 2, dtype=np.float32) / 64.0)).astype(np.float32)
    ang_r = (row[:, None] * inv[None, :]).astype(np.float32)
    ang_c = (col[:, None] * inv[None, :]).astype(np.float32)
    cr, sr, cc, sc = np.cos(ang_r), np.sin(ang_r), np.cos(ang_c), np.sin(ang_c)
    cosT = np.concatenate([cr, cr, cc, cc], axis=1).T
    sinT = np.concatenate([-sr, sr, -sc, sc], axis=1).T
    return np.ascontiguousarray(cosT, dtype=np.float32), np.ascontiguousarray(sinT, dtype=np.float32)


def _partner_idx(n):
    d = np.arange(n)
    w = d % 64
    return np.where(w < 32, d + 32, d - 32)


_PROG_CACHE = {}


def _get_prog():
    if "p" not in _PROG_CACHE:
        _PROG_CACHE["p"] = build_program()
    return _PROG_CACHE["p"]


def _prep(x, c, w_ada, b_ada, g_norm, g_final, a_w_in, a_g_v, a_w_s, a_b_s, a_w_out,
          b_w_qkv, b_g_q, b_g_k, b_w_o, f_w_up, f_conv_w, f_conv_b, f_w_down):
    f = lambda a: np.ascontiguousarray(np.asarray(a), dtype=np.float32)
    x, c, w_ada, b_ada, g_norm, g_final = f(x), f(c), f(w_ada), f(b_ada), f(g_norm), f(g_final)
    a_w_in, a_g_v, a_w_s, a_b_s, a_w_out = f(a_w_in), f(a_g_v), f(a_w_s), f(a_b_s), f(a_w_out)
    b_w_qkv, b_g_q, b_g_k, b_w_o = f(b_w_qkv), f(b_g_q), f(b_g_k), f(b_w_o)
    f_w_up, f_conv_w, f_conv_b, f_w_down = f(f_w_up), f(f_conv_w), f(f_conv_b), f(f_w_down)

    shared = {
        "g_normT": f(g_norm.reshape(2, 2, 16, 128).transpose(3, 0, 1, 2)),
        "conv_wT": f(f_conv_w.reshape(2, 3, 88, 128).transpose(3, 0, 1, 2)),
        "conv_bT": f(f_conv_b.reshape(2, 88, 128).transpose(2, 0, 1)),
        "g_finalT": f(g_final.reshape(16, 128).T),
        "a_w_in": a_w_in[0], "a_w_out": a_w_out[0],
        "w_sT": f(a_w_s[0].transpose(2, 0, 1).reshape(128, 2048)),
        "g_v_bc": f(np.broadcast_to(a_g_v[0][None, :], (128, 2048))),
        "b_s_bc": f(np.broadcast_to(a_b_s[0].reshape(1, 2048), (128, 2048))),
        "f_w_up0": f_w_up[0], "f_w_up1": f_w_up[1], "f_w_down0": f_w_down[0], "f_w_down1": f_w_down[1],
        "b_w_o": b_w_o[0],
    }
    wqk = f(b_w_qkv[0][:, 0:2560])
    pidx = _partner_idx(2560)
    shared["w_qk"] = wqk
    rp = np.zeros((128, 128), np.float32)
    pp = _partner_idx(128)
    rp[pp, np.arange(128)] = 1.0
    shared["rperm"] = rp
    shared["w_v"] = f(b_w_qkv[0][:, 2560:3072])
    p128 = _partner_idx(128)
    shared["gqk"] = f(np.stack([b_g_q[0], b_g_q[0][p128], b_g_k[0], b_g_k[0][p128]], axis=1))

    cores = []
    for ci in range(8):
        b, r = ci // 4, ci % 4
        t0 = r * NT
        m = dict(shared)
        m["xT"] = f(x[b, t0:t0 + NT, :].T)
        m["cT"] = f(c[b].reshape(16, 128).T)
        cs, sn = _rope_tables(t0)
        m["cosT"], m["sinT"] = cs, sn
        m["hmask"] = f(np.broadcast_to(np.array([[0.0 if r == 0 else 1.0, 0.0 if r == 3 else 1.0]], np.float32), (128, 2)))
        sl = np.zeros((128, 8), np.float32)
        if r > 0:
            sl[:, r - 1] = 1.0
        if r < 3:
            sl[:, 4 + r + 1] = 1.0
        m["sel"] = sl
        m["w_adaS"] = f(w_ada[:, :, r * 3072:(r + 1) * 3072])
        m["b_adaS"] = f(b_ada[:, r * 3072:(r + 1) * 3072].reshape(2, 24, 128).transpose(2, 0, 1).reshape(128, 48))
        cores.append(m)

    return cores


IN_NAMES = ["cT", "g_normT", "conv_wT", "conv_bT", "g_finalT", "gqk", "hmask", "sel", "w_adaS", "b_adaS", "xT",
            "a_w_in", "a_w_out", "w_sT", "g_v_bc", "b_s_bc", "f_w_up0", "f_w_down0", "f_w_up1", "f_w_down1",
            "w_qk", "rperm", "w_v", "cosT", "sinT", "b_w_o"]


def kernel(**inputs):
    cores = _prep(**inputs)
    nc, T = _get_prog()
    maps = [{n: m[n] for n in IN_NAMES} for m in cores]
    res = run_bass_kernel_spmd(nc, maps, core_ids=list(range(8)))
    out = np.empty((2, 16384, D), np.float32)
    for ci in range(8):
        b, r = ci // 4, ci % 4
        out[b, r * NT:(r + 1) * NT, :] = res.results[ci]["outT"].T
    return out
```
